# Optimizing a Trainium2 kernel written in Bass

```python
import math
import jax, jax.numpy as jnp
from jax import lax
import numpy as np

D_MODEL = 1024
BATCH = 4
SEQ = 4096
DEPTH = 2

GRID_W = 64
CTX_LEN = 256
EPS = 1e-6

A_HEADS = 4
A_HEAD_DIM = 128
A_WIDTH = A_HEADS * A_HEAD_DIM
SHORT_CONV_W = 5
DELTA_CHUNK = 64
B_Q_HEADS = 8
B_KV_HEADS = 2
B_HEAD_DIM = 64
B_GROUP = B_Q_HEADS // B_KV_HEADS
B_WIDTH = B_Q_HEADS * B_HEAD_DIM
B_KV_WIDTH = B_KV_HEADS * B_HEAD_DIM
ROPE_AXIS_PAIRS = B_HEAD_DIM // 4
ROPE_THETA = 10000.0
Q_BLOCK = 128
OFF_Z_A = 3 * A_WIDTH
OFF_BA_A = OFF_Z_A + A_WIDTH
OFF_Q_B = OFF_BA_A + 4 * A_HEADS
OFF_K_B = OFF_Q_B + B_WIDTH
OFF_V_B = OFF_K_B + B_KV_WIDTH
HYB_IN_WIDTH = OFF_V_B + B_KV_WIDTH
MIX_WIDTH = A_WIDTH + B_WIDTH
CONF_WIDTH = D_MODEL
CONF_KERNEL = 31
FFN_HIDDEN = ((8 * D_MODEL + 3 * 256 - 1) // (3 * 256)) * 256
N_EVEN = (DEPTH + 1) // 2
N_ODD = DEPTH // 2

kernel_name = "hybrid_deltanet_gqa_conformer_diffusion_trunk"

F32 = jnp.float32


def rms_norm(x, gain):
    xf = x.astype(F32)
    y = xf * lax.rsqrt(jnp.mean(jnp.square(xf), axis=-1, keepdims=True) + EPS)
    return (y * gain.astype(F32)).astype(x.dtype)


def layer_norm(x, gain, bias):
    xf = x.astype(F32)
    mu = jnp.mean(xf, axis=-1, keepdims=True)
    var = jnp.mean(jnp.square(xf - mu), axis=-1, keepdims=True)
    return ((xf - mu) * lax.rsqrt(var + EPS) * gain.astype(F32) + bias.astype(F32)).astype(x.dtype)


def l2_normalize(x):
    return x * lax.rsqrt(jnp.sum(jnp.square(x), axis=-1, keepdims=True) + EPS)


def depthwise_conv_centred(x, w):
    pad = w.shape[0] // 2
    return lax.conv_general_dilated(
        x, w[:, None, :].astype(x.dtype), window_strides=(1,), padding=[(pad, pad)],
        dimension_numbers=("NWC", "WIO", "NWC"), feature_group_count=x.shape[-1])


def ada_terms(cond, w_mod, b_mod):
    m = jax.nn.silu(cond) @ w_mod + b_mod
    return jnp.split(m, 6, axis=-1)


def modulate(h, shift, scale):
    return h * (1.0 + scale) + shift


def axial_rope_tables(n_tokens):
    rows = n_tokens // GRID_W
    row = jnp.broadcast_to(jnp.arange(rows, dtype=F32)[:, None], (rows, GRID_W)).reshape(n_tokens)
    col = jnp.broadcast_to(jnp.arange(GRID_W, dtype=F32)[None, :], (rows, GRID_W)).reshape(n_tokens)
    inv_freq = ROPE_THETA ** (-jnp.arange(ROPE_AXIS_PAIRS, dtype=F32) / ROPE_AXIS_PAIRS)
    ang_r = row[:, None] * inv_freq
    ang_c = col[:, None] * inv_freq
    ang = jnp.concatenate([ang_r, ang_r, ang_c, ang_c], axis=-1)
    return jnp.cos(ang), jnp.sin(ang)


def apply_axial_rope(x, cos, sin):
    xf = x.astype(F32)
    xr = xf.reshape(*x.shape[:-1], 2, 2, ROPE_AXIS_PAIRS)
    rot = jnp.stack([-xr[..., 1, :], xr[..., 0, :]], axis=-2).reshape(x.shape)
    return (xf * cos[None, :, None, :] + rot * sin[None, :, None, :]).astype(x.dtype)


def gated_delta_chunked(q, k, v, g, beta, s0, need_out):
    bsz, n_tok, n_h, _ = q.shape
    dv = v.shape[-1]
    n_ch = n_tok // DELTA_CHUNK

    def chunks(t):
        return t.reshape(bsz, n_ch, DELTA_CHUNK, n_h, -1).transpose(1, 0, 3, 2, 4)

    qc, kc, vc = chunks(q), chunks(k), chunks(v)
    gcum = jnp.cumsum(chunks(g[..., None])[..., 0], axis=-1)
    bc = chunks(beta[..., None])
    idx = jnp.arange(DELTA_CHUNK)
    incl = idx[:, None] >= idx[None, :]
    strict = idx[:, None] > idx[None, :]
    decay = jnp.exp(jnp.where(incl, gcum[..., :, None] - gcum[..., None, :], -jnp.inf))
    kb = kc * bc
    lower = jnp.where(strict, jnp.einsum("nbhcd,nbhsd->nbhcs", kb, kc) * decay, 0.0)
    eye = jnp.eye(DELTA_CHUNK, dtype=F32)
    rhs = jnp.concatenate([vc * bc, kb * jnp.exp(gcum)[..., None]], axis=-1)
    sol = lax.linalg.triangular_solve(lower + eye, rhs, left_side=True, lower=True, unit_diagonal=True)
    u, w = sol[..., :dv], sol[..., dv:]
    g_last = gcum[..., -1]
    k_tail = kc * jnp.exp(g_last[..., None] - gcum)[..., None]
    xs = (u, w, k_tail, g_last)
    if need_out:
        intra = jnp.einsum("nbhcd,nbhsd->nbhcs", qc, kc) * decay
        q_dec = qc * jnp.exp(gcum)[..., None]
        xs = xs + (q_dec, intra)

    def step(state, xs_i):
        u_i, w_i, kt_i, gl_i = xs_i[:4]
        v_new = u_i - jnp.einsum("bhcd,bhde->bhce", w_i, state)
        new_state = state * jnp.exp(gl_i)[..., None, None] + jnp.einsum("bhcd,bhce->bhde", kt_i, v_new)
        if need_out:
            qd_i, a_i = xs_i[4:]
            o_i = jnp.einsum("bhcd,bhde->bhce", qd_i, state) + jnp.einsum("bhcs,bhse->bhce", a_i, v_new)
            return new_state, o_i
        return new_state, None

    s_fin, o = lax.scan(step, s0, xs)
    if need_out:
        o = o.transpose(1, 0, 3, 2, 4).reshape(bsz, n_tok, n_h, dv)
    return o, s_fin


def bidirectional_delta(q, k, v, g, beta, s0_fwd, s0_bwd, need_out):
    o_f, s_f = gated_delta_chunked(q, k, v, g[:, :, 0], beta[:, :, 0], s0_fwd, need_out)
    rev = lambda t: jnp.flip(t, axis=1)
    o_b, s_b = gated_delta_chunked(rev(q), rev(k), rev(v), rev(g[:, :, 1]), rev(beta[:, :, 1]), s0_bwd, need_out)
    o = o_f + rev(o_b) if need_out else None
    return o, s_f, s_b


def split_hybrid_projection(p, conv_w, a_log, dt_bias, q_norm, k_norm):
    bsz, n_tok, _ = p.shape
    qkv = jax.nn.silu(depthwise_conv_centred(p[..., :OFF_Z_A], conv_w))
    qkv = qkv.astype(F32).reshape(bsz, n_tok, 3, A_HEADS, A_HEAD_DIM)
    qa = l2_normalize(qkv[:, :, 0]) * (A_HEAD_DIM ** -0.5)
    ka = l2_normalize(qkv[:, :, 1])
    va = qkv[:, :, 2]
    za = p[..., OFF_Z_A:OFF_BA_A].reshape(bsz, n_tok, A_HEADS, A_HEAD_DIM)
    ba = p[..., OFF_BA_A:OFF_Q_B].astype(F32).reshape(bsz, n_tok, 2, 2, A_HEADS)
    beta = jax.nn.sigmoid(ba[:, :, 0])
    g = -jnp.exp(a_log.astype(F32)) * jax.nn.softplus(ba[:, :, 1] + dt_bias.astype(F32))
    qb = rms_norm(p[..., OFF_Q_B:OFF_K_B].reshape(bsz, n_tok, B_Q_HEADS, B_HEAD_DIM), q_norm)
    kb = rms_norm(p[..., OFF_K_B:OFF_V_B].reshape(bsz, n_tok, B_KV_HEADS, B_HEAD_DIM), k_norm)
    vb = p[..., OFF_V_B:].reshape(bsz, n_tok, B_KV_HEADS, B_HEAD_DIM)
    return qa, ka, va, za, g, beta, qb, kb, vb


def attend(q, k, v):
    s = jnp.einsum("bqkgd,bskd->bkgqs", q, k, preferred_element_type=F32) * (B_HEAD_DIM ** -0.5)
    p = jax.nn.softmax(s, axis=-1).astype(v.dtype)
    return jnp.einsum("bkgqs,bskd->bqkgd", p, v)


def latent_attention_blocks(q, k, v):
    bsz, n_tok = q.shape[:2]
    nb = n_tok // Q_BLOCK
    qb = q.reshape(bsz, nb, Q_BLOCK, *q.shape[2:]).swapaxes(0, 1)
    o = lax.map(lambda blk: attend(blk, k, v), qb)
    return o.swapaxes(0, 1).reshape(bsz, n_tok, B_WIDTH)


def hybrid_mixer(a_lat, a_ctx, w_in, conv_w, a_log, dt_bias, out_norm, q_norm, k_norm, w_out, cos, sin, ctx_out):
    bsz, n_tok, _ = a_lat.shape
    n_ctx = a_ctx.shape[1]
    qa_l, ka_l, va_l, za_l, g_l, be_l, qb_l, kb_l, vb_l = split_hybrid_projection(
        a_lat @ w_in, conv_w, a_log, dt_bias, q_norm, k_norm)
    qa_c, ka_c, va_c, za_c, g_c, be_c, qb_c, kb_c, vb_c = split_hybrid_projection(
        a_ctx @ w_in, conv_w, a_log, dt_bias, q_norm, k_norm)

    zeros = jnp.zeros((bsz, A_HEADS, A_HEAD_DIM, A_HEAD_DIM), F32)
    o_ctx_a, s_fwd, s_bwd = bidirectional_delta(qa_c, ka_c, va_c, g_c, be_c, zeros, zeros, ctx_out)
    o_lat_a, _, _ = bidirectional_delta(qa_l, ka_l, va_l, g_l, be_l, s_fwd, s_bwd, True)

    def delta_readout(o, z, n):
        y = rms_norm(o, out_norm) * jax.nn.silu(z.astype(F32))
        return y.reshape(bsz, n, A_WIDTH).astype(a_lat.dtype)

    qb_l = apply_axial_rope(qb_l, cos, sin)
    kb_l = apply_axial_rope(kb_l, cos, sin)
    k_all = jnp.concatenate([kb_c, kb_l], axis=1)
    v_all = jnp.concatenate([vb_c, vb_l], axis=1)
    group = lambda t: t.reshape(*t.shape[:2], B_KV_HEADS, B_GROUP, B_HEAD_DIM)
    o_lat_b = latent_attention_blocks(group(qb_l), k_all, v_all)

    y_lat = jnp.concatenate([delta_readout(o_lat_a, za_l, n_tok), o_lat_b.astype(a_lat.dtype)], axis=-1) @ w_out
    y_ctx = None
    if ctx_out:
        o_ctx_b = attend(group(qb_c), kb_c, vb_c).reshape(bsz, n_ctx, B_WIDTH)
        y_ctx = jnp.concatenate([delta_readout(o_ctx_a, za_c, n_ctx), o_ctx_b.astype(a_ctx.dtype)], axis=-1) @ w_out
    return y_lat, y_ctx


def conformer_conv(h, w_in, b_in, dw_w, dw_b, ln_g, ln_b, w_out, b_out):
    val, gate = jnp.split(h @ w_in + b_in, 2, axis=-1)
    u = val * jax.nn.sigmoid(gate)
    u = depthwise_conv_centred(u, dw_w) + dw_b
    u = jax.nn.silu(layer_norm(u, ln_g, ln_b))
    return u @ w_out + b_out


def swiglu_ffn(h, w_in, w_out):
    gate, up = jnp.split(h @ w_in, 2, axis=-1)
    return (jax.nn.silu(gate) * up) @ w_out


def setup_inputs(seed: int = 0) -> dict:
    key = jax.random.key(seed)
    ks = iter(jax.random.split(key, 40))
    D = D_MODEL

    def nrm(shape, scale):
        return jax.random.normal(next(ks), shape, F32) * scale

    def gain(shape):
        return 1.0 + nrm(shape, 0.05)

    a_log = jnp.log(jax.random.uniform(next(ks), (N_EVEN, 2, A_HEADS), F32, 1.0, 16.0))
    dt = jnp.exp(jax.random.uniform(next(ks), (N_EVEN, 2, A_HEADS), F32, math.log(1e-3), math.log(1e-1)))
    dt_bias = dt + jnp.log(-jnp.expm1(-dt))
    return {
        "x": nrm((BATCH, SEQ, D), 1.0),
        "c": nrm((BATCH, D), 1.0),
        "ctx": nrm((BATCH, CTX_LEN, D), 1.0),
        "c_ctx": nrm((D,), 1.0),
        "w_mod": nrm((DEPTH, D, 6 * D), 0.5 * D ** -0.5),
        "b_mod": nrm((DEPTH, 6 * D), 0.02),
        "g_mix_pre": gain((DEPTH, D)),
        "g_mix_post": gain((DEPTH, D)),
        "g_ffn_pre": gain((DEPTH, D)),
        "g_ffn_post": gain((DEPTH, D)),
        "w_ffn_in": nrm((DEPTH, D, 2 * FFN_HIDDEN), D ** -0.5),
        "w_ffn_out": nrm((DEPTH, FFN_HIDDEN, D), FFN_HIDDEN ** -0.5),
        "hyb_w_in": nrm((N_EVEN, D, HYB_IN_WIDTH), D ** -0.5),
        "hyb_conv_w": nrm((N_EVEN, SHORT_CONV_W, 3 * A_WIDTH), SHORT_CONV_W ** -0.5),
        "hyb_a_log": a_log,
        "hyb_dt_bias": dt_bias,
        "hyb_out_norm": gain((N_EVEN, A_HEAD_DIM)),
        "hyb_q_norm": gain((N_EVEN, B_HEAD_DIM)),
        "hyb_k_norm": gain((N_EVEN, B_HEAD_DIM)),
        "hyb_w_out": nrm((N_EVEN, MIX_WIDTH, D), MIX_WIDTH ** -0.5),
        "conf_w_in": nrm((N_ODD, D, 2 * CONF_WIDTH), D ** -0.5),
        "conf_b_in": nrm((N_ODD, 2 * CONF_WIDTH), 0.02),
        "conf_dw_w": nrm((N_ODD, CONF_KERNEL, CONF_WIDTH), CONF_KERNEL ** -0.5),
        "conf_dw_b": nrm((N_ODD, CONF_WIDTH), 0.02),
        "conf_ln_g": gain((N_ODD, CONF_WIDTH)),
        "conf_ln_b": nrm((N_ODD, CONF_WIDTH), 0.02),
        "conf_w_out": nrm((N_ODD, CONF_WIDTH, D), CONF_WIDTH ** -0.5),
        "conf_b_out": nrm((N_ODD, D), 0.02),
    }


def reference(x, c, ctx, c_ctx, w_mod, b_mod, g_mix_pre, g_mix_post, g_ffn_pre, g_ffn_post, w_ffn_in, w_ffn_out,
              hyb_w_in, hyb_conv_w, hyb_a_log, hyb_dt_bias, hyb_out_norm, hyb_q_norm, hyb_k_norm, hyb_w_out,
              conf_w_in, conf_b_in, conf_dw_w, conf_dw_b, conf_ln_g, conf_ln_b, conf_w_out, conf_b_out):
    cos, sin = axial_rope_tables(x.shape[1])
    h, hc = x, ctx
    for layer in range(DEPTH):
        is_hybrid = layer % 2 == 0
        idx = layer // 2
        advance_ctx = any(j % 2 == 0 for j in range(layer + 1, DEPTH))
        sh1, sc1, gt1, sh2, sc2, gt2 = ada_terms(c[:, None, :], w_mod[layer], b_mod[layer])
        a = modulate(rms_norm(h, g_mix_pre[layer]), sh1, sc1)
        if is_hybrid or advance_ctx:
            csh1, csc1, cgt1, csh2, csc2, cgt2 = ada_terms(c_ctx, w_mod[layer], b_mod[layer])
            ac = modulate(rms_norm(hc, g_mix_pre[layer]), csh1, csc1)
        if is_hybrid:
            y, yc = hybrid_mixer(a, ac, hyb_w_in[idx], hyb_conv_w[idx], hyb_a_log[idx], hyb_dt_bias[idx],
                                 hyb_out_norm[idx], hyb_q_norm[idx], hyb_k_norm[idx], hyb_w_out[idx],
                                 cos, sin, advance_ctx)
        else:
            conf = (conf_w_in[idx], conf_b_in[idx], conf_dw_w[idx], conf_dw_b[idx],
                    conf_ln_g[idx], conf_ln_b[idx], conf_w_out[idx], conf_b_out[idx])
            y = conformer_conv(a, *conf)
            yc = conformer_conv(ac, *conf) if advance_ctx else None
        h = h + gt1 * rms_norm(y, g_mix_post[layer])
        f = swiglu_ffn(modulate(rms_norm(h, g_ffn_pre[layer]), sh2, sc2), w_ffn_in[layer], w_ffn_out[layer])
        h = h + gt2 * rms_norm(f, g_ffn_post[layer])
        if advance_ctx:
            hc = hc + cgt1 * rms_norm(yc, g_mix_post[layer])
            fc = swiglu_ffn(modulate(rms_norm(hc, g_ffn_pre[layer]), csh2, csc2), w_ffn_in[layer], w_ffn_out[layer])
            hc = hc + cgt2 * rms_norm(fc, g_ffn_post[layer])
    return h
```

```python
import numpy as np
from contextlib import ExitStack, contextmanager
from collections import deque
import concourse.bass as bass
import concourse.mybir as mybir
from concourse.bass_utils import run_bass_kernel_spmd

F32 = mybir.dt.float32
BF16 = mybir.dt.bfloat16
AF = mybir.ActivationFunctionType
ALU = mybir.AluOpType

NDMA = 12
D = 1024
KC = 8
TC = 256
TL = 4096
TF = TC + TL
NOWN = 2048
NT = 2176
NTP = NT + 16
FF = 2816
FFC = 22
EPS = 1e-6
CH = 128
NCH = TF // CH
NEG = -30000.0
import os
UNIT_CUT = int(os.environ.get('UNIT_CUT', '99'))

C_ID, C_ONES, C_TRIA, C_TRID, C_NSA, C_NSD, C_BLK, C_ROT, NCONST = 0, 128, 256, 384, 512, 640, 768, 896, 1024
V_L = 80
V_CBIN, V_CDWB, V_CLNG, V_CLNB, V_CBOUT = 160, 176, 184, 192, 200
V_CDW = 208
V_HCONV = V_CDW + 248
V_ONORM = V_HCONV + 60
V_QN = V_ONORM + 1
V_KN = V_QN + 1
NV = V_KN + 1

OWN_TILES = [(0, 512), (512, 512), (1024, 512), (1536, 512), (2048, 128)]
FULL_TILES = [(0, 256)] + [(256 + i * 512, 512) for i in range(8)]


class Trk:
    __slots__ = ("w", "r", "psum")

    def __init__(self, psum=False):
        self.w = None
        self.r = []
        self.psum = psum


class Buf:
    def __init__(self, t):
        self.t = t
        self.k = Trk()
        self._ks = {}

    def tr(self, key):
        if key not in self._ks:
            self._ks[key] = Trk()
        return self._ks[key]


class Prog:
    def __init__(self, nc, stack):
        self.nc = nc
        self.eng = {}
        for k, h in (("pe", nc.tensor), ("act", nc.scalar), ("dve", nc.vector),
                     ("pool", nc.gpsimd), ("sp", nc.sync)):
            sem = stack.enter_context(nc.semaphore("s_" + k))
            self.eng[k] = dict(k=k, h=h, sem=sem, cnt=0, seen={}, dslots=None, dnext=0)
        for q in ("sp", "pool"):
            self.eng[q]["dslots"] = [[stack.enter_context(nc.semaphore("d%s%d" % (q, i))), 0]
                                     for i in range(NDMA)]
        self.n_ops = 0

    def _wait(self, e, ev):
        sem, val = ev
        key = id(sem)
        if e["seen"].get(key, 0) >= val:
            return
        if e["k"] == "pe" and sem is e["sem"]:
            return
        e["h"].wait_ge(sem, val)
        e["seen"][key] = val

    def _deps(self, e, reads, writes):
        for t in reads:
            if t.w is not None:
                self._wait(e, t.w)
        for t in writes:
            if t.w is not None:
                self._wait(e, t.w)
            for r in t.r:
                self._wait(e, r)

    def _commit(self, ev, reads, writes):
        for t in reads:
            t.r.append(ev)
            if len(t.r) > 16:
                best = {}
                for s, v in t.r:
                    if id(s) not in best or best[id(s)][1] < v:
                        best[id(s)] = (s, v)
                t.r = list(best.values())
        for t in writes:
            t.w = ev
            t.r = []

    def op(self, ek, fn, reads=(), writes=()):
        e = self.eng[ek]
        pr = [t for t in reads if getattr(t, "psum", False)]
        if pr:
            reads = [t for t in reads if not getattr(t, "psum", False)]
            writes = list(writes) + pr
        self._deps(e, reads, writes)
        ins = fn(e["h"])
        e["cnt"] += 1
        ins.then_inc(e["sem"], 1)
        self._commit((e["sem"], e["cnt"]), reads, writes)
        self.n_ops += 1

    def dma(self, qk, out, in_, reads=(), writes=()):
        e = self.eng[qk]
        slot = e["dslots"][e["dnext"] % NDMA]
        e["dnext"] += 1
        if slot[1] > 0:
            self._wait(e, (slot[0], slot[1]))
        self._deps(e, reads, writes)
        e["h"].dma_start(out=out, in_=in_).then_inc(slot[0], 16)
        slot[1] += 16
        self._commit((slot[0], slot[1]), reads, writes)
        self.n_ops += 1

    def barrier(self):
        evs = []
        for o in self.eng.values():
            if o["cnt"] > 0:
                evs.append((o["sem"], o["cnt"]))
            if o["dslots"]:
                for sm, v in o["dslots"]:
                    if v > 0:
                        evs.append((sm, v))
        for e in self.eng.values():
            for ev in evs:
                if ev[0] is e["sem"]:
                    continue
                self._wait(e, ev)

    def finish(self):
        e = self.eng["sp"]
        for o in self.eng.values():
            if o["dslots"]:
                for s, v in o["dslots"]:
                    if v > 0:
                        self._wait(e, (s, v))
            if o["cnt"] > 0 and o is not e:
                self._wait(e, (o["sem"], o["cnt"]))


class Ring:
    def __init__(self, bufs):
        self.b = bufs
        self.i = 0

    def next(self):
        b = self.b[self.i % len(self.b)]
        self.i += 1
        return b


def build(taps=(), stop_after=None):
    nc = bass.Bass("TRN2", target_bir_lowering=False)
    dr = {}

    def din(name, shape):
        dr[name] = nc.dram_tensor(name, list(shape), F32, kind="ExternalInput").ap()
        return dr[name]

    xT = din("xT", [D, TL]); ctxT = din("ctxT", [D, TC]); cin = din("cin", [128, 16])
    vecs_d = din("vecs", [128, NV]); abc_d = din("abc", [128, 16]); cst_d = din("consts", [128, NCONST])
    rope_d = din("rope", [2, 128, TL])
    w_mod = din("w_mod", [2, D, 6 * D]); hyb_w_in = din("hyb_w_in", [D, 2832]); hyb_w_out = din("hyb_w_out", [D, D])
    w_ffn_in = din("w_ffn_in", [2, D, 2 * FF]); w_ffn_out = din("w_ffn_out", [2, FF, D])
    conf_w_in = din("conf_w_in", [D, 2 * D]); conf_w_out = din("conf_w_out", [D, D])
    outT = nc.dram_tensor("outT", [D, NOWN], F32, kind="ExternalOutput").ap()
    tap_out = {}

    with ExitStack() as st:
        P = Prog(nc, st)
        op = P.op

        @contextmanager
        def scope():
            with ExitStack() as s_:
                yield s_
                P.barrier()

        nctr = [0]

        def sbuf(stack, name, shape, dt):
            nctr[0] += 1
            return Buf(stack.enter_context(nc.sbuf_tensor("s%d_%s" % (nctr[0], name), list(shape), dt)))

        pb = [st.enter_context(nc.psum_tensor("pb%d" % i, [128, 512], F32)) for i in range(8)]
        pk = [[Trk(psum=True) for _ in range(4)] for _ in range(8)]

        def pkr(i, c0, n):
            return [pk[i][0]]

        def tap(name, ap, trks, shape):
            if name not in taps:
                return
            t = nc.dram_tensor("tap_" + name, list(shape), F32, kind="ExternalOutput").ap()
            tap_out[name] = t
            P.dma("pool", t, ap, reads=trks)

        cst = sbuf(st, "cst", [128, NCONST], F32)
        vec = sbuf(st, "vec", [128, NV], F32)
        abc = sbuf(st, "abc", [128, 16], F32)
        cin_s = sbuf(st, "cin_s", [128, 16], F32)
        P.dma("sp", cst.t[:], cst_d[:, :], writes=[cst.k])
        P.dma("sp", vec.t[:], vecs_d[:, :], writes=[vec.k])
        P.dma("sp", abc.t[:], abc_d[:, :], writes=[abc.k])
        P.dma("sp", cin_s.t[:], cin[:, :], writes=[cin_s.k])
        cb = sbuf(st, "cb", [128, 3, 128], BF16)
        op("dve", lambda e: e.tensor_copy(out=cb.t[:, 0, :], in_=cst.t[:, C_ID:C_ID + 128]), reads=[cst.k], writes=[cb.k])
        op("dve", lambda e: e.tensor_copy(out=cb.t[:, 1, :], in_=cst.t[:, C_ONES:C_ONES + 128]), reads=[cst.k], writes=[cb.k])
        op("dve", lambda e: e.tensor_copy(out=cb.t[:, 2, :], in_=cst.t[:, C_BLK:C_BLK + 128]), reads=[cst.k], writes=[cb.k])
        identb = cb.t[:, 0, :]; onesb = cb.t[:, 1, :]; blkb = cb.t[:, 2, :]
        identf = cst.t[:, C_ID:C_ID + 128]; onesf = cst.t[:, C_ONES:C_ONES + 128]

        modv = sbuf(st, "modv", [128, 2, 48, 2], F32)
        A1 = sbuf(st, "A1", [128, 2, 8, 2], F32)
        A2 = sbuf(st, "A2", [128, 2, 8, 2], F32)
        G1 = sbuf(st, "G1", [128, 2, 8], F32)
        G2 = sbuf(st, "G2", [128, 2, 8], F32)
        sqr = Ring([sbuf(st, "sq%d" % i, [128, 512], BF16) for i in range(2)])
        rstd_r = Ring([sbuf(st, "rstd%d" % i, [128, 512], F32) for i in range(2)])
        tmp_r = Ring([sbuf(st, "tmp%d" % i, [128, 512], F32) for i in range(3)])
        stat_banks = Ring([6, 7])

        def vcol(c):
            return vec.t[:, c:c + 1]

        def load_w(stack_buf, src2d, kc, eng="pool"):
            P.dma(eng, stack_buf.t[:, 0:kc, 0:src2d.shape[1]], src2d.rearrange("(k p) c -> p k c", p=128), writes=[stack_buf.k])

        def rms_rstd(srcs, N, dsz, ones_ap, eps=EPS, f32mm=False):
            bank = stat_banks.next()
            for i, (ap, tk) in enumerate(srcs):
                sq = sqr.next()
                op("act", lambda e, sq=sq, ap=ap: e.activation(out=sq.t[:, 0:N], in_=ap, func=AF.Square), reads=tk, writes=[sq.k])
                op("pe", lambda e, sq=sq, i=i: e.matmul(pb[bank][:, 0:N], lhsT=ones_ap, rhs=sq.t[:, 0:N], start=(i == 0), stop=(i == len(srcs) - 1)),
                   reads=[sq.k, cb.k], writes=pkr(bank, 0, N))
            r = rstd_r.next()
            op("act", lambda e: e.activation(out=r.t[:, 0:N], in_=pb[bank][:, 0:N], func=AF.Ln, bias=eps, scale=1.0 / dsz), reads=pkr(bank, 0, N), writes=[r.k])
            op("act", lambda e: e.activation(out=r.t[:, 0:N], in_=r.t[:, 0:N], func=AF.Exp, scale=-0.5), reads=[r.k], writes=[r.k])
            return r

        with scope() as sa:
            scb = sbuf(sa, "scb", [128, 8, 2], BF16)
            op("act", lambda e: e.activation(out=scb.t[:].rearrange("p k j -> p (k j)"), in_=cin_s.t[:, :], func=AF.Silu), reads=[cin_s.k], writes=[scb.k])
            wmr = Ring([sbuf(sa, "wm%d" % i, [128, 8, 1536], BF16) for i in range(2)])
            for l in range(2):
                for blk in range(4):
                    wm = wmr.next()
                    load_w(wm, w_mod[l, :, blk * 1536:(blk + 1) * 1536], 8)
                    bank = blk % 2
                    for m in range(12):
                        for k in range(8):
                            op("pe", lambda e, wm=wm, m=m, k=k: e.matmul(pb[bank][:, 2 * m:2 * m + 2], lhsT=wm.t[:, k, m * 128:(m + 1) * 128], rhs=scb.t[:, k, :],
                                                                             start=(k == 0), stop=(k == 7)), reads=[wm.k, scb.k], writes=pkr(bank, 0, 24))
                    bcol = l * V_L + 32 + blk * 12
                    op("dve", lambda e, l=l, blk=blk, bcol=bcol: e.tensor_tensor(
                        out=modv.t[:, l, blk * 12:(blk + 1) * 12, :], in0=pb[bank][:, 0:24].rearrange("p (m j) -> p m j", j=2),
                        in1=vec.t[:, bcol:bcol + 12].unsqueeze(2).to_broadcast([128, 12, 2]), op=ALU.add),
                        reads=pkr(bank, 0, 24) + [vec.k], writes=[modv.k])
            for l in range(2):
                for (Ab, sc0, gcol) in ((A1, 8, l * V_L + 0), (A2, 32, l * V_L + 16)):
                    op("dve", lambda e, Ab=Ab, sc0=sc0, gcol=gcol, l=l: e.scalar_tensor_tensor(
                        out=Ab.t[:, l, :, :], in0=modv.t[:, l, sc0:sc0 + 8, :], scalar=1.0,
                        in1=vec.t[:, gcol:gcol + 8].unsqueeze(2).to_broadcast([128, 8, 2]), op0=ALU.add, op1=ALU.mult),
                        reads=[modv.k, vec.k], writes=[Ab.k])
                for (Gb, g0, gcol) in ((G1, 16, l * V_L + 8), (G2, 40, l * V_L + 24)):
                    op("dve", lambda e, Gb=Gb, g0=g0, gcol=gcol, l=l: e.tensor_tensor(
                        out=Gb.t[:, l, :], in0=modv.t[:, l, g0:g0 + 8, 0], in1=vec.t[:, gcol:gcol + 8], op=ALU.mult),
                        reads=[modv.k, vec.k], writes=[Gb.k])
            tap("mod", modv.t[:].rearrange("p l m j -> p (l m j)"), [modv.k], [128, 192])

        def norm_mod(srcs, N, outs, Ab, l, j, shift0):
            r = rms_rstd(srcs, N, D, onesb)
            for k in range(8):
                t = tmp_r.next()
                op("dve", lambda e, k=k, t=t: e.scalar_tensor_tensor(out=t.t[:, 0:N], in0=srcs[k][0], scalar=Ab.t[:, l, k, j:j + 1], in1=r.t[:, 0:N],
                                                                     op0=ALU.mult, op1=ALU.mult), reads=srcs[k][1] + [Ab.k, r.k], writes=[t.k])
                op("act", lambda e, k=k, t=t: e.activation(out=outs[k][0], in_=t.t[:, 0:N], func=AF.Identity, bias=modv.t[:, l, shift0 + k, j:j + 1], scale=1.0),
                   reads=[t.k, modv.k], writes=outs[k][1])

        mix = sbuf(st, "mix", [128, 8, NTP], BF16)

        with scope() as s0:
            aT = sbuf(s0, "aT", [128, 8, TF], BF16)
            with scope() as sb_:
                xs_r = Ring([sbuf(sb_, "xs%d" % i, [128, 8, 512], F32) for i in range(2)])
                for ti, (c0, N) in enumerate(FULL_TILES):
                    xs = xs_r.next()
                    src = ctxT[:, 0:TC] if ti == 0 else xT[:, c0 - TC:c0 - TC + N]
                    P.dma("sp", xs.t[:, :, 0:N], src.rearrange("(k p) n -> p k n", p=128), writes=[xs.k])
                    norm_mod([(xs.t[:, k, 0:N], [xs.k]) for k in range(8)], N,
                             [(aT.t[:, k, c0:c0 + N], [aT.tr(ti)]) for k in range(8)], A1, 0, 1 if ti == 0 else 0, 0)
            tap("aT", aT.t[:, :, 0:768], [aT.tr(0), aT.tr(1)], [128, 8, 768])
            if stop_after == "aT":
                P.finish()
                return nc, tap_out

            def aT_rhs(k, c0, N):
                trs = [aT.tr(ti) for ti, (t0, tn) in enumerate(FULL_TILES) if t0 < c0 + N and c0 < t0 + tn]
                return aT.t[:, k, c0:c0 + N], trs

            with scope() as sB:
                qbT = sbuf(sB, "qbT", [128, 4, NT], BF16)
                kbT = sbuf(sB, "kbT", [128, TF], BF16)
                vaug = sbuf(sB, "vaug", [128, NCH, 2, 128], BF16)
                wq = sbuf(sB, "wq", [128, 8, 512], BF16)
                wkv = sbuf(sB, "wkv", [128, 8, 256], BF16)
                rope_r = Ring([sbuf(sB, "rope%d" % i, [128, 2, 512], F32) for i in range(2)])
                kn_r = Ring([sbuf(sB, "kn%d" % i, [128, 512], F32) for i in range(2)])
                pt_r = Ring([sbuf(sB, "pt%d" % i, [128, 512], BF16) for i in range(4)])
                rs_r = Ring([sbuf(sB, "rs%d" % i, [128, 512], F32) for i in range(2)])
                load_w(wq, hyb_w_in[:, 2064:2576], 8)
                load_w(wkv, hyb_w_in[:, 2576:2832], 8)
                op("pool", lambda e: e.memset(vaug.t[:, :, 0, 64:128], 1.0), writes=[vaug.k])
                op("pool", lambda e: e.memset(vaug.t[:, :, 1, 0:64], 1.0), writes=[vaug.k])
                mmb = Ring([0, 1, 2, 3])

                def qk_tile(wbuf, wc0, c0, N, gcol, use_rope, rope_c0, dst_ap, dst_trk):
                    bank = mmb.next()
                    for k in range(8):
                        rhs, trs = aT_rhs(k, c0, N)
                        op("pe", lambda e, k=k, rhs=rhs: e.matmul(pb[bank][:, 0:N], lhsT=wbuf.t[:, k, wc0:wc0 + 128], rhs=rhs, start=(k == 0), stop=(k == 7)),
                           reads=[wbuf.k] + trs, writes=pkr(bank, 0, N))
                    r = rms_rstd([(pb[bank][:, 0:N], pkr(bank, 0, N))], N, 64, blkb)
                    kn = kn_r.next()
                    if not use_rope:
                        op("dve", lambda e: e.scalar_tensor_tensor(out=dst_ap, in0=pb[bank][:, 0:N], scalar=vcol(gcol), in1=r.t[:, 0:N], op0=ALU.mult, op1=ALU.mult),
                           reads=pkr(bank, 0, N) + [vec.k, r.k], writes=dst_trk)
                        return
                    op("dve", lambda e: e.scalar_tensor_tensor(out=kn.t[:, 0:N], in0=pb[bank][:, 0:N], scalar=vcol(gcol), in1=r.t[:, 0:N], op0=ALU.mult, op1=ALU.mult),
                       reads=pkr(bank, 0, N) + [vec.k, r.k], writes=[kn.k])
                    rp = rope_r.next()
                    P.dma("sp", rp.t[:, :, 0:N], rope_d[:, :, rope_c0:rope_c0 + N].rearrange("c p n -> p c n"), writes=[rp.k])
                    b2 = mmb.next()
                    op("pe", lambda e: e.matmul(pb[b2][:, 0:N], lhsT=cst.t[:, C_ROT:C_ROT + 128], rhs=kn.t[:, 0:N], start=True, stop=True),
                       reads=[cst.k, kn.k], writes=pkr(b2, 0, N))
                    t1 = tmp_r.next(); t2 = tmp_r.next()
                    op("dve", lambda e: e.tensor_tensor(out=t1.t[:, 0:N], in0=kn.t[:, 0:N], in1=rp.t[:, 0, 0:N], op=ALU.mult), reads=[kn.k, rp.k], writes=[t1.k])
                    op("dve", lambda e: e.tensor_tensor(out=t2.t[:, 0:N], in0=pb[b2][:, 0:N], in1=rp.t[:, 1, 0:N], op=ALU.mult), reads=pkr(b2, 0, N) + [rp.k], writes=[t2.k])
                    op("dve", lambda e: e.tensor_tensor(out=dst_ap, in0=t1.t[:, 0:N], in1=t2.t[:, 0:N], op=ALU.add), reads=[t1.k, t2.k], writes=dst_trk)

                for ti, (c0, N) in enumerate(FULL_TILES):
                    qk_tile(wkv, 0, c0, N, V_KN, ti > 0, c0 - TC, kbT.t[:, c0:c0 + N], [kbT.k])
                for n in range(NCH):
                    bank = mmb.next()
                    for k in range(8):
                        lhs, trs = aT_rhs(k, n * 128, 128)
                        op("pe", lambda e, k=k, lhs=lhs: e.matmul(pb[bank][:, 0:128], lhsT=lhs, rhs=wkv.t[:, k, 128:256], start=(k == 0), stop=(k == 7)),
                           reads=[wkv.k] + trs, writes=pkr(bank, 0, 128))
                    op("act", lambda e, n=n: e.activation(out=vaug.t[:, n, 0, 0:64], in_=pb[bank][:, 0:64], func=AF.Copy), reads=pkr(bank, 0, 128), writes=[vaug.k])
                    op("act", lambda e, n=n: e.activation(out=vaug.t[:, n, 1, 64:128], in_=pb[bank][:, 64:128], func=AF.Copy), reads=pkr(bank, 0, 128), writes=[vaug.k])
                for c in range(4):
                    for (c0, N) in OWN_TILES:
                        qk_tile(wq, c * 128, TC + c0, N, V_QN, True, c0, qbT.t[:, c, c0:c0 + N], [qbT.tr(c)])
                tap("kbT", kbT.t[:, 0:1024], [kbT.k], [128, 1024])
                tap("qbT", qbT.t[:, 0, 0:512], [qbT.tr(0)], [128, 512])
                sbank = Ring([0, 1, 2, 3, 6, 7])
                obank = Ring([4, 5])
                LOOK = 2
                iters = [(c, c0, N, n) for c in range(4) for (c0, N) in OWN_TILES for n in range(NCH)]
                sb_of = {}
                obs = [4, 5]

                def issue_qk(i):
                    c, c0, N, n = iters[i]
                    bl = []
                    for hh in range(2):
                        sbk = sbank.next()
                        bl.append(sbk)
                        op("pe", lambda e: e.matmul(pb[sbk][:, 0:N], lhsT=kbT.t[64 * hh:64 * hh + 64, n * 128:(n + 1) * 128],
                                                    rhs=qbT.t[64 * hh:64 * hh + 64, c, c0:c0 + N], start=True, stop=True),
                           reads=[kbT.k, qbT.tr(c)], writes=pkr(sbk, 0, N))
                    sb_of[i] = bl

                for i in range(min(LOOK, len(iters))):
                    issue_qk(i)
                for i, (c, c0, N, n) in enumerate(iters):
                    bl = sb_of.pop(i)
                    pts = []
                    for sbk in bl:
                        pt = pt_r.next()
                        pts.append(pt)
                        op("act", lambda e: e.activation(out=pt.t[:, 0:N], in_=pb[sbk][:, 0:N], func=AF.Exp, scale=0.125), reads=pkr(sbk, 0, N), writes=[pt.k])
                    if i + LOOK < len(iters):
                        issue_qk(i + LOOK)
                    for hh, pt in enumerate(pts):
                        ob = obs[hh]
                        op("pe", lambda e: e.matmul(pb[ob][:, 0:N], lhsT=vaug.t[:, n, hh, :], rhs=pt.t[:, 0:N], start=(n == 0), stop=(n == NCH - 1)),
                           reads=[vaug.k, pt.k], writes=pkr(ob, 0, N))
                    if n == NCH - 1:
                        for hh in range(2):
                            ob = obs[hh]
                            rs = rs_r.next()
                            lo, so = (0, 64) if hh == 0 else (64, 0)
                            op("dve", lambda e: e.reciprocal(out=rs.t[so:so + 64, 0:N], in_=pb[ob][so:so + 64, 0:N]), reads=pkr(ob, 0, N), writes=[rs.k])
                            op("dve", lambda e: e.tensor_tensor(out=mix.t[lo:lo + 64, 4 + c, c0:c0 + N], in0=pb[ob][lo:lo + 64, 0:N], in1=rs.t[so:so + 64, 0:N], op=ALU.mult),
                               reads=pkr(ob, 0, N) + [rs.k], writes=[mix.tr(4 + c)])
                tap("ob", mix.t[:, 4, 0:512], [mix.tr(4)], [128, 512])
            if stop_after == "attn":
                for k in range(4, 8):
                    P.dma("pool", outT[k * 128:(k + 1) * 128, :], mix.t[:, k, 0:NOWN], reads=[mix.tr(k)])
                P.finish()
                return nc, tap_out

            with scope() as sA:
                QK = sbuf(sA, "QK", [128, 2, TF], BF16)
                vT = sbuf(sA, "vT", [128, TF], BF16)
                PRE = TF + 8
                pre = sbuf(sA, "pre", [128, PRE], BF16)
                oT = sbuf(sA, "oT", [128, NT], F32)
                wh = Ring([sbuf(sA, "wh%d" % i, [128, 8, 128], BF16) for i in range(2)])
                wba = sbuf(sA, "wba", [128, 8, 16], BF16)
                acc_r = Ring([sbuf(sA, "acc%d" % i, [128, 512], F32) for i in range(1)])
                tb = {nm: sbuf(sA, "tb_" + nm, [128, 16 if nm == "ba" else 8, NCH], F32)
                      for nm in ("ba", "beta", "g", "cum", "gtot", "bec", "etail")}
                nexpA = sbuf(sA, "nexpA", [128, 8], F32)
                op("pool", lambda e: e.memset(pre.t[:, :], 0.0), writes=[pre.k])
                load_w(wba, hyb_w_in[:, 2048:2064], 8)
                for n in range(NCH):
                    bank = n % 4
                    for k in range(8):
                        lhs, trs = aT_rhs(k, n * 128, 128)
                        op("pe", lambda e: e.matmul(pb[bank][:, 0:16], lhsT=lhs, rhs=wba.t[:, k, :], start=(k == 0), stop=(k == 7)),
                           reads=[wba.k] + trs, writes=pkr(bank, 0, 16))
                    op("act", lambda e: e.activation(out=tb["ba"].t[:, :, n], in_=pb[bank][:, 0:16], func=AF.Copy), reads=pkr(bank, 0, 16), writes=[tb["ba"].k])
                op("act", lambda e: e.activation(out=tb["beta"].t[:, :, :], in_=tb["ba"].t[:, 0:8, :], func=AF.Sigmoid), reads=[tb["ba"].k], writes=[tb["beta"].k])
                op("dve", lambda e: e.tensor_tensor(out=tb["g"].t[:, :, :], in0=tb["ba"].t[:, 8:16, :], in1=abc.t[:, 8:16].unsqueeze(2).to_broadcast([128, 8, NCH]), op=ALU.add),
                   reads=[tb["ba"].k, abc.k], writes=[tb["g"].k])
                op("act", lambda e: e.activation(out=tb["g"].t[:, :, :], in_=tb["g"].t[:, :, :], func=AF.Exp), reads=[tb["g"].k], writes=[tb["g"].k])
                op("act", lambda e: e.activation(out=tb["g"].t[:, :, :], in_=tb["g"].t[:, :, :], func=AF.Ln, bias=1.0), reads=[tb["g"].k], writes=[tb["g"].k])
                op("act", lambda e: e.activation(out=nexpA.t[:, :], in_=abc.t[:, 0:8], func=AF.Exp), reads=[abc.k], writes=[nexpA.k])
                op("dve", lambda e: e.tensor_scalar(out=nexpA.t[:, :], in0=nexpA.t[:, :], scalar1=-1.0, scalar2=None, op0=ALU.mult), reads=[nexpA.k], writes=[nexpA.k])
                op("dve", lambda e: e.tensor_tensor(out=tb["g"].t[:, :, :], in0=tb["g"].t[:, :, :], in1=nexpA.t[:, :].unsqueeze(2).to_broadcast([128, 8, NCH]), op=ALU.mult),
                   reads=[tb["g"].k, nexpA.k], writes=[tb["g"].k])
                gflat = tb["g"].t[:].rearrange("p j n -> p (j n)")
                NJ = 4 * NCH
                op("pe", lambda e: e.matmul(pb[0][:, 0:NJ], lhsT=cst.t[:, C_TRIA:C_TRIA + 128], rhs=gflat[:, 0:NJ], start=True, stop=True), reads=[cst.k, tb["g"].k], writes=pkr(0, 0, NJ))
                op("pe", lambda e: e.matmul(pb[1][:, 0:NJ], lhsT=cst.t[:, C_TRID:C_TRID + 128], rhs=gflat[:, NJ:2 * NJ], start=True, stop=True), reads=[cst.k, tb["g"].k], writes=pkr(1, 0, NJ))
                op("pe", lambda e: e.matmul(pb[2][:, 0:2 * NJ], lhsT=onesf, rhs=gflat[:, 0:2 * NJ], start=True, stop=True), reads=[cst.k, tb["g"].k], writes=pkr(2, 0, 2 * NJ))
                cflat = tb["cum"].t[:].rearrange("p j n -> p (j n)")
                op("act", lambda e: e.activation(out=cflat[:, 0:NJ], in_=pb[0][:, 0:NJ], func=AF.Copy), reads=pkr(0, 0, NJ), writes=[tb["cum"].k])
                op("act", lambda e: e.activation(out=cflat[:, NJ:2 * NJ], in_=pb[1][:, 0:NJ], func=AF.Copy), reads=pkr(1, 0, NJ), writes=[tb["cum"].k])
                op("act", lambda e: e.activation(out=tb["gtot"].t[:].rearrange("p j n -> p (j n)"), in_=pb[2][:, 0:2 * NJ], func=AF.Copy), reads=pkr(2, 0, 2 * NJ), writes=[tb["gtot"].k])
                op("dve", lambda e: e.tensor_tensor(out=tb["etail"].t[:, :, :], in0=tb["gtot"].t[:, :, :], in1=tb["cum"].t[:, :, :], op=ALU.subtract), reads=[tb["gtot"].k, tb["cum"].k], writes=[tb["etail"].k])
                op("act", lambda e: e.activation(out=tb["etail"].t[:, :, :], in_=tb["etail"].t[:, :, :], func=AF.Exp), reads=[tb["etail"].k], writes=[tb["etail"].k])
                op("act", lambda e: e.activation(out=tb["gtot"].t[:, :, :], in_=tb["gtot"].t[:, :, :], func=AF.Exp), reads=[tb["gtot"].k], writes=[tb["gtot"].k])
                tb["egl"] = tb["gtot"]
                op("act", lambda e: e.activation(out=tb["bec"].t[:, :, :], in_=tb["cum"].t[:, :, :], func=AF.Exp), reads=[tb["cum"].k], writes=[tb["bec"].k])
                op("dve", lambda e: e.tensor_tensor(out=tb["bec"].t[:, :, :], in0=tb["bec"].t[:, :, :], in1=tb["beta"].t[:, :, :], op=ALU.mult), reads=[tb["bec"].k, tb["beta"].k], writes=[tb["bec"].k])
                tap("beta", tb["beta"].t[:, :, :], [tb["beta"].k], [128, 8, NCH])
                tap("g", tb["g"].t[:, :, :], [tb["g"].k], [128, 8, NCH])
                if stop_after == "dtab":
                    P.finish()
                    return nc, tap_out

                PAR_NAMES = ("M", "L", "Ld", "iT", "qd", "vb", "kbg", "ktl", "Qb", "wTn")
                US = []
                for si in range(2):
                    u = {}
                    for nm, shp, dt in (("dg", [128, 256], F32), ("Es", [128, 128], F32), ("G", [128, 128], F32),
                                        ("M", [128, 128], F32), ("L", [128, 128], F32), ("LM0", [128, 256], F32), ("LM1", [128, 256], F32),
                                        ("Q", [128, 128], F32), ("Ld", [128, 128], F32), ("Xo", [128, 128], F32), ("Qt", [128, 128], F32), ("Ei", [128, 128], F32), ("eR", [128, 128], F32),
                                        ("Qb", [128, 128], BF16), ("vb", [128, 128], BF16), ("kbg", [128, 128], BF16), ("ktl", [128, 128], BF16),
                                        ("wTn", [128, 128], BF16), ("vn", [128, 128], BF16), ("qd", [128, 128], BF16), ("iT", [128, 128], BF16),
                                        ("S", [128, 128], F32), ("Sb", [128, 128], BF16)):
                        if nm in PAR_NAMES:
                            for p_ in range(2):
                                u["%s@%d" % (nm, p_)] = sbuf(sA, "u%d_%s_%d" % (si, nm, p_), shp, dt)
                        else:
                            u[nm] = sbuf(sA, "u%d_%s" % (si, nm), shp, dt)
                    u["banks"] = (0, 1, 2, 3) if si == 0 else (4, 5, 6, 7)
                    US.append(u)

                def unit(h, d, n, need_out, written, par=0):
                    u = dict(US[d])
                    for nm_ in PAR_NAMES:
                        u[nm_] = US[d]["%s@%d" % (nm_, par)]
                    bX, bY, bZ, bW = u["banks"]
                    j = d * 4 + h
                    col = lambda nm: tb[nm].t[:, j, n:n + 1]
                    cs = slice(n * 128, (n + 1) * 128)
                    negs = cst.t[:, C_NSA:C_NSA + 128] if d == 0 else cst.t[:, C_NSD:C_NSD + 128]
                    op("dve", lambda e: e.tensor_scalar(out=u["dg"].t[:, 0:128], in0=identf, scalar1=col("beta"), scalar2=None, op0=ALU.mult), reads=[cst.k, tb["beta"].k], writes=[u["dg"].k])
                    yield
                    op("dve", lambda e: e.tensor_scalar(out=u["dg"].t[:, 128:256], in0=identf, scalar1=col("cum"), scalar2=None, op0=ALU.mult), reads=[cst.k, tb["cum"].k], writes=[u["dg"].k])
                    yield
                    op("pe", lambda e: e.matmul(pb[bX][:, 0:256], lhsT=onesf, rhs=u["dg"].t[:, 0:256], start=True, stop=True), reads=[cst.k, u["dg"].k], writes=pkr(bX, 0, 256))
                    yield
                    op("pe", lambda e: e.matmul(pb[bX][:, 256:512].rearrange("p (a b) -> p a b", a=2), lhsT=QK.t[:, 0, cs], rhs=QK.t[:, :, cs], start=True, stop=True),
                       reads=[QK.k], writes=pkr(bX, 256, 256))
                    yield
                    op("dve", lambda e: e.scalar_tensor_tensor(out=u["Es"].t[:, :], in0=pb[bX][:, 128:256], scalar=col("cum"), in1=negs, op0=ALU.subtract, op1=ALU.add),
                       reads=pkr(bX, 128, 128) + [tb["cum"].k, cst.k], writes=[u["Es"].k])
                    yield
                    op("act", lambda e: e.activation(out=u["Es"].t[:, :], in_=u["Es"].t[:, :], func=AF.Exp), reads=[u["Es"].k], writes=[u["Es"].k])
                    yield
                    op("dve", lambda e: e.tensor_tensor(out=u["G"].t[:, :], in0=pb[bX][:, 0:128], in1=u["Es"].t[:, :], op=ALU.mult), reads=pkr(bX, 0, 128) + [u["Es"].k], writes=[u["G"].k])
                    yield
                    op("dve", lambda e: e.tensor_tensor(out=u["M"].t[:, :], in0=pb[bX][:, 256:384], in1=u["G"].t[:, :], op=ALU.mult), reads=pkr(bX, 256, 128) + [u["G"].k], writes=[u["M"].k])
                    yield
                    if need_out:
                        op("pool", lambda e: e.tensor_tensor(out=u["Ei"].t[:, :], in0=u["Es"].t[:, :], in1=identf, op=ALU.add), reads=[u["Es"].k, cst.k], writes=[u["Ei"].k])
                        yield
                        op("dve", lambda e: e.tensor_tensor(out=u["iT"].t[:, :], in0=pb[bX][:, 384:512], in1=u["Ei"].t[:, :], op=ALU.mult), reads=pkr(bX, 384, 128) + [u["Ei"].k], writes=[u["iT"].k])
                        yield
                        op("act", lambda e: e.activation(out=u["eR"].t[:, :], in_=pb[bX][:, 128:256], func=AF.Exp), reads=pkr(bX, 128, 128), writes=[u["eR"].k])
                        yield
                        op("dve", lambda e: e.tensor_tensor(out=u["qd"].t[:, :], in0=QK.t[:, 1, cs], in1=u["eR"].t[:, :], op=ALU.mult), reads=[QK.k, u["eR"].k], writes=[u["qd"].k])
                        yield
                    if UNIT_CUT == 1:
                        return
                    negsT = cst.t[:, C_NSD:C_NSD + 128] if d == 0 else cst.t[:, C_NSA:C_NSA + 128]
                    op("dve", lambda e: e.scalar_tensor_tensor(out=u["L"].t[:, :], in0=pb[bX][:, 128:256], scalar=-1.0, in1=negsT, op0=ALU.mult, op1=ALU.add),
                       reads=pkr(bX, 128, 128) + [cst.k], writes=[u["L"].k])
                    yield
                    op("act", lambda e: e.activation(out=u["L"].t[:, :], in_=u["L"].t[:, :], func=AF.Exp, bias=col("cum"), scale=1.0), reads=[u["L"].k, tb["cum"].k], writes=[u["L"].k])
                    yield
                    op("dve", lambda e: e.scalar_tensor_tensor(out=u["L"].t[:, :], in0=u["L"].t[:, :], scalar=col("beta"), in1=pb[bX][:, 256:384], op0=ALU.mult, op1=ALU.mult),
                       reads=[u["L"].k, tb["beta"].k] + pkr(bX, 256, 128), writes=[u["L"].k])
                    yield
                    blkf = cst.t[:, C_BLK:C_BLK + 128]
                    op("dve", lambda e: e.tensor_tensor(out=u["Ld"].t[:, :], in0=u["L"].t[:, :], in1=blkf, op=ALU.mult), reads=[u["L"].k, cst.k], writes=[u["Ld"].k])
                    yield
                    op("dve", lambda e: e.tensor_tensor(out=u["L"].t[:, :], in0=u["L"].t[:, :], in1=u["Ld"].t[:, :], op=ALU.subtract), reads=[u["L"].k, u["Ld"].k], writes=[u["L"].k])
                    yield
                    op("dve", lambda e: e.tensor_tensor(out=u["M"].t[:, :], in0=u["M"].t[:, :], in1=blkf, op=ALU.mult), reads=[u["M"].k, cst.k], writes=[u["M"].k])
                    yield
                    pbf = pb[bZ][:, 0:128].bitcast(BF16)
                    op("pe", lambda e: e.transpose(pbf[:, 0:128], QK.t[:, 0, cs], identb), reads=[QK.k, cb.k], writes=pkr(bZ, 0, 128))
                    yield
                    op("pe", lambda e: e.transpose(pbf[:, 128:256], vT.t[:, cs], identb), reads=[vT.k, cb.k], writes=pkr(bZ, 0, 128))
                    yield
                    op("act", lambda e: e.activation(out=u["vb"].t[:, :], in_=pbf[:, 128:256], func=AF.Copy, scale=col("beta")), reads=pkr(bZ, 0, 128) + [tb["beta"].k], writes=[u["vb"].k])
                    yield
                    op("dve", lambda e: e.tensor_scalar(out=u["kbg"].t[:, :], in0=pbf[:, 0:128], scalar1=col("bec"), scalar2=None, op0=ALU.mult), reads=pkr(bZ, 0, 128) + [tb["bec"].k], writes=[u["kbg"].k])
                    yield
                    op("dve", lambda e: e.tensor_scalar(out=u["ktl"].t[:, :], in0=pbf[:, 0:128], scalar1=col("etail"), scalar2=None, op0=ALU.mult), reads=pkr(bZ, 0, 128) + [tb["etail"].k], writes=[u["ktl"].k])
                    yield
                    yield "S2"
                    op("pool", lambda e: e.tensor_tensor(out=u["Q"].t[:, :], in0=identf, in1=u["M"].t[:, :], op=ALU.subtract), reads=[cst.k, u["M"].k], writes=[u["Q"].k])
                    yield
                    Lk, Lkt, Mk, Mkt = u["Ld"].t[:, :], [u["Ld"].k], u["M"].t[:, :], [u["M"].k]
                    for lv in range(5):
                        LM = u["LM%d" % (lv % 2)]
                        last = lv == 4
                        op("pe", lambda e: e.matmul(pb[bY][:, 128:256], lhsT=Mk, rhs=Lk, start=True, stop=True), reads=Lkt + Mkt, writes=pkr(bY, 128, 128))
                        yield
                        if not last:
                            op("pe", lambda e: e.matmul(pb[bY][:, 256:384], lhsT=Lk, rhs=Mk, start=True, stop=True), reads=Lkt + Mkt, writes=pkr(bY, 256, 128))
                            yield
                        w_ = 128 if last else 256
                        op("act", lambda e: e.activation(out=LM.t[:, 0:w_], in_=pb[bY][:, 128:128 + w_], func=AF.Copy), reads=pkr(bY, 128, w_), writes=[LM.k])
                        yield
                        op("pe", lambda e: e.matmul(pb[bW][:, 128:256], lhsT=LM.t[:, 0:128], rhs=u["Q"].t[:, :], start=True, stop=True), reads=[LM.k, u["Q"].k], writes=pkr(bW, 128, 128))
                        yield
                        op("dve", lambda e: e.tensor_tensor(out=u["Q"].t[:, :], in0=pb[bW][:, 128:256], in1=u["Q"].t[:, :], op=ALU.add), reads=pkr(bW, 128, 128) + [u["Q"].k], writes=[u["Q"].k])
                        yield
                        Lk, Lkt, Mk, Mkt = LM.t[:, 0:128], [LM.k], LM.t[:, 128:256], [LM.k]
                    op("pe", lambda e: e.matmul(pb[bY][:, 128:256], lhsT=u["Q"].t[:, :], rhs=identf, start=True, stop=True), reads=[u["Q"].k, cst.k], writes=pkr(bY, 128, 128))
                    yield
                    op("act", lambda e: e.activation(out=u["Qt"].t[:, :], in_=pb[bY][:, 128:256], func=AF.Copy), reads=pkr(bY, 128, 128), writes=[u["Qt"].k])
                    yield
                    op("pe", lambda e: e.matmul(pb[bW][:, 128:256], lhsT=u["L"].t[:, :], rhs=u["Q"].t[:, :], start=True, stop=True), reads=[u["L"].k, u["Q"].k], writes=pkr(bW, 128, 128))
                    yield
                    op("dve", lambda e: e.tensor_copy(out=u["Xo"].t[:, :], in_=pb[bW][:, 128:256]), reads=pkr(bW, 128, 128), writes=[u["Xo"].k])
                    yield
                    op("pe", lambda e: e.matmul(pb[bY][:, 128:256], lhsT=u["Qt"].t[:, :], rhs=u["Xo"].t[:, :], start=True, stop=True), reads=[u["Qt"].k, u["Xo"].k], writes=pkr(bY, 128, 128))
                    yield
                    op("dve", lambda e: e.tensor_tensor(out=u["Qb"].t[:, :], in0=u["Q"].t[:, :], in1=pb[bY][:, 128:256], op=ALU.subtract), reads=[u["Q"].k] + pkr(bY, 128, 128), writes=[u["Qb"].k])
                    yield
                    if UNIT_CUT == 2:
                        return
                    op("pe", lambda e: e.matmul(pb[bZ][:, 128:256], lhsT=u["kbg"].t[:, :], rhs=u["Qb"].t[:, :], start=True, stop=True), reads=[u["kbg"].k, u["Qb"].k], writes=pkr(bZ, 128, 128))
                    yield
                    op("act", lambda e: e.activation(out=u["wTn"].t[:, :], in_=pb[bZ][:, 128:256], func=AF.Copy, scale=-1.0), reads=pkr(bZ, 128, 128), writes=[u["wTn"].k])
                    yield
                    if UNIT_CUT == 3:
                        return
                    yield "S3"
                    op("pe", lambda e: e.matmul(pb[bZ][:, 256:384], lhsT=u["Qb"].t[:, :], rhs=u["vb"].t[:, :], start=True, stop=False), reads=[u["Qb"].k, u["vb"].k], writes=pkr(bZ, 256, 128))
                    yield
                    op("pe", lambda e: e.matmul(pb[bZ][:, 256:384], lhsT=u["wTn"].t[:, :], rhs=u["Sb"].t[:, :], start=False, stop=True), reads=[u["wTn"].k, u["Sb"].k], writes=pkr(bZ, 256, 128))
                    yield
                    op("act", lambda e: e.activation(out=u["vn"].t[:, :], in_=pb[bZ][:, 256:384], func=AF.Copy), reads=pkr(bZ, 256, 128), writes=[u["vn"].k])
                    yield
                    if need_out:
                        t0 = (n - 2) * 128
                        op("pe", lambda e: e.matmul(pb[bZ][:, 384:512], lhsT=u["Sb"].t[:, :], rhs=u["qd"].t[:, :], start=True, stop=False), reads=[u["Sb"].k, u["qd"].k], writes=pkr(bZ, 384, 128))
                        yield
                        op("pe", lambda e: e.matmul(pb[bZ][:, 384:512], lhsT=u["vn"].t[:, :], rhs=u["iT"].t[:, :], start=False, stop=True), reads=[u["vn"].k, u["iT"].k], writes=pkr(bZ, 384, 128))
                        yield
                        if n not in written:
                            op("act", lambda e: e.activation(out=oT.t[:, t0:t0 + 128], in_=pb[bZ][:, 384:512], func=AF.Copy), reads=pkr(bZ, 384, 128), writes=[oT.tr(n)])
                            yield
                            written.add(n)
                        else:
                            op("dve", lambda e: e.tensor_tensor(out=oT.t[:, t0:t0 + 128], in0=pb[bZ][:, 384:512], in1=oT.t[:, t0:t0 + 128], op=ALU.add), reads=pkr(bZ, 384, 128) + [oT.tr(n)], writes=[oT.tr(n)])
                            yield
                    op("pe", lambda e: e.matmul(pb[bW][:, 0:128], lhsT=u["ktl"].t[:, :], rhs=u["vn"].t[:, :], start=True, stop=True), reads=[u["ktl"].k, u["vn"].k], writes=pkr(bW, 0, 128))
                    yield
                    op("dve", lambda e: e.scalar_tensor_tensor(out=u["S"].t[:, :], in0=u["S"].t[:, :], scalar=col("egl"), in1=pb[bW][:, 0:128], op0=ALU.mult, op1=ALU.add),
                       reads=[u["S"].k, tb["egl"].k] + pkr(bW, 0, 128), writes=[u["S"].k])
                    yield
                    op("act", lambda e: e.activation(out=u["Sb"].t[:, :], in_=u["S"].t[:, :], func=AF.Copy), reads=[u["S"].k], writes=[u["Sb"].k])
                    yield

                NOUT = NT // 128
                asc_order = [0, 1] + list(range(2, 2 + NOUT))
                desc_order = [1, 0] + list(range(NCH - 1, 1, -1))
                for h in range(4):
                    for si, (wc0, dst) in enumerate(((h * 128, "q"), (512 + h * 128, "k"), (1024 + h * 128, "v"))):
                        w_ = wh.next()
                        load_w(w_, hyb_w_in[:, wc0:wc0 + 128], 8)
                        for ti, (c0, N) in enumerate(FULL_TILES):
                            bank = ti % 4
                            for k in range(8):
                                rhs, trs = aT_rhs(k, c0, N)
                                op("pe", lambda e: e.matmul(pb[bank][:, 0:N], lhsT=w_.t[:, k, :], rhs=rhs, start=(k == 0), stop=(k == 7)), reads=[w_.k] + trs, writes=pkr(bank, 0, N))
                            po = 2 if ti == 0 else 6 + c0
                            op("act", lambda e: e.activation(out=pre.t[:, po:po + N], in_=pb[bank][:, 0:N], func=AF.Copy), reads=pkr(bank, 0, N), writes=[pre.tr(ti)])
                        ch = {"q": 0, "k": 4, "v": 8}[dst] + h
                        for ti, (c0, N) in enumerate(FULL_TILES):
                            po = 0 if ti == 0 else 4 + c0
                            acc = acc_r.next()
                            eng = "dve"
                            ptr = [pre.k] + [pre.tr(t_) for t_ in (ti - 1, ti, ti + 1) if 0 <= t_ < len(FULL_TILES)]
                            op(eng, lambda e: e.tensor_scalar(out=acc.t[:, 0:N], in0=pre.t[:, po:po + N], scalar1=vcol(V_HCONV + ch), scalar2=None, op0=ALU.mult), reads=ptr + [vec.k], writes=[acc.k])
                            for jt in range(1, 5):
                                op(eng, lambda e: e.scalar_tensor_tensor(out=acc.t[:, 0:N], in0=pre.t[:, po + jt:po + jt + N], scalar=vcol(V_HCONV + jt * 12 + ch), in1=acc.t[:, 0:N], op0=ALU.mult, op1=ALU.add),
                                   reads=ptr + [vec.k, acc.k], writes=[acc.k])
                            if dst == "v":
                                op("act", lambda e: e.activation(out=vT.t[:, c0:c0 + N], in_=acc.t[:, 0:N], func=AF.Silu), reads=[acc.k], writes=[vT.k])
                            else:
                                op("act", lambda e: e.activation(out=acc.t[:, 0:N], in_=acc.t[:, 0:N], func=AF.Silu), reads=[acc.k], writes=[acc.k])
                                r = rms_rstd([(acc.t[:, 0:N], [acc.k])], N, 1.0, onesb)
                                sc_ = (128.0 ** -0.5) if dst == "q" else 1.0
                                op("dve", lambda e: e.scalar_tensor_tensor(out=QK.t[:, 1 if dst == "q" else 0, c0:c0 + N], in0=acc.t[:, 0:N], scalar=sc_, in1=r.t[:, 0:N], op0=ALU.mult, op1=ALU.mult),
                                   reads=[acc.k, r.k], writes=[QK.k])
                    if h == 0:
                        tap("qT", QK.t[:, 1, 0:768], [QK.k], [128, 768])
                        tap("kT", QK.t[:, 0, 0:768], [QK.k], [128, 768])
                        tap("vT", vT.t[:, 0:768], [vT.k], [128, 768])
                        if stop_after == "dproj":
                            P.finish()
                            return nc, tap_out
                    for d in range(2):
                        op("pool", lambda e: e.memset(US[d]["S"].t[:, :], 0.0), writes=[US[d]["S"].k])
                        op("pool", lambda e: e.memset(US[d]["Sb"].t[:, :], 0.0), writes=[US[d]["Sb"].k])
                    written = set()
                    if stop_after == "dunit":
                        for _ in unit(h, 0, 0, False, written):
                            pass
                        for _ in unit(h, 1, 3, True, written):
                            pass
                        tap("M", US[0]["M"].t[:, :], [US[0]["M"].k], [128, 128])
                        tap("Q", US[0]["Q"].t[:, :], [US[0]["Q"].k], [128, 128])
                        P.finish()
                        return nc, tap_out
                    orders = [asc_order, desc_order]
                    nxt = [0, 0]
                    act = [[], []]
                    while act[0] or act[1] or nxt[0] < len(orders[0]) or nxt[1] < len(orders[1]):
                        for d_ in (1, 0, 1):
                            if nxt[d_] < len(orders[d_]) and len(act[d_]) < 2 and (not act[d_] or act[d_][-1][1] >= 2 or act[d_][-1][2] is not None):
                                i_ = nxt[d_]
                                n_ = orders[d_][i_]
                                nxt[d_] += 1
                                need_ = (n_ >= 2) if d_ == 0 else (2 <= n_ < 2 + NOUT)
                                act[d_].append([unit(h, d_, n_, need_, written, i_ % 2), 1, None])
                            for ent in list(act[d_]):
                                if ent[2] == "S2":
                                    if act[d_][0] is ent or act[d_][0][1] == 3:
                                        ent[1], ent[2] = 2, None
                                    else:
                                        continue
                                elif ent[2] == "S3":
                                    if act[d_][0] is ent:
                                        ent[1], ent[2] = 3, None
                                    else:
                                        continue
                                try:
                                    r_ = next(ent[0])
                                except StopIteration:
                                    act[d_].remove(ent)
                                    continue
                                if r_ in ("S2", "S3"):
                                    ent[2] = r_
                    if h == 0:
                        tap("oT", oT.t[:, 0:512], [oT.tr(n_) for n_ in range(2, 6)], [128, 512])
                    wz = wh.next()
                    load_w(wz, hyb_w_in[:, 1536 + h * 128:1536 + (h + 1) * 128], 8)
                    for (c0, N) in OWN_TILES:
                        otr = [oT.tr(n_) for n_ in range(2 + c0 // 128, 2 + (c0 + N) // 128)]
                        r = rms_rstd([(oT.t[:, c0:c0 + N], otr)], N, 128.0, onesb)
                        bank = 0
                        for k in range(8):
                            rhs, trs = aT_rhs(k, TC + c0, N)
                            op("pe", lambda e: e.matmul(pb[bank][:, 0:N], lhsT=wz.t[:, k, :], rhs=rhs, start=(k == 0), stop=(k == 7)), reads=[wz.k] + trs, writes=pkr(bank, 0, N))
                        t1 = tmp_r.next(); t2 = tmp_r.next()
                        op("act", lambda e: e.activation(out=t1.t[:, 0:N], in_=pb[bank][:, 0:N], func=AF.Silu), reads=pkr(bank, 0, N), writes=[t1.k])
                        op("dve", lambda e: e.scalar_tensor_tensor(out=t2.t[:, 0:N], in0=oT.t[:, c0:c0 + N], scalar=vcol(V_ONORM), in1=r.t[:, 0:N], op0=ALU.mult, op1=ALU.mult),
                           reads=otr + [vec.k, r.k], writes=[t2.k])
                        op("dve", lambda e: e.tensor_tensor(out=mix.t[:, h, c0:c0 + N], in0=t2.t[:, 0:N], in1=t1.t[:, 0:N], op=ALU.mult), reads=[t1.k, t2.k], writes=[mix.tr(h)])
                tap("ya", mix.t[:, 0, 0:512], [mix.tr(0)], [128, 512])
            if stop_after == "delta":
                P.finish()
                return nc, tap_out

        h = sbuf(st, "h", [128, 8, NT], F32)
        for ti, (c0, N) in enumerate(OWN_TILES):
            P.dma("sp", h.t[:, :, c0:c0 + N], xT[:, c0:c0 + N].rearrange("(k p) n -> p k n", p=128), writes=[h.tr(ti)])
        yb_r = Ring([sbuf(st, "yb%d" % i, [128, 8, 512], F32) for i in range(2)])
        mmb2 = Ring([0, 1, 2, 3, 4, 5])

        def post_res(yb, N, l, Gb, ti, c0):
            r = rms_rstd([(yb.t[:, m, 0:N], [yb.k]) for m in range(8)], N, D, onesb)
            for m in range(8):
                t = tmp_r.next()
                op("dve", lambda e: e.scalar_tensor_tensor(out=t.t[:, 0:N], in0=yb.t[:, m, 0:N], scalar=Gb.t[:, l, m:m + 1], in1=r.t[:, 0:N], op0=ALU.mult, op1=ALU.mult),
                   reads=[yb.k, Gb.k, r.k], writes=[t.k])
                op("dve", lambda e: e.tensor_tensor(out=h.t[:, m, c0:c0 + N], in0=h.t[:, m, c0:c0 + N], in1=t.t[:, 0:N], op=ALU.add), reads=[t.k, h.tr(ti)], writes=[h.tr(ti)])

        with scope() as sF:
            wo = sbuf(sF, "wo", [128, 8, D], BF16)
            load_w(wo, hyb_w_out[:, :], 8)
            for ti, (c0, N) in enumerate(OWN_TILES):
                yb = yb_r.next()
                for m in range(8):
                    bank = mmb2.next()
                    for k in range(8):
                        op("pe", lambda e: e.matmul(pb[bank][:, 0:N], lhsT=wo.t[:, k, m * 128:(m + 1) * 128], rhs=mix.t[:, k, c0:c0 + N], start=(k == 0), stop=(k == 7)),
                           reads=[wo.k, mix.tr(k)], writes=pkr(bank, 0, N))
                    op("act", lambda e: e.activation(out=yb.t[:, m, 0:N], in_=pb[bank][:, 0:N], func=AF.Copy), reads=pkr(bank, 0, N), writes=[yb.k])
                if ti == 0:
                    tap("ylat", yb.t[:, 0, 0:512], [yb.k], [128, 512])
                post_res(yb, N, 0, G1, ti, c0)
        tap("hmix0", h.t[:, 0, 0:512], [h.tr(0)], [128, 512])
        if stop_after == "mix0":
            P.finish()
            return nc, tap_out

        def ffn(l, tiles):
            groups = [tiles[i:i + 2] for i in range(0, len(tiles), 2)]
            with scope() as sf:
                mixflat = mix.t[:].rearrange("p k n -> p (k n)")
                fx = sbuf(sf, "fx", [128, 13, 1024], BF16)
                ag_k = [Trk(), Trk()]
                hid_k = [[Trk() for _ in range(FFC)] for _ in range(2)]
                agv = lambda k, s_, N: fx.t[:, k, s_ * 512:s_ * 512 + N]

                def hidv(j, s_, N):
                    if j < 17:
                        return mixflat[:, j * 1024 + s_ * 512:j * 1024 + s_ * 512 + N]
                    return fx.t[:, 8 + j - 17, s_ * 512:s_ * 512 + N]
                wg_r = Ring([sbuf(sf, "wg%d" % i, [128, 8, 256], BF16) for i in range(2)])
                wu_r = Ring([sbuf(sf, "wu%d" % i, [128, 8, 256], BF16) for i in range(2)])
                wo_r = Ring([sbuf(sf, "wfo%d" % i, [128, FFC, 128], BF16) for i in range(2)])
                for grp in groups:
                    for s_, (ti, (c0, N)) in enumerate(grp):
                        norm_mod([(h.t[:, k, c0:c0 + N], [h.tr(ti)]) for k in range(8)], N,
                                 [(agv(k, s_, N), [ag_k[s_]]) for k in range(8)], A2, l, 0, 24)
                    for jb in range(FFC // 2):
                        wg = wg_r.next(); wu = wu_r.next()
                        load_w(wg, w_ffn_in[l, :, jb * 256:(jb + 1) * 256], 8)
                        load_w(wu, w_ffn_in[l, :, FF + jb * 256:FF + (jb + 1) * 256], 8)
                        for jj in range(2):
                            j = 2 * jb + jj
                            for s_, (ti, (c0, N)) in enumerate(grp):
                                bg = mmb2.next(); bu = mmb2.next()
                                for k in range(8):
                                    op("pe", lambda e: e.matmul(pb[bg][:, 0:N], lhsT=wg.t[:, k, jj * 128:(jj + 1) * 128], rhs=agv(k, s_, N), start=(k == 0), stop=(k == 7)),
                                       reads=[wg.k, ag_k[s_]], writes=pkr(bg, 0, N))
                                for k in range(8):
                                    op("pe", lambda e: e.matmul(pb[bu][:, 0:N], lhsT=wu.t[:, k, jj * 128:(jj + 1) * 128], rhs=agv(k, s_, N), start=(k == 0), stop=(k == 7)),
                                       reads=[wu.k, ag_k[s_]], writes=pkr(bu, 0, N))
                                t = tmp_r.next()
                                op("act", lambda e: e.activation(out=t.t[:, 0:N], in_=pb[bg][:, 0:N], func=AF.Silu), reads=pkr(bg, 0, N), writes=[t.k])
                                op("dve", lambda e: e.tensor_tensor(out=hidv(j, s_, N), in0=pb[bu][:, 0:N], in1=t.t[:, 0:N], op=ALU.mult), reads=pkr(bu, 0, N) + [t.k], writes=[hid_k[s_][j]])
                    ybs = [yb_r.next() for _ in grp]
                    for m in range(8):
                        wfo = wo_r.next()
                        P.dma("pool", wfo.t[:, :, :], w_ffn_out[l, :, m * 128:(m + 1) * 128].rearrange("(j p) c -> p j c", p=128), writes=[wfo.k])
                        for s_, (ti, (c0, N)) in enumerate(grp):
                            bank = mmb2.next()
                            for j in range(FFC):
                                op("pe", lambda e: e.matmul(pb[bank][:, 0:N], lhsT=wfo.t[:, j, :], rhs=hidv(j, s_, N), start=(j == 0), stop=(j == FFC - 1)),
                                   reads=[wfo.k, hid_k[s_][j]], writes=pkr(bank, 0, N))
                            op("act", lambda e: e.activation(out=ybs[s_].t[:, m, 0:N], in_=pb[bank][:, 0:N], func=AF.Copy), reads=pkr(bank, 0, N), writes=[ybs[s_].k])
                    for s_, (ti, (c0, N)) in enumerate(grp):
                        post_res(ybs[s_], N, l, G2, ti, c0)

        ffn(0, list(enumerate(OWN_TILES)))
        tap("hl0", h.t[:, 0, 0:512], [h.tr(0)], [128, 512])
        if stop_after == "l0":
            P.finish()
            return nc, tap_out

        ub = mix
        with scope() as sC1:
            a1 = sbuf(sC1, "a1", [128, 8, NT], BF16)
            wcv_r = Ring([sbuf(sC1, "wcv%d" % i, [128, 8, 128], BF16) for i in range(2)])
            wcg_r = Ring([sbuf(sC1, "wcg%d" % i, [128, 8, 128], BF16) for i in range(2)])
            for ti, (c0, N) in enumerate(OWN_TILES):
                norm_mod([(h.t[:, k, c0:c0 + N], [h.tr(ti)]) for k in range(8)], N,
                         [(a1.t[:, k, c0:c0 + N], [a1.tr(ti)]) for k in range(8)], A1, 1, 0, 0)
            op("pool", lambda e: e.memset(ub.t[:, :, 0:15], 0.0), writes=[ub.tr(k) for k in range(8)])
            for m in range(8):
                wcv = wcv_r.next(); wcg = wcg_r.next()
                load_w(wcv, conf_w_in[:, m * 128:(m + 1) * 128], 8)
                load_w(wcg, conf_w_in[:, D + m * 128:D + (m + 1) * 128], 8)
                for ti, (c0, N) in enumerate(OWN_TILES):
                    bv = mmb2.next(); bg = mmb2.next()
                    for k in range(8):
                        op("pe", lambda e: e.matmul(pb[bv][:, 0:N], lhsT=wcv.t[:, k, :], rhs=a1.t[:, k, c0:c0 + N], start=(k == 0), stop=(k == 7)), reads=[wcv.k, a1.tr(ti)], writes=pkr(bv, 0, N))
                    for k in range(8):
                        op("pe", lambda e: e.matmul(pb[bg][:, 0:N], lhsT=wcg.t[:, k, :], rhs=a1.t[:, k, c0:c0 + N], start=(k == 0), stop=(k == 7)), reads=[wcg.k, a1.tr(ti)], writes=pkr(bg, 0, N))
                    t = tmp_r.next()
                    op("act", lambda e: e.activation(out=t.t[:, 0:N], in_=pb[bg][:, 0:N], func=AF.Sigmoid, bias=vcol(V_CBIN + 8 + m), scale=1.0), reads=pkr(bg, 0, N) + [vec.k], writes=[t.k])
                    op("dve", lambda e: e.scalar_tensor_tensor(out=ub.t[:, m, 15 + c0:15 + c0 + N], in0=pb[bv][:, 0:N], scalar=vcol(V_CBIN + m), in1=t.t[:, 0:N], op0=ALU.add, op1=ALU.mult),
                       reads=pkr(bv, 0, N) + [vec.k, t.k], writes=[ub.tr(m)])
        with scope() as sC2:
            actb = sbuf(sC2, "actb", [128, 8, 512], BF16)
            wco = sbuf(sC2, "wco", [128, 8, D], BF16)
            dgm_r = Ring([sbuf(sC2, "dgm%d" % i, [128, 31, 128], BF16) for i in range(2)])
            lnt = [sbuf(sC2, "lnt%d" % i, [128, 512], F32) for i in range(3)]
            load_w(wco, conf_w_out[:, :], 8)
            for ti, (c0, N) in enumerate(OWN_TILES[:4]):
                cv = yb_r.next()
                for k in range(8):
                    dgm = dgm_r.next()
                    op("dve", lambda e: e.tensor_tensor(out=dgm.t[:, :, :], in0=identb.unsqueeze(1).to_broadcast([128, 31, 128]),
                                                        in1=vec.t[:, V_CDW + k * 31:V_CDW + (k + 1) * 31].unsqueeze(2).to_broadcast([128, 31, 128]), op=ALU.mult),
                       reads=[cb.k, vec.k], writes=[dgm.k])
                    bank = mmb2.next()
                    for jt in range(31):
                        op("pe", lambda e: e.matmul(pb[bank][:, 0:N], lhsT=dgm.t[:, jt, :], rhs=ub.t[:, k, c0 + jt:c0 + jt + N], start=(jt == 0), stop=(jt == 30)),
                           reads=[dgm.k, ub.tr(k)], writes=pkr(bank, 0, N))
                    op("act", lambda e: e.activation(out=cv.t[:, k, 0:N], in_=pb[bank][:, 0:N], func=AF.Identity, bias=vcol(V_CDWB + k), scale=1.0), reads=pkr(bank, 0, N) + [vec.k], writes=[cv.k])
                b1 = stat_banks.next(); b2 = stat_banks.next()
                for k in range(8):
                    op("pe", lambda e: e.matmul(pb[b1][:, 0:N], lhsT=onesf, rhs=cv.t[:, k, 0:N], start=(k == 0), stop=(k == 7)), reads=[cst.k, cv.k], writes=pkr(b1, 0, N))
                for k in range(8):
                    sq = sqr.next()
                    op("act", lambda e: e.activation(out=sq.t[:, 0:N], in_=cv.t[:, k, 0:N], func=AF.Square), reads=[cv.k], writes=[sq.k])
                    op("pe", lambda e: e.matmul(pb[b2][:, 0:N], lhsT=onesb, rhs=sq.t[:, 0:N], start=(k == 0), stop=(k == 7)), reads=[sq.k, cb.k], writes=pkr(b2, 0, N))
                mean, msq, rs = lnt
                op("act", lambda e: e.activation(out=mean.t[:, 0:N], in_=pb[b1][:, 0:N], func=AF.Copy, scale=1.0 / D), reads=pkr(b1, 0, N), writes=[mean.k])
                op("dve", lambda e: e.tensor_tensor(out=msq.t[:, 0:N], in0=mean.t[:, 0:N], in1=mean.t[:, 0:N], op=ALU.mult), reads=[mean.k], writes=[msq.k])
                op("dve", lambda e: e.scalar_tensor_tensor(out=rs.t[:, 0:N], in0=pb[b2][:, 0:N], scalar=1.0 / D, in1=msq.t[:, 0:N], op0=ALU.mult, op1=ALU.subtract),
                   reads=pkr(b2, 0, N) + [msq.k], writes=[rs.k])
                op("act", lambda e: e.activation(out=rs.t[:, 0:N], in_=rs.t[:, 0:N], func=AF.Ln, bias=EPS, scale=1.0), reads=[rs.k], writes=[rs.k])
                op("act", lambda e: e.activation(out=rs.t[:, 0:N], in_=rs.t[:, 0:N], func=AF.Exp, scale=-0.5), reads=[rs.k], writes=[rs.k])
                for k in range(8):
                    t1 = tmp_r.next(); t2 = tmp_r.next()
                    op("dve", lambda e: e.tensor_tensor(out=t1.t[:, 0:N], in0=cv.t[:, k, 0:N], in1=mean.t[:, 0:N], op=ALU.subtract), reads=[cv.k, mean.k], writes=[t1.k])
                    op("dve", lambda e: e.scalar_tensor_tensor(out=t2.t[:, 0:N], in0=t1.t[:, 0:N], scalar=vcol(V_CLNG + k), in1=rs.t[:, 0:N], op0=ALU.mult, op1=ALU.mult), reads=[t1.k, vec.k, rs.k], writes=[t2.k])
                    op("act", lambda e: e.activation(out=actb.t[:, k, 0:N], in_=t2.t[:, 0:N], func=AF.Silu, bias=vcol(V_CLNB + k), scale=1.0), reads=[t2.k, vec.k], writes=[actb.k])
                yb = cv
                for m in range(8):
                    bank = mmb2.next()
                    for k in range(8):
                        op("pe", lambda e: e.matmul(pb[bank][:, 0:N], lhsT=wco.t[:, k, m * 128:(m + 1) * 128], rhs=actb.t[:, k, 0:N], start=(k == 0), stop=(k == 7)), reads=[wco.k, actb.k], writes=pkr(bank, 0, N))
                    op("act", lambda e: e.activation(out=yb.t[:, m, 0:N], in_=pb[bank][:, 0:N], func=AF.Identity, bias=vcol(V_CBOUT + m), scale=1.0), reads=pkr(bank, 0, N) + [vec.k], writes=[yb.k])
                if ti == 0:
                    tap("yconf", yb.t[:, 0, 0:512], [yb.k], [128, 512])
                post_res(yb, N, 1, G1, ti, c0)
        tap("hmix1", h.t[:, 0, 0:512], [h.tr(0)], [128, 512])
        if stop_after == "mix1":
            P.finish()
            return nc, tap_out

        ffn(1, list(enumerate(OWN_TILES[:4])))
        for ti, (c0, N) in enumerate(OWN_TILES[:4]):
            P.dma("sp", outT[:, c0:c0 + N].rearrange("(k p) n -> p k n", p=128), h.t[:, :, c0:c0 + N], reads=[h.tr(ti)])
        P.finish()
    return nc, tap_out


def _consts():
    c = np.zeros((128, NCONST), np.float32)
    i = np.arange(128)
    c[:, C_ID:C_ID + 128] = np.eye(128, dtype=np.float32)
    c[:, C_ONES:C_ONES + 128] = 1.0
    c[:, C_TRIA:C_TRIA + 128] = (i[:, None] <= i[None, :])
    c[:, C_TRID:C_TRID + 128] = (i[:, None] >= i[None, :])
    c[:, C_NSA:C_NSA + 128] = np.where(i[None, :] > i[:, None], 0.0, NEG)
    c[:, C_NSD:C_NSD + 128] = np.where(i[None, :] < i[:, None], 0.0, NEG)
    c[:, C_BLK:C_BLK + 128] = (i[:, None] // 64 == i[None, :] // 64)
    rot = np.zeros((128, 128), np.float32)
    for hd in range(2):
        for ax in range(2):
            for f in range(16):
                a0 = hd * 64 + ax * 32 + f
                a1 = a0 + 16
                rot[a1, a0] = -1.0
                rot[a0, a1] = 1.0
    c[:, C_ROT:C_ROT + 128] = rot
    return c


def _rope_tables():
    t = np.arange(TL)
    row = (t // 64).astype(np.float32)
    col = (t % 64).astype(np.float32)
    inv = (np.float32(10000.0) ** (-np.arange(16, dtype=np.float32) / np.float32(16))).astype(np.float32)
    ar = row[:, None] * inv
    ac = col[:, None] * inv
    ang = np.concatenate([ar, ar, ac, ac], axis=-1).astype(np.float32)
    return np.cos(ang).astype(np.float32), np.sin(ang).astype(np.float32)


def _fm(v):
    return np.ascontiguousarray(v.reshape(-1, 128).T)


def prep_core(inp, core, shared):
    b, half = core // 2, core % 2
    rev = half == 1
    f = lambda a: np.ascontiguousarray(a, dtype=np.float32)
    x_b = inp["x"][b]
    ctx_b = inp["ctx"][b]
    cos, sin = shared["rope"]
    if rev:
        x_b = x_b[::-1]; ctx_b = ctx_b[::-1]; cos = cos[::-1]; sin = sin[::-1]
    m = {}
    m["xT"] = f(x_b.T)
    m["ctxT"] = f(ctx_b.T)
    cinv = np.zeros((128, 8, 2), np.float32)
    cinv[:, :, 0] = _fm(inp["c"][b]); cinv[:, :, 1] = _fm(inp["c_ctx"])
    m["cin"] = cinv.reshape(128, 16)
    v = np.zeros((128, NV), np.float32)
    for l in range(2):
        o = l * V_L
        v[:, o:o + 8] = _fm(inp["g_mix_pre"][l]); v[:, o + 8:o + 16] = _fm(inp["g_mix_post"][l])
        v[:, o + 16:o + 24] = _fm(inp["g_ffn_pre"][l]); v[:, o + 24:o + 32] = _fm(inp["g_ffn_post"][l])
        v[:, o + 32:o + 80] = _fm(inp["b_mod"][l])
    v[:, V_CBIN:V_CBIN + 16] = _fm(inp["conf_b_in"][0]); v[:, V_CDWB:V_CDWB + 8] = _fm(inp["conf_dw_b"][0])
    v[:, V_CLNG:V_CLNG + 8] = _fm(inp["conf_ln_g"][0]); v[:, V_CLNB:V_CLNB + 8] = _fm(inp["conf_ln_b"][0])
    v[:, V_CBOUT:V_CBOUT + 8] = _fm(inp["conf_b_out"][0])
    dw = inp["conf_dw_w"][0]; hc = inp["hyb_conv_w"][0]
    if rev:
        dw = dw[::-1]; hc = hc[::-1]
    for j in range(31):
        fm = _fm(dw[j])
        for k in range(8):
            v[:, V_CDW + k * 31 + j] = fm[:, k]
    for j in range(5):
        v[:, V_HCONV + j * 12:V_HCONV + j * 12 + 12] = _fm(hc[j])
    v[:, V_ONORM] = inp["hyb_out_norm"][0]
    v[:, V_QN] = np.tile(inp["hyb_q_norm"][0], 2); v[:, V_KN] = np.tile(inp["hyb_k_norm"][0], 2)
    m["vecs"] = v
    al = inp["hyb_a_log"][0]; dtb = inp["hyb_dt_bias"][0]
    if rev:
        al = al[::-1]; dtb = dtb[::-1]
    m["abc"] = f(np.tile(np.concatenate([al.reshape(-1), dtb.reshape(-1)])[None, :], (128, 1)))
    m["consts"] = shared["consts"]
    tab = np.zeros((2, 128, TL), np.float32)
    tab[0, 0:64] = cos.T; tab[0, 64:128] = cos.T; tab[1, 0:64] = sin.T; tab[1, 64:128] = sin.T
    m["rope"] = tab
    m["w_mod"] = shared["w_mod"]
    m["hyb_w_in"] = shared["hyb_w_in_rev"] if rev else shared["hyb_w_in"]
    m["hyb_w_out"] = shared["hyb_w_out"]
    m["w_ffn_in"] = shared["w_ffn_in"]; m["w_ffn_out"] = shared["w_ffn_out"]
    m["conf_w_in"] = shared["conf_w_in"]; m["conf_w_out"] = shared["conf_w_out"]
    return m


def prep_shared(inp):
    f = lambda a: np.ascontiguousarray(a, dtype=np.float32)
    sh = {"consts": _consts(), "rope": _rope_tables()}
    sh["w_mod"] = f(inp["w_mod"]); sh["w_ffn_in"] = f(inp["w_ffn_in"]); sh["w_ffn_out"] = f(inp["w_ffn_out"])
    sh["conf_w_in"] = f(inp["conf_w_in"][0]); sh["conf_w_out"] = f(inp["conf_w_out"][0])
    w = inp["hyb_w_in"][0]
    qcols = []
    for c in range(4):
        qcols += list(range(2064 + c * 64, 2064 + c * 64 + 64)) + list(range(2064 + (4 + c) * 64, 2064 + (4 + c) * 64 + 64))
    tail = list(range(2576, 2832))
    ba = list(range(2048, 2064))
    ba_rev = [2048 + kind * 8 + (1 - d) * 4 + h for kind in range(2) for d in range(2) for h in range(4)]
    base = list(range(2048))
    sh["hyb_w_in"] = f(w[:, base + ba + qcols + tail])
    sh["hyb_w_in_rev"] = f(w[:, base + ba_rev + qcols + tail])
    wo = inp["hyb_w_out"][0]
    rows = list(range(512))
    for c in range(4):
        rows += list(range(512 + c * 64, 512 + c * 64 + 64)) + list(range(512 + (4 + c) * 64, 512 + (4 + c) * 64 + 64))
    sh["hyb_w_out"] = f(wo[rows, :])
    return sh


def kernel(**inputs):
    inp = {k: np.asarray(v) for k, v in inputs.items()}
    sh = prep_shared(inp)
    nc, _ = build()
    in_maps = [prep_core(inp, c, sh) for c in range(8)]
    res = run_bass_kernel_spmd(nc, in_maps, core_ids=list(range(8)))
    out = np.zeros((4, TL, D), np.float32)
    for c in range(8):
        b, half = c // 2, c % 2
        o = res.results[c]["outT"].T
        if half == 0:
            out[b, 0:NOWN] = o
        else:
            out[b, TL - 1 - np.arange(NOWN)] = o
    return out
```

```python
import numpy as np
from contextlib import ExitStack, contextmanager
from collections import deque
import concourse.bass as bass
import concourse.mybir as mybir
from concourse.bass_utils import run_bass_kernel_spmd

F32 = mybir.dt.float32
BF16 = mybir.dt.bfloat16
AF = mybir.ActivationFunctionType
ALU = mybir.AluOpType

NDMA = 12
D = 1024
KC = 8
TC = 256
TL = 4096
TF = TC + TL
NOWN = 2048
NT = 2176
NTP = NT + 16
FF = 2816
FFC = 22
EPS = 1e-6
CH = 128
NCH = TF // CH
NEG = -30000.0
import os
UNIT_CUT = int(os.environ.get('UNIT_CUT', '99'))

C_ID, C_ONES, C_TRIA, C_TRID, C_NSA, C_NSD, C_BLK, C_ROT, NCONST = 0, 128, 256, 384, 512, 640, 768, 896, 1024
V_L = 80
V_CBIN, V_CDWB, V_CLNG, V_CLNB, V_CBOUT = 160, 176, 184, 192, 200
V_CDW = 208
V_HCONV = V_CDW + 248
V_ONORM = V_HCONV + 60
V_QN = V_ONORM + 1
V_KN = V_QN + 1
NV = V_KN + 1

OWN_TILES = [(0, 512), (512, 512), (1024, 512), (1536, 512), (2048, 128)]
FULL_TILES = [(0, 256)] + [(256 + i * 512, 512) for i in range(8)]


class Trk:
    __slots__ = ("w", "r", "psum")

    def __init__(self, psum=False):
        self.w = None
        self.r = []
        self.psum = psum


class Buf:
    def __init__(self, t):
        self.t = t
        self.k = Trk()
        self._ks = {}

    def tr(self, key):
        if key not in self._ks:
            self._ks[key] = Trk()
        return self._ks[key]


class Prog:
    def __init__(self, nc, stack):
        self.nc = nc
        self.eng = {}
        for k, h in (("pe", nc.tensor), ("act", nc.scalar), ("dve", nc.vector),
                     ("pool", nc.gpsimd), ("sp", nc.sync)):
            sem = stack.enter_context(nc.semaphore("s_" + k))
            self.eng[k] = dict(k=k, h=h, sem=sem, cnt=0, seen={}, dslots=None, dnext=0)
        for q in ("sp", "pool"):
            self.eng[q]["dslots"] = [[stack.enter_context(nc.semaphore("d%s%d" % (q, i))), 0]
                                     for i in range(NDMA)]
        self.n_ops = 0

    def _wait(self, e, ev):
        sem, val = ev
        key = id(sem)
        if e["seen"].get(key, 0) >= val:
            return
        if e["k"] == "pe" and sem is e["sem"]:
            return
        e["h"].wait_ge(sem, val)
        e["seen"][key] = val

    def _deps(self, e, reads, writes):
        for t in reads:
            if t.w is not None:
                self._wait(e, t.w)
        for t in writes:
            if t.w is not None:
                self._wait(e, t.w)
            for r in t.r:
                self._wait(e, r)

    def _commit(self, ev, reads, writes):
        for t in reads:
            t.r.append(ev)
            if len(t.r) > 16:
                best = {}
                for s, v in t.r:
                    if id(s) not in best or best[id(s)][1] < v:
                        best[id(s)] = (s, v)
                t.r = list(best.values())
        for t in writes:
            t.w = ev
            t.r = []

    def op(self, ek, fn, reads=(), writes=()):
        e = self.eng[ek]
        pr = [t for t in reads if getattr(t, "psum", False)]
        if pr:
            reads = [t for t in reads if not getattr(t, "psum", False)]
            writes = list(writes) + pr
        self._deps(e, reads, writes)
        ins = fn(e["h"])
        e["cnt"] += 1
        ins.then_inc(e["sem"], 1)
        self._commit((e["sem"], e["cnt"]), reads, writes)
        self.n_ops += 1

    def dma(self, qk, out, in_, reads=(), writes=()):
        e = self.eng[qk]
        slot = e["dslots"][e["dnext"] % NDMA]
        e["dnext"] += 1
        if slot[1] > 0:
            self._wait(e, (slot[0], slot[1]))
        self._deps(e, reads, writes)
        e["h"].dma_start(out=out, in_=in_).then_inc(slot[0], 16)
        slot[1] += 16
        self._commit((slot[0], slot[1]), reads, writes)
        self.n_ops += 1

    def barrier(self):
        evs = []
        for o in self.eng.values():
            if o["cnt"] > 0:
                evs.append((o["sem"], o["cnt"]))
            if o["dslots"]:
                for sm, v in o["dslots"]:
                    if v > 0:
                        evs.append((sm, v))
        for e in self.eng.values():
            for ev in evs:
                if ev[0] is e["sem"]:
                    continue
                self._wait(e, ev)

    def finish(self):
        e = self.eng["sp"]
        for o in self.eng.values():
            if o["dslots"]:
                for s, v in o["dslots"]:
                    if v > 0:
                        self._wait(e, (s, v))
            if o["cnt"] > 0 and o is not e:
                self._wait(e, (o["sem"], o["cnt"]))


class Ring:
    def __init__(self, bufs):
        self.b = bufs
        self.i = 0

    def next(self):
        b = self.b[self.i % len(self.b)]
        self.i += 1
        return b


def build(taps=(), stop_after=None):
    nc = bass.Bass("TRN2", target_bir_lowering=False)
    dr = {}

    def din(name, shape):
        dr[name] = nc.dram_tensor(name, list(shape), F32, kind="ExternalInput").ap()
        return dr[name]

    xT = din("xT", [D, TL]); ctxT = din("ctxT", [D, TC]); cin = din("cin", [128, 16])
    vecs_d = din("vecs", [128, NV]); abc_d = din("abc", [128, 16]); cst_d = din("consts", [128, NCONST])
    rope_d = din("rope", [2, 128, TL])
    w_mod = din("w_mod", [2, D, 6 * D]); hyb_w_in = din("hyb_w_in", [D, 2832]); hyb_w_out = din("hyb_w_out", [D, D])
    w_ffn_in = din("w_ffn_in", [2, D, 2 * FF]); w_ffn_out = din("w_ffn_out", [2, FF, D])
    conf_w_in = din("conf_w_in", [D, 2 * D]); conf_w_out = din("conf_w_out", [D, D])
    outT = nc.dram_tensor("outT", [D, NOWN], F32, kind="ExternalOutput").ap()
    tap_out = {}

    with ExitStack() as st:
        P = Prog(nc, st)
        op = P.op

        @contextmanager
        def scope():
            with ExitStack() as s_:
                yield s_
                P.barrier()

        nctr = [0]

        def sbuf(stack, name, shape, dt):
            nctr[0] += 1
            return Buf(stack.enter_context(nc.sbuf_tensor("s%d_%s" % (nctr[0], name), list(shape), dt)))

        pb = [st.enter_context(nc.psum_tensor("pb%d" % i, [128, 512], F32)) for i in range(8)]
        pk = [[Trk(psum=True) for _ in range(4)] for _ in range(8)]

        def pkr(i, c0, n):
            return [pk[i][0]]

        def tap(name, ap, trks, shape):
            if name not in taps:
                return
            t = nc.dram_tensor("tap_" + name, list(shape), F32, kind="ExternalOutput").ap()
            tap_out[name] = t
            P.dma("pool", t, ap, reads=trks)

        cst = sbuf(st, "cst", [128, NCONST], F32)
        vec = sbuf(st, "vec", [128, NV], F32)
        abc = sbuf(st, "abc", [128, 16], F32)
        cin_s = sbuf(st, "cin_s", [128, 16], F32)
        P.dma("sp", cst.t[:], cst_d[:, :], writes=[cst.k])
        P.dma("sp", vec.t[:], vecs_d[:, :], writes=[vec.k])
        P.dma("sp", abc.t[:], abc_d[:, :], writes=[abc.k])
        P.dma("sp", cin_s.t[:], cin[:, :], writes=[cin_s.k])
        cb = sbuf(st, "cb", [128, 3, 128], BF16)
        op("dve", lambda e: e.tensor_copy(out=cb.t[:, 0, :], in_=cst.t[:, C_ID:C_ID + 128]), reads=[cst.k], writes=[cb.k])
        op("dve", lambda e: e.tensor_copy(out=cb.t[:, 1, :], in_=cst.t[:, C_ONES:C_ONES + 128]), reads=[cst.k], writes=[cb.k])
        op("dve", lambda e: e.tensor_copy(out=cb.t[:, 2, :], in_=cst.t[:, C_BLK:C_BLK + 128]), reads=[cst.k], writes=[cb.k])
        identb = cb.t[:, 0, :]; onesb = cb.t[:, 1, :]; blkb = cb.t[:, 2, :]
        identf = cst.t[:, C_ID:C_ID + 128]; onesf = cst.t[:, C_ONES:C_ONES + 128]

        modv = sbuf(st, "modv", [128, 2, 48, 2], F32)
        A1 = sbuf(st, "A1", [128, 2, 8, 2], F32)
        A2 = sbuf(st, "A2", [128, 2, 8, 2], F32)
        G1 = sbuf(st, "G1", [128, 2, 8], F32)
        G2 = sbuf(st, "G2", [128, 2, 8], F32)
        sqr = Ring([sbuf(st, "sq%d" % i, [128, 512], BF16) for i in range(2)])
        rstd_r = Ring([sbuf(st, "rstd%d" % i, [128, 512], F32) for i in range(2)])
        tmp_r = Ring([sbuf(st, "tmp%d" % i, [128, 512], F32) for i in range(3)])
        stat_banks = Ring([6, 7])

        def vcol(c):
            return vec.t[:, c:c + 1]

        def load_w(stack_buf, src2d, kc, eng="pool"):
            P.dma(eng, stack_buf.t[:, 0:kc, 0:src2d.shape[1]], src2d.rearrange("(k p) c -> p k c", p=128), writes=[stack_buf.k])

        def rms_rstd(srcs, N, dsz, ones_ap, eps=EPS, f32mm=False):
            bank = stat_banks.next()
            for i, (ap, tk) in enumerate(srcs):
                sq = sqr.next()
                op("act", lambda e, sq=sq, ap=ap: e.activation(out=sq.t[:, 0:N], in_=ap, func=AF.Square), reads=tk, writes=[sq.k])
                op("pe", lambda e, sq=sq, i=i: e.matmul(pb[bank][:, 0:N], lhsT=ones_ap, rhs=sq.t[:, 0:N], start=(i == 0), stop=(i == len(srcs) - 1)),
                   reads=[sq.k, cb.k], writes=pkr(bank, 0, N))
            r = rstd_r.next()
            op("act", lambda e: e.activation(out=r.t[:, 0:N], in_=pb[bank][:, 0:N], func=AF.Ln, bias=eps, scale=1.0 / dsz), reads=pkr(bank, 0, N), writes=[r.k])
            op("act", lambda e: e.activation(out=r.t[:, 0:N], in_=r.t[:, 0:N], func=AF.Exp, scale=-0.5), reads=[r.k], writes=[r.k])
            return r

        with scope() as sa:
            scb = sbuf(sa, "scb", [128, 8, 2], BF16)
            op("act", lambda e: e.activation(out=scb.t[:].rearrange("p k j -> p (k j)"), in_=cin_s.t[:, :], func=AF.Silu), reads=[cin_s.k], writes=[scb.k])
            wmr = Ring([sbuf(sa, "wm%d" % i, [128, 8, 1536], BF16) for i in range(2)])
            for l in range(2):
                for blk in range(4):
                    wm = wmr.next()
                    load_w(wm, w_mod[l, :, blk * 1536:(blk + 1) * 1536], 8)
                    bank = blk % 2
                    for m in range(12):
                        for k in range(8):
                            op("pe", lambda e, wm=wm, m=m, k=k: e.matmul(pb[bank][:, 2 * m:2 * m + 2], lhsT=wm.t[:, k, m * 128:(m + 1) * 128], rhs=scb.t[:, k, :],
                                                                             start=(k == 0), stop=(k == 7)), reads=[wm.k, scb.k], writes=pkr(bank, 0, 24))
                    bcol = l * V_L + 32 + blk * 12
                    op("dve", lambda e, l=l, blk=blk, bcol=bcol: e.tensor_tensor(
                        out=modv.t[:, l, blk * 12:(blk + 1) * 12, :], in0=pb[bank][:, 0:24].rearrange("p (m j) -> p m j", j=2),
                        in1=vec.t[:, bcol:bcol + 12].unsqueeze(2).to_broadcast([128, 12, 2]), op=ALU.add),
                        reads=pkr(bank, 0, 24) + [vec.k], writes=[modv.k])
            for l in range(2):
                for (Ab, sc0, gcol) in ((A1, 8, l * V_L + 0), (A2, 32, l * V_L + 16)):
                    op("dve", lambda e, Ab=Ab, sc0=sc0, gcol=gcol, l=l: e.scalar_tensor_tensor(
                        out=Ab.t[:, l, :, :], in0=modv.t[:, l, sc0:sc0 + 8, :], scalar=1.0,
                        in1=vec.t[:, gcol:gcol + 8].unsqueeze(2).to_broadcast([128, 8, 2]), op0=ALU.add, op1=ALU.mult),
                        reads=[modv.k, vec.k], writes=[Ab.k])
                for (Gb, g0, gcol) in ((G1, 16, l * V_L + 8), (G2, 40, l * V_L + 24)):
                    op("dve", lambda e, Gb=Gb, g0=g0, gcol=gcol, l=l: e.tensor_tensor(
                        out=Gb.t[:, l, :], in0=modv.t[:, l, g0:g0 + 8, 0], in1=vec.t[:, gcol:gcol + 8], op=ALU.mult),
                        reads=[modv.k, vec.k], writes=[Gb.k])
            tap("mod", modv.t[:].rearrange("p l m j -> p (l m j)"), [modv.k], [128, 192])

        def norm_mod(srcs, N, outs, Ab, l, j, shift0):
            r = rms_rstd(srcs, N, D, onesb)
            for k in range(8):
                t = tmp_r.next()
                op("dve", lambda e, k=k, t=t: e.scalar_tensor_tensor(out=t.t[:, 0:N], in0=srcs[k][0], scalar=Ab.t[:, l, k, j:j + 1], in1=r.t[:, 0:N],
                                                                     op0=ALU.mult, op1=ALU.mult), reads=srcs[k][1] + [Ab.k, r.k], writes=[t.k])
                op("act", lambda e, k=k, t=t: e.activation(out=outs[k][0], in_=t.t[:, 0:N], func=AF.Identity, bias=modv.t[:, l, shift0 + k, j:j + 1], scale=1.0),
                   reads=[t.k, modv.k], writes=outs[k][1])

        mix = sbuf(st, "mix", [128, 8, NTP], BF16)

        with scope() as s0:
            aT = sbuf(s0, "aT", [128, 8, TF], BF16)
            with scope() as sb_:
                xs_r = Ring([sbuf(sb_, "xs%d" % i, [128, 8, 512], F32) for i in range(2)])
                for ti, (c0, N) in enumerate(FULL_TILES):
                    xs = xs_r.next()
                    src = ctxT[:, 0:TC] if ti == 0 else xT[:, c0 - TC:c0 - TC + N]
                    P.dma("sp", xs.t[:, :, 0:N], src.rearrange("(k p) n -> p k n", p=128), writes=[xs.k])
                    norm_mod([(xs.t[:, k, 0:N], [xs.k]) for k in range(8)], N,
                             [(aT.t[:, k, c0:c0 + N], [aT.tr(ti)]) for k in range(8)], A1, 0, 1 if ti == 0 else 0, 0)
            tap("aT", aT.t[:, :, 0:768], [aT.tr(0), aT.tr(1)], [128, 8, 768])
            if stop_after == "aT":
                P.finish()
                return nc, tap_out

            def aT_rhs(k, c0, N):
                trs = [aT.tr(ti) for ti, (t0, tn) in enumerate(FULL_TILES) if t0 < c0 + N and c0 < t0 + tn]
                return aT.t[:, k, c0:c0 + N], trs

            with scope() as sB:
                qbT = sbuf(sB, "qbT", [128, 4, NT], BF16)
                kbT = sbuf(sB, "kbT", [128, TF], BF16)
                vaug = sbuf(sB, "vaug", [128, NCH, 2, 128], BF16)
                wq = sbuf(sB, "wq", [128, 8, 512], BF16)
                wkv = sbuf(sB, "wkv", [128, 8, 256], BF16)
                rope_r = Ring([sbuf(sB, "rope%d" % i, [128, 2, 512], F32) for i in range(2)])
                kn_r = Ring([sbuf(sB, "kn%d" % i, [128, 512], F32) for i in range(2)])
                pt_r = Ring([sbuf(sB, "pt%d" % i, [128, 512], BF16) for i in range(4)])
                rs_r = Ring([sbuf(sB, "rs%d" % i, [128, 512], F32) for i in range(2)])
                load_w(wq, hyb_w_in[:, 2064:2576], 8)
                load_w(wkv, hyb_w_in[:, 2576:2832], 8)
                op("pool", lambda e: e.memset(vaug.t[:, :, 0, 64:128], 1.0), writes=[vaug.k])
                op("pool", lambda e: e.memset(vaug.t[:, :, 1, 0:64], 1.0), writes=[vaug.k])
                mmb = Ring([0, 1, 2, 3])

                def qk_tile(wbuf, wc0, c0, N, gcol, use_rope, rope_c0, dst_ap, dst_trk):
                    bank = mmb.next()
                    for k in range(8):
                        rhs, trs = aT_rhs(k, c0, N)
                        op("pe", lambda e, k=k, rhs=rhs: e.matmul(pb[bank][:, 0:N], lhsT=wbuf.t[:, k, wc0:wc0 + 128], rhs=rhs, start=(k == 0), stop=(k == 7)),
                           reads=[wbuf.k] + trs, writes=pkr(bank, 0, N))
                    r = rms_rstd([(pb[bank][:, 0:N], pkr(bank, 0, N))], N, 64, blkb)
                    kn = kn_r.next()
                    if not use_rope:
                        op("dve", lambda e: e.scalar_tensor_tensor(out=dst_ap, in0=pb[bank][:, 0:N], scalar=vcol(gcol), in1=r.t[:, 0:N], op0=ALU.mult, op1=ALU.mult),
                           reads=pkr(bank, 0, N) + [vec.k, r.k], writes=dst_trk)
                        return
                    op("dve", lambda e: e.scalar_tensor_tensor(out=kn.t[:, 0:N], in0=pb[bank][:, 0:N], scalar=vcol(gcol), in1=r.t[:, 0:N], op0=ALU.mult, op1=ALU.mult),
                       reads=pkr(bank, 0, N) + [vec.k, r.k], writes=[kn.k])
                    rp = rope_r.next()
                    P.dma("sp", rp.t[:, :, 0:N], rope_d[:, :, rope_c0:rope_c0 + N].rearrange("c p n -> p c n"), writes=[rp.k])
                    b2 = mmb.next()
                    op("pe", lambda e: e.matmul(pb[b2][:, 0:N], lhsT=cst.t[:, C_ROT:C_ROT + 128], rhs=kn.t[:, 0:N], start=True, stop=True),
                       reads=[cst.k, kn.k], writes=pkr(b2, 0, N))
                    t1 = tmp_r.next(); t2 = tmp_r.next()
                    op("dve", lambda e: e.tensor_tensor(out=t1.t[:, 0:N], in0=kn.t[:, 0:N], in1=rp.t[:, 0, 0:N], op=ALU.mult), reads=[kn.k, rp.k], writes=[t1.k])
                    op("dve", lambda e: e.tensor_tensor(out=t2.t[:, 0:N], in0=pb[b2][:, 0:N], in1=rp.t[:, 1, 0:N], op=ALU.mult), reads=pkr(b2, 0, N) + [rp.k], writes=[t2.k])
                    op("dve", lambda e: e.tensor_tensor(out=dst_ap, in0=t1.t[:, 0:N], in1=t2.t[:, 0:N], op=ALU.add), reads=[t1.k, t2.k], writes=dst_trk)

                for ti, (c0, N) in enumerate(FULL_TILES):
                    qk_tile(wkv, 0, c0, N, V_KN, ti > 0, c0 - TC, kbT.t[:, c0:c0 + N], [kbT.k])
                for n in range(NCH):
                    bank = mmb.next()
                    for k in range(8):
                        lhs, trs = aT_rhs(k, n * 128, 128)
                        op("pe", lambda e, k=k, lhs=lhs: e.matmul(pb[bank][:, 0:128], lhsT=lhs, rhs=wkv.t[:, k, 128:256], start=(k == 0), stop=(k == 7)),
                           reads=[wkv.k] + trs, writes=pkr(bank, 0, 128))
                    op("act", lambda e, n=n: e.activation(out=vaug.t[:, n, 0, 0:64], in_=pb[bank][:, 0:64], func=AF.Copy), reads=pkr(bank, 0, 128), writes=[vaug.k])
                    op("act", lambda e, n=n: e.activation(out=vaug.t[:, n, 1, 64:128], in_=pb[bank][:, 64:128], func=AF.Copy), reads=pkr(bank, 0, 128), writes=[vaug.k])
                for c in range(4):
                    for (c0, N) in OWN_TILES:
                        qk_tile(wq, c * 128, TC + c0, N, V_QN, True, c0, qbT.t[:, c, c0:c0 + N], [qbT.tr(c)])
                tap("kbT", kbT.t[:, 0:1024], [kbT.k], [128, 1024])
                tap("qbT", qbT.t[:, 0, 0:512], [qbT.tr(0)], [128, 512])
                sbank = Ring([0, 1, 2, 3, 6, 7])
                obank = Ring([4, 5])
                LOOK = 2
                iters = [(c, c0, N, n) for c in range(4) for (c0, N) in OWN_TILES for n in range(NCH)]
                sb_of = {}
                obs = [4, 5]

                def issue_qk(i):
                    c, c0, N, n = iters[i]
                    bl = []
                    for hh in range(2):
                        sbk = sbank.next()
                        bl.append(sbk)
                        op("pe", lambda e: e.matmul(pb[sbk][:, 0:N], lhsT=kbT.t[64 * hh:64 * hh + 64, n * 128:(n + 1) * 128],
                                                    rhs=qbT.t[64 * hh:64 * hh + 64, c, c0:c0 + N], start=True, stop=True),
                           reads=[kbT.k, qbT.tr(c)], writes=pkr(sbk, 0, N))
                    sb_of[i] = bl

                for i in range(min(LOOK, len(iters))):
                    issue_qk(i)
                for i, (c, c0, N, n) in enumerate(iters):
                    bl = sb_of.pop(i)
                    pts = []
                    for sbk in bl:
                        pt = pt_r.next()
                        pts.append(pt)
                        op("act", lambda e: e.activation(out=pt.t[:, 0:N], in_=pb[sbk][:, 0:N], func=AF.Exp, scale=0.125), reads=pkr(sbk, 0, N), writes=[pt.k])
                    if i + LOOK < len(iters):
                        issue_qk(i + LOOK)
                    for hh, pt in enumerate(pts):
                        ob = obs[hh]
                        op("pe", lambda e: e.matmul(pb[ob][:, 0:N], lhsT=vaug.t[:, n, hh, :], rhs=pt.t[:, 0:N], start=(n == 0), stop=(n == NCH - 1)),
                           reads=[vaug.k, pt.k], writes=pkr(ob, 0, N))
                    if n == NCH - 1:
                        for hh in range(2):
                            ob = obs[hh]
                            rs = rs_r.next()
                            lo, so = (0, 64) if hh == 0 else (64, 0)
                            op("dve", lambda e: e.reciprocal(out=rs.t[so:so + 64, 0:N], in_=pb[ob][so:so + 64, 0:N]), reads=pkr(ob, 0, N), writes=[rs.k])
                            op("dve", lambda e: e.tensor_tensor(out=mix.t[lo:lo + 64, 4 + c, c0:c0 + N], in0=pb[ob][lo:lo + 64, 0:N], in1=rs.t[so:so + 64, 0:N], op=ALU.mult),
                               reads=pkr(ob, 0, N) + [rs.k], writes=[mix.tr(4 + c)])
                tap("ob", mix.t[:, 4, 0:512], [mix.tr(4)], [128, 512])
            if stop_after == "attn":
                for k in range(4, 8):
                    P.dma("pool", outT[k * 128:(k + 1) * 128, :], mix.t[:, k, 0:NOWN], reads=[mix.tr(k)])
                P.finish()
                return nc, tap_out

            with scope() as sA:
                QK = sbuf(sA, "QK", [128, 2, TF], BF16)
                vT = sbuf(sA, "vT", [128, TF], BF16)
                PRE = TF + 8
                pre = sbuf(sA, "pre", [128, PRE], BF16)
                oT = sbuf(sA, "oT", [128, NT], F32)
                wh = Ring([sbuf(sA, "wh%d" % i, [128, 8, 128], BF16) for i in range(2)])
                wba = sbuf(sA, "wba", [128, 8, 16], BF16)
                acc_r = Ring([sbuf(sA, "acc%d" % i, [128, 512], F32) for i in range(1)])
                tb = {nm: sbuf(sA, "tb_" + nm, [128, 16 if nm == "ba" else 8, NCH], F32)
                      for nm in ("ba", "beta", "g", "cum", "gtot", "bec", "etail")}
                nexpA = sbuf(sA, "nexpA", [128, 8], F32)
                op("pool", lambda e: e.memset(pre.t[:, :], 0.0), writes=[pre.k])
                load_w(wba, hyb_w_in[:, 2048:2064], 8)
                for n in range(NCH):
                    bank = n % 4
                    for k in range(8):
                        lhs, trs = aT_rhs(k, n * 128, 128)
                        op("pe", lambda e: e.matmul(pb[bank][:, 0:16], lhsT=lhs, rhs=wba.t[:, k, :], start=(k == 0), stop=(k == 7)),
                           reads=[wba.k] + trs, writes=pkr(bank, 0, 16))
                    op("act", lambda e: e.activation(out=tb["ba"].t[:, :, n], in_=pb[bank][:, 0:16], func=AF.Copy), reads=pkr(bank, 0, 16), writes=[tb["ba"].k])
                op("act", lambda e: e.activation(out=tb["beta"].t[:, :, :], in_=tb["ba"].t[:, 0:8, :], func=AF.Sigmoid), reads=[tb["ba"].k], writes=[tb["beta"].k])
                op("dve", lambda e: e.tensor_tensor(out=tb["g"].t[:, :, :], in0=tb["ba"].t[:, 8:16, :], in1=abc.t[:, 8:16].unsqueeze(2).to_broadcast([128, 8, NCH]), op=ALU.add),
                   reads=[tb["ba"].k, abc.k], writes=[tb["g"].k])
                op("act", lambda e: e.activation(out=tb["g"].t[:, :, :], in_=tb["g"].t[:, :, :], func=AF.Exp), reads=[tb["g"].k], writes=[tb["g"].k])
                op("act", lambda e: e.activation(out=tb["g"].t[:, :, :], in_=tb["g"].t[:, :, :], func=AF.Ln, bias=1.0), reads=[tb["g"].k], writes=[tb["g"].k])
                op("act", lambda e: e.activation(out=nexpA.t[:, :], in_=abc.t[:, 0:8], func=AF.Exp), reads=[abc.k], writes=[nexpA.k])
                op("dve", lambda e: e.tensor_scalar(out=nexpA.t[:, :], in0=nexpA.t[:, :], scalar1=-1.0, scalar2=None, op0=ALU.mult), reads=[nexpA.k], writes=[nexpA.k])
                op("dve", lambda e: e.tensor_tensor(out=tb["g"].t[:, :, :], in0=tb["g"].t[:, :, :], in1=nexpA.t[:, :].unsqueeze(2).to_broadcast([128, 8, NCH]), op=ALU.mult),
                   reads=[tb["g"].k, nexpA.k], writes=[tb["g"].k])
                gflat = tb["g"].t[:].rearrange("p j n -> p (j n)")
                NJ = 4 * NCH
                op("pe", lambda e: e.matmul(pb[0][:, 0:NJ], lhsT=cst.t[:, C_TRIA:C_TRIA + 128], rhs=gflat[:, 0:NJ], start=True, stop=True), reads=[cst.k, tb["g"].k], writes=pkr(0, 0, NJ))
                op("pe", lambda e: e.matmul(pb[1][:, 0:NJ], lhsT=cst.t[:, C_TRID:C_TRID + 128], rhs=gflat[:, NJ:2 * NJ], start=True, stop=True), reads=[cst.k, tb["g"].k], writes=pkr(1, 0, NJ))
                op("pe", lambda e: e.matmul(pb[2][:, 0:2 * NJ], lhsT=onesf, rhs=gflat[:, 0:2 * NJ], start=True, stop=True), reads=[cst.k, tb["g"].k], writes=pkr(2, 0, 2 * NJ))
                cflat = tb["cum"].t[:].rearrange("p j n -> p (j n)")
                op("act", lambda e: e.activation(out=cflat[:, 0:NJ], in_=pb[0][:, 0:NJ], func=AF.Copy), reads=pkr(0, 0, NJ), writes=[tb["cum"].k])
                op("act", lambda e: e.activation(out=cflat[:, NJ:2 * NJ], in_=pb[1][:, 0:NJ], func=AF.Copy), reads=pkr(1, 0, NJ), writes=[tb["cum"].k])
                op("act", lambda e: e.activation(out=tb["gtot"].t[:].rearrange("p j n -> p (j n)"), in_=pb[2][:, 0:2 * NJ], func=AF.Copy), reads=pkr(2, 0, 2 * NJ), writes=[tb["gtot"].k])
                op("dve", lambda e: e.tensor_tensor(out=tb["etail"].t[:, :, :], in0=tb["gtot"].t[:, :, :], in1=tb["cum"].t[:, :, :], op=ALU.subtract), reads=[tb["gtot"].k, tb["cum"].k], writes=[tb["etail"].k])
                op("act", lambda e: e.activation(out=tb["etail"].t[:, :, :], in_=tb["etail"].t[:, :, :], func=AF.Exp), reads=[tb["etail"].k], writes=[tb["etail"].k])
                op("act", lambda e: e.activation(out=tb["gtot"].t[:, :, :], in_=tb["gtot"].t[:, :, :], func=AF.Exp), reads=[tb["gtot"].k], writes=[tb["gtot"].k])
                tb["egl"] = tb["gtot"]
                op("act", lambda e: e.activation(out=tb["bec"].t[:, :, :], in_=tb["cum"].t[:, :, :], func=AF.Exp), reads=[tb["cum"].k], writes=[tb["bec"].k])
                op("dve", lambda e: e.tensor_tensor(out=tb["bec"].t[:, :, :], in0=tb["bec"].t[:, :, :], in1=tb["beta"].t[:, :, :], op=ALU.mult), reads=[tb["bec"].k, tb["beta"].k], writes=[tb["bec"].k])
                tap("beta", tb["beta"].t[:, :, :], [tb["beta"].k], [128, 8, NCH])
                tap("g", tb["g"].t[:, :, :], [tb["g"].k], [128, 8, NCH])
                if stop_after == "dtab":
                    P.finish()
                    return nc, tap_out

                PAR_NAMES = ("M", "L", "Ld", "iT", "qd", "vb", "kbg", "ktl", "Qb", "wTn")
                US = []
                for si in range(2):
                    u = {}
                    for nm, shp, dt in (("dg", [128, 256], F32), ("Es", [128, 128], F32), ("G", [128, 128], F32),
                                        ("M", [128, 128], F32), ("L", [128, 128], F32), ("LM0", [128, 256], F32), ("LM1", [128, 256], F32),
                                        ("Q", [128, 128], F32), ("Ld", [128, 128], F32), ("Xo", [128, 128], F32), ("Qt", [128, 128], F32), ("Ei", [128, 128], F32), ("eR", [128, 128], F32),
                                        ("Qb", [128, 128], BF16), ("vb", [128, 128], BF16), ("kbg", [128, 128], BF16), ("ktl", [128, 128], BF16),
                                        ("wTn", [128, 128], BF16), ("vn", [128, 128], BF16), ("qd", [128, 128], BF16), ("iT", [128, 128], BF16),
                                        ("S", [128, 128], F32), ("Sb", [128, 128], BF16)):
                        if nm in PAR_NAMES:
                            for p_ in range(2):
                                u["%s@%d" % (nm, p_)] = sbuf(sA, "u%d_%s_%d" % (si, nm, p_), shp, dt)
                        else:
                            u[nm] = sbuf(sA, "u%d_%s" % (si, nm), shp, dt)
                    u["banks"] = (0, 1, 2, 3) if si == 0 else (4, 5, 6, 7)
                    US.append(u)

                def unit(h, d, n, need_out, written, par=0):
                    u = dict(US[d])
                    for nm_ in PAR_NAMES:
                        u[nm_] = US[d]["%s@%d" % (nm_, par)]
                    bX, bY, bZ, bW = u["banks"]
                    j = d * 4 + h
                    col = lambda nm: tb[nm].t[:, j, n:n + 1]
                    cs = slice(n * 128, (n + 1) * 128)
                    negs = cst.t[:, C_NSA:C_NSA + 128] if d == 0 else cst.t[:, C_NSD:C_NSD + 128]
                    op("dve", lambda e: e.tensor_scalar(out=u["dg"].t[:, 0:128], in0=identf, scalar1=col("beta"), scalar2=None, op0=ALU.mult), reads=[cst.k, tb["beta"].k], writes=[u["dg"].k])
                    yield
                    op("dve", lambda e: e.tensor_scalar(out=u["dg"].t[:, 128:256], in0=identf, scalar1=col("cum"), scalar2=None, op0=ALU.mult), reads=[cst.k, tb["cum"].k], writes=[u["dg"].k])
                    yield
                    op("pe", lambda e: e.matmul(pb[bX][:, 0:256], lhsT=onesf, rhs=u["dg"].t[:, 0:256], start=True, stop=True), reads=[cst.k, u["dg"].k], writes=pkr(bX, 0, 256))
                    yield
                    op("pe", lambda e: e.matmul(pb[bX][:, 256:512].rearrange("p (a b) -> p a b", a=2), lhsT=QK.t[:, 0, cs], rhs=QK.t[:, :, cs], start=True, stop=True),
                       reads=[QK.k], writes=pkr(bX, 256, 256))
                    yield
                    op("dve", lambda e: e.scalar_tensor_tensor(out=u["Es"].t[:, :], in0=pb[bX][:, 128:256], scalar=col("cum"), in1=negs, op0=ALU.subtract, op1=ALU.add),
                       reads=pkr(bX, 128, 128) + [tb["cum"].k, cst.k], writes=[u["Es"].k])
                    yield
                    op("act", lambda e: e.activation(out=u["Es"].t[:, :], in_=u["Es"].t[:, :], func=AF.Exp), reads=[u["Es"].k], writes=[u["Es"].k])
                    yield
                    op("dve", lambda e: e.tensor_tensor(out=u["G"].t[:, :], in0=pb[bX][:, 0:128], in1=u["Es"].t[:, :], op=ALU.mult), reads=pkr(bX, 0, 128) + [u["Es"].k], writes=[u["G"].k])
                    yield
                    op("dve", lambda e: e.tensor_tensor(out=u["M"].t[:, :], in0=pb[bX][:, 256:384], in1=u["G"].t[:, :], op=ALU.mult), reads=pkr(bX, 256, 128) + [u["G"].k], writes=[u["M"].k])
                    yield
                    if need_out:
                        op("pool", lambda e: e.tensor_tensor(out=u["Ei"].t[:, :], in0=u["Es"].t[:, :], in1=identf, op=ALU.add), reads=[u["Es"].k, cst.k], writes=[u["Ei"].k])
                        yield
                        op("dve", lambda e: e.tensor_tensor(out=u["iT"].t[:, :], in0=pb[bX][:, 384:512], in1=u["Ei"].t[:, :], op=ALU.mult), reads=pkr(bX, 384, 128) + [u["Ei"].k], writes=[u["iT"].k])
                        yield
                        op("act", lambda e: e.activation(out=u["eR"].t[:, :], in_=pb[bX][:, 128:256], func=AF.Exp), reads=pkr(bX, 128, 128), writes=[u["eR"].k])
                        yield
                        op("dve", lambda e: e.tensor_tensor(out=u["qd"].t[:, :], in0=QK.t[:, 1, cs], in1=u["eR"].t[:, :], op=ALU.mult), reads=[QK.k, u["eR"].k], writes=[u["qd"].k])
                        yield
                    if UNIT_CUT == 1:
                        return
                    negsT = cst.t[:, C_NSD:C_NSD + 128] if d == 0 else cst.t[:, C_NSA:C_NSA + 128]
                    op("dve", lambda e: e.scalar_tensor_tensor(out=u["L"].t[:, :], in0=pb[bX][:, 128:256], scalar=-1.0, in1=negsT, op0=ALU.mult, op1=ALU.add),
                       reads=pkr(bX, 128, 128) + [cst.k], writes=[u["L"].k])
                    yield
                    op("act", lambda e: e.activation(out=u["L"].t[:, :], in_=u["L"].t[:, :], func=AF.Exp, bias=col("cum"), scale=1.0), reads=[u["L"].k, tb["cum"].k], writes=[u["L"].k])
                    yield
                    op("dve", lambda e: e.scalar_tensor_tensor(out=u["L"].t[:, :], in0=u["L"].t[:, :], scalar=col("beta"), in1=pb[bX][:, 256:384], op0=ALU.mult, op1=ALU.mult),
                       reads=[u["L"].k, tb["beta"].k] + pkr(bX, 256, 128), writes=[u["L"].k])
                    yield
                    blkf = cst.t[:, C_BLK:C_BLK + 128]
                    op("dve", lambda e: e.tensor_tensor(out=u["Ld"].t[:, :], in0=u["L"].t[:, :], in1=blkf, op=ALU.mult), reads=[u["L"].k, cst.k], writes=[u["Ld"].k])
                    yield
                    op("dve", lambda e: e.tensor_tensor(out=u["L"].t[:, :], in0=u["L"].t[:, :], in1=u["Ld"].t[:, :], op=ALU.subtract), reads=[u["L"].k, u["Ld"].k], writes=[u["L"].k])
                    yield
                    op("dve", lambda e: e.tensor_tensor(out=u["M"].t[:, :], in0=u["M"].t[:, :], in1=blkf, op=ALU.mult), reads=[u["M"].k, cst.k], writes=[u["M"].k])
                    yield
                    pbf = pb[bZ][:, 0:128].bitcast(BF16)
                    op("pe", lambda e: e.transpose(pbf[:, 0:128], QK.t[:, 0, cs], identb), reads=[QK.k, cb.k], writes=pkr(bZ, 0, 128))
                    yield
                    op("pe", lambda e: e.transpose(pbf[:, 128:256], vT.t[:, cs], identb), reads=[vT.k, cb.k], writes=pkr(bZ, 0, 128))
                    yield
                    op("act", lambda e: e.activation(out=u["vb"].t[:, :], in_=pbf[:, 128:256], func=AF.Copy, scale=col("beta")), reads=pkr(bZ, 0, 128) + [tb["beta"].k], writes=[u["vb"].k])
                    yield
                    op("dve", lambda e: e.tensor_scalar(out=u["kbg"].t[:, :], in0=pbf[:, 0:128], scalar1=col("bec"), scalar2=None, op0=ALU.mult), reads=pkr(bZ, 0, 128) + [tb["bec"].k], writes=[u["kbg"].k])
                    yield
                    op("dve", lambda e: e.tensor_scalar(out=u["ktl"].t[:, :], in0=pbf[:, 0:128], scalar1=col("etail"), scalar2=None, op0=ALU.mult), reads=pkr(bZ, 0, 128) + [tb["etail"].k], writes=[u["ktl"].k])
                    yield
                    yield "S2"
                    op("pool", lambda e: e.tensor_tensor(out=u["Q"].t[:, :], in0=identf, in1=u["M"].t[:, :], op=ALU.subtract), reads=[cst.k, u["M"].k], writes=[u["Q"].k])
                    yield
                    Lk, Lkt, Mk, Mkt = u["Ld"].t[:, :], [u["Ld"].k], u["M"].t[:, :], [u["M"].k]
                    for lv in range(5):
                        LM = u["LM%d" % (lv % 2)]
                        last = lv == 4
                        op("pe", lambda e: e.matmul(pb[bY][:, 128:256], lhsT=Mk, rhs=Lk, start=True, stop=True), reads=Lkt + Mkt, writes=pkr(bY, 128, 128))
                        yield
                        if not last:
                            op("pe", lambda e: e.matmul(pb[bY][:, 256:384], lhsT=Lk, rhs=Mk, start=True, stop=True), reads=Lkt + Mkt, writes=pkr(bY, 256, 128))
                            yield
                        w_ = 128 if last else 256
                        op("act", lambda e: e.activation(out=LM.t[:, 0:w_], in_=pb[bY][:, 128:128 + w_], func=AF.Copy), reads=pkr(bY, 128, w_), writes=[LM.k])
                        yield
                        op("pe", lambda e: e.matmul(pb[bW][:, 128:256], lhsT=LM.t[:, 0:128], rhs=u["Q"].t[:, :], start=True, stop=True), reads=[LM.k, u["Q"].k], writes=pkr(bW, 128, 128))
                        yield
                        op("dve", lambda e: e.tensor_tensor(out=u["Q"].t[:, :], in0=pb[bW][:, 128:256], in1=u["Q"].t[:, :], op=ALU.add), reads=pkr(bW, 128, 128) + [u["Q"].k], writes=[u["Q"].k])
                        yield
                        Lk, Lkt, Mk, Mkt = LM.t[:, 0:128], [LM.k], LM.t[:, 128:256], [LM.k]
                    op("pe", lambda e: e.transpose(pb[bY][:, 128:256], u["Q"].t[:, :], identf), reads=[u["Q"].k, cst.k], writes=pkr(bY, 128, 128))
                    yield
                    op("act", lambda e: e.activation(out=u["Qt"].t[:, :], in_=pb[bY][:, 128:256], func=AF.Copy), reads=pkr(bY, 128, 128), writes=[u["Qt"].k])
                    yield
                    op("pe", lambda e: e.matmul(pb[bW][:, 128:256], lhsT=u["L"].t[:, :], rhs=u["Q"].t[:, :], start=True, stop=True), reads=[u["L"].k, u["Q"].k], writes=pkr(bW, 128, 128))
                    yield
                    op("dve", lambda e: e.tensor_copy(out=u["Xo"].t[:, :], in_=pb[bW][:, 128:256]), reads=pkr(bW, 128, 128), writes=[u["Xo"].k])
                    yield
                    op("pe", lambda e: e.matmul(pb[bY][:, 128:256], lhsT=u["Qt"].t[:, :], rhs=u["Xo"].t[:, :], start=True, stop=True), reads=[u["Qt"].k, u["Xo"].k], writes=pkr(bY, 128, 128))
                    yield
                    op("dve", lambda e: e.tensor_tensor(out=u["Qb"].t[:, :], in0=u["Q"].t[:, :], in1=pb[bY][:, 128:256], op=ALU.subtract), reads=[u["Q"].k] + pkr(bY, 128, 128), writes=[u["Qb"].k])
                    yield
                    if UNIT_CUT == 2:
                        return
                    op("pe", lambda e: e.matmul(pb[bZ][:, 128:256], lhsT=u["kbg"].t[:, :], rhs=u["Qb"].t[:, :], start=True, stop=True), reads=[u["kbg"].k, u["Qb"].k], writes=pkr(bZ, 128, 128))
                    yield
                    op("act", lambda e: e.activation(out=u["wTn"].t[:, :], in_=pb[bZ][:, 128:256], func=AF.Copy, scale=-1.0), reads=pkr(bZ, 128, 128), writes=[u["wTn"].k])
                    yield
                    if UNIT_CUT == 3:
                        return
                    yield "S3"
                    op("pe", lambda e: e.matmul(pb[bZ][:, 256:384], lhsT=u["Qb"].t[:, :], rhs=u["vb"].t[:, :], start=True, stop=False), reads=[u["Qb"].k, u["vb"].k], writes=pkr(bZ, 256, 128))
                    yield
                    op("pe", lambda e: e.matmul(pb[bZ][:, 256:384], lhsT=u["wTn"].t[:, :], rhs=u["Sb"].t[:, :], start=False, stop=True), reads=[u["wTn"].k, u["Sb"].k], writes=pkr(bZ, 256, 128))
                    yield
                    op("act", lambda e: e.activation(out=u["vn"].t[:, :], in_=pb[bZ][:, 256:384], func=AF.Copy), reads=pkr(bZ, 256, 128), writes=[u["vn"].k])
                    yield
                    if need_out:
                        t0 = (n - 2) * 128
                        op("pe", lambda e: e.matmul(pb[bZ][:, 384:512], lhsT=u["Sb"].t[:, :], rhs=u["qd"].t[:, :], start=True, stop=False), reads=[u["Sb"].k, u["qd"].k], writes=pkr(bZ, 384, 128))
                        yield
                        op("pe", lambda e: e.matmul(pb[bZ][:, 384:512], lhsT=u["vn"].t[:, :], rhs=u["iT"].t[:, :], start=False, stop=True), reads=[u["vn"].k, u["iT"].k], writes=pkr(bZ, 384, 128))
                        yield
                        if n not in written:
                            op("act", lambda e: e.activation(out=oT.t[:, t0:t0 + 128], in_=pb[bZ][:, 384:512], func=AF.Copy), reads=pkr(bZ, 384, 128), writes=[oT.tr(n)])
                            yield
                            written.add(n)
                        else:
                            op("dve", lambda e: e.tensor_tensor(out=oT.t[:, t0:t0 + 128], in0=pb[bZ][:, 384:512], in1=oT.t[:, t0:t0 + 128], op=ALU.add), reads=pkr(bZ, 384, 128) + [oT.tr(n)], writes=[oT.tr(n)])
                            yield
                    op("pe", lambda e: e.matmul(pb[bW][:, 0:128], lhsT=u["ktl"].t[:, :], rhs=u["vn"].t[:, :], start=True, stop=True), reads=[u["ktl"].k, u["vn"].k], writes=pkr(bW, 0, 128))
                    yield
                    op("dve", lambda e: e.scalar_tensor_tensor(out=u["S"].t[:, :], in0=u["S"].t[:, :], scalar=col("egl"), in1=pb[bW][:, 0:128], op0=ALU.mult, op1=ALU.add),
                       reads=[u["S"].k, tb["egl"].k] + pkr(bW, 0, 128), writes=[u["S"].k])
                    yield
                    op("act", lambda e: e.activation(out=u["Sb"].t[:, :], in_=u["S"].t[:, :], func=AF.Copy), reads=[u["S"].k], writes=[u["Sb"].k])
                    yield

                NOUT = NT // 128
                asc_order = [0, 1] + list(range(2, 2 + NOUT))
                desc_order = [1, 0] + list(range(NCH - 1, 1, -1))
                for h in range(4):
                    for si, (wc0, dst) in enumerate(((h * 128, "q"), (512 + h * 128, "k"), (1024 + h * 128, "v"))):
                        w_ = wh.next()
                        load_w(w_, hyb_w_in[:, wc0:wc0 + 128], 8)
                        for ti, (c0, N) in enumerate(FULL_TILES):
                            bank = ti % 4
                            for k in range(8):
                                rhs, trs = aT_rhs(k, c0, N)
                                op("pe", lambda e: e.matmul(pb[bank][:, 0:N], lhsT=w_.t[:, k, :], rhs=rhs, start=(k == 0), stop=(k == 7)), reads=[w_.k] + trs, writes=pkr(bank, 0, N))
                            po = 2 if ti == 0 else 6 + c0
                            op("act", lambda e: e.activation(out=pre.t[:, po:po + N], in_=pb[bank][:, 0:N], func=AF.Copy), reads=pkr(bank, 0, N), writes=[pre.tr(ti)])
                        ch = {"q": 0, "k": 4, "v": 8}[dst] + h
                        for ti, (c0, N) in enumerate(FULL_TILES):
                            po = 0 if ti == 0 else 4 + c0
                            acc = acc_r.next()
                            eng = "dve"
                            ptr = [pre.k] + [pre.tr(t_) for t_ in (ti - 1, ti, ti + 1) if 0 <= t_ < len(FULL_TILES)]
                            op(eng, lambda e: e.tensor_scalar(out=acc.t[:, 0:N], in0=pre.t[:, po:po + N], scalar1=vcol(V_HCONV + ch), scalar2=None, op0=ALU.mult), reads=ptr + [vec.k], writes=[acc.k])
                            for jt in range(1, 5):
                                op(eng, lambda e: e.scalar_tensor_tensor(out=acc.t[:, 0:N], in0=pre.t[:, po + jt:po + jt + N], scalar=vcol(V_HCONV + jt * 12 + ch), in1=acc.t[:, 0:N], op0=ALU.mult, op1=ALU.add),
                                   reads=ptr + [vec.k, acc.k], writes=[acc.k])
                            if dst == "v":
                                op("act", lambda e: e.activation(out=vT.t[:, c0:c0 + N], in_=acc.t[:, 0:N], func=AF.Silu), reads=[acc.k], writes=[vT.k])
                            else:
                                op("act", lambda e: e.activation(out=acc.t[:, 0:N], in_=acc.t[:, 0:N], func=AF.Silu), reads=[acc.k], writes=[acc.k])
                                r = rms_rstd([(acc.t[:, 0:N], [acc.k])], N, 1.0, onesb)
                                sc_ = (128.0 ** -0.5) if dst == "q" else 1.0
                                op("dve", lambda e: e.scalar_tensor_tensor(out=QK.t[:, 1 if dst == "q" else 0, c0:c0 + N], in0=acc.t[:, 0:N], scalar=sc_, in1=r.t[:, 0:N], op0=ALU.mult, op1=ALU.mult),
                                   reads=[acc.k, r.k], writes=[QK.k])
                    if h == 0:
                        tap("qT", QK.t[:, 1, 0:768], [QK.k], [128, 768])
                        tap("kT", QK.t[:, 0, 0:768], [QK.k], [128, 768])
                        tap("vT", vT.t[:, 0:768], [vT.k], [128, 768])
                        if stop_after == "dproj":
                            P.finish()
                            return nc, tap_out
                    for d in range(2):
                        op("pool", lambda e: e.memset(US[d]["S"].t[:, :], 0.0), writes=[US[d]["S"].k])
                        op("pool", lambda e: e.memset(US[d]["Sb"].t[:, :], 0.0), writes=[US[d]["Sb"].k])
                    written = set()
                    if stop_after == "dunit":
                        for _ in unit(h, 0, 0, False, written):
                            pass
                        for _ in unit(h, 1, 3, True, written):
                            pass
                        tap("M", US[0]["M"].t[:, :], [US[0]["M"].k], [128, 128])
                        tap("Q", US[0]["Q"].t[:, :], [US[0]["Q"].k], [128, 128])
                        P.finish()
                        return nc, tap_out
                    orders = [asc_order, desc_order]
                    nxt = [0, 0]
                    act = [[], []]
                    while act[0] or act[1] or nxt[0] < len(orders[0]) or nxt[1] < len(orders[1]):
                        for d_ in (0, 1):
                            if nxt[d_] < len(orders[d_]) and len(act[d_]) < 2 and (not act[d_] or act[d_][-1][1] >= 2 or act[d_][-1][2] is not None):
                                i_ = nxt[d_]
                                n_ = orders[d_][i_]
                                nxt[d_] += 1
                                need_ = (n_ >= 2) if d_ == 0 else (2 <= n_ < 2 + NOUT)
                                act[d_].append([unit(h, d_, n_, need_, written, i_ % 2), 1, None])
                            for ent in list(act[d_]):
                                if ent[2] == "S2":
                                    if act[d_][0] is ent or act[d_][0][1] == 3:
                                        ent[1], ent[2] = 2, None
                                    else:
                                        continue
                                elif ent[2] == "S3":
                                    if act[d_][0] is ent:
                                        ent[1], ent[2] = 3, None
                                    else:
                                        continue
                                try:
                                    r_ = next(ent[0])
                                except StopIteration:
                                    act[d_].remove(ent)
                                    continue
                                if r_ in ("S2", "S3"):
                                    ent[2] = r_
                    if h == 0:
                        tap("oT", oT.t[:, 0:512], [oT.tr(n_) for n_ in range(2, 6)], [128, 512])
                    wz = wh.next()
                    load_w(wz, hyb_w_in[:, 1536 + h * 128:1536 + (h + 1) * 128], 8)
                    for (c0, N) in OWN_TILES:
                        otr = [oT.tr(n_) for n_ in range(2 + c0 // 128, 2 + (c0 + N) // 128)]
                        r = rms_rstd([(oT.t[:, c0:c0 + N], otr)], N, 128.0, onesb)
                        bank = 0
                        for k in range(8):
                            rhs, trs = aT_rhs(k, TC + c0, N)
                            op("pe", lambda e: e.matmul(pb[bank][:, 0:N], lhsT=wz.t[:, k, :], rhs=rhs, start=(k == 0), stop=(k == 7)), reads=[wz.k] + trs, writes=pkr(bank, 0, N))
                        t1 = tmp_r.next(); t2 = tmp_r.next()
                        op("act", lambda e: e.activation(out=t1.t[:, 0:N], in_=pb[bank][:, 0:N], func=AF.Silu), reads=pkr(bank, 0, N), writes=[t1.k])
                        op("dve", lambda e: e.scalar_tensor_tensor(out=t2.t[:, 0:N], in0=oT.t[:, c0:c0 + N], scalar=vcol(V_ONORM), in1=r.t[:, 0:N], op0=ALU.mult, op1=ALU.mult),
                           reads=otr + [vec.k, r.k], writes=[t2.k])
                        op("dve", lambda e: e.tensor_tensor(out=mix.t[:, h, c0:c0 + N], in0=t2.t[:, 0:N], in1=t1.t[:, 0:N], op=ALU.mult), reads=[t1.k, t2.k], writes=[mix.tr(h)])
                tap("ya", mix.t[:, 0, 0:512], [mix.tr(0)], [128, 512])
            if stop_after == "delta":
                P.finish()
                return nc, tap_out

        h = sbuf(st, "h", [128, 8, NT], F32)
        for ti, (c0, N) in enumerate(OWN_TILES):
            P.dma("sp", h.t[:, :, c0:c0 + N], xT[:, c0:c0 + N].rearrange("(k p) n -> p k n", p=128), writes=[h.tr(ti)])
        yb_r = Ring([sbuf(st, "yb%d" % i, [128, 8, 512], F32) for i in range(2)])
        mmb2 = Ring([0, 1, 2, 3, 4, 5])

        def post_res(yb, N, l, Gb, ti, c0):
            r = rms_rstd([(yb.t[:, m, 0:N], [yb.k]) for m in range(8)], N, D, onesb)
            for m in range(8):
                t = tmp_r.next()
                op("dve", lambda e: e.scalar_tensor_tensor(out=t.t[:, 0:N], in0=yb.t[:, m, 0:N], scalar=Gb.t[:, l, m:m + 1], in1=r.t[:, 0:N], op0=ALU.mult, op1=ALU.mult),
                   reads=[yb.k, Gb.k, r.k], writes=[t.k])
                op("dve", lambda e: e.tensor_tensor(out=h.t[:, m, c0:c0 + N], in0=h.t[:, m, c0:c0 + N], in1=t.t[:, 0:N], op=ALU.add), reads=[t.k, h.tr(ti)], writes=[h.tr(ti)])

        with scope() as sF:
            wo = sbuf(sF, "wo", [128, 8, D], BF16)
            load_w(wo, hyb_w_out[:, :], 8)
            for ti, (c0, N) in enumerate(OWN_TILES):
                yb = yb_r.next()
                for m in range(8):
                    bank = mmb2.next()
                    for k in range(8):
                        op("pe", lambda e: e.matmul(pb[bank][:, 0:N], lhsT=wo.t[:, k, m * 128:(m + 1) * 128], rhs=mix.t[:, k, c0:c0 + N], start=(k == 0), stop=(k == 7)),
                           reads=[wo.k, mix.tr(k)], writes=pkr(bank, 0, N))
                    op("act", lambda e: e.activation(out=yb.t[:, m, 0:N], in_=pb[bank][:, 0:N], func=AF.Copy), reads=pkr(bank, 0, N), writes=[yb.k])
                if ti == 0:
                    tap("ylat", yb.t[:, 0, 0:512], [yb.k], [128, 512])
                post_res(yb, N, 0, G1, ti, c0)
        tap("hmix0", h.t[:, 0, 0:512], [h.tr(0)], [128, 512])
        if stop_after == "mix0":
            P.finish()
            return nc, tap_out

        def ffn(l, tiles):
            groups = [tiles[i:i + 2] for i in range(0, len(tiles), 2)]
            with scope() as sf:
                mixflat = mix.t[:].rearrange("p k n -> p (k n)")
                fx = sbuf(sf, "fx", [128, 13, 1024], BF16)
                ag_k = [Trk(), Trk()]
                hid_k = [[Trk() for _ in range(FFC)] for _ in range(2)]
                agv = lambda k, s_, N: fx.t[:, k, s_ * 512:s_ * 512 + N]

                def hidv(j, s_, N):
                    if j < 17:
                        return mixflat[:, j * 1024 + s_ * 512:j * 1024 + s_ * 512 + N]
                    return fx.t[:, 8 + j - 17, s_ * 512:s_ * 512 + N]
                wg_r = Ring([sbuf(sf, "wg%d" % i, [128, 8, 256], BF16) for i in range(2)])
                wu_r = Ring([sbuf(sf, "wu%d" % i, [128, 8, 256], BF16) for i in range(2)])
                wo_r = Ring([sbuf(sf, "wfo%d" % i, [128, FFC, 128], BF16) for i in range(2)])
                for grp in groups:
                    for s_, (ti, (c0, N)) in enumerate(grp):
                        norm_mod([(h.t[:, k, c0:c0 + N], [h.tr(ti)]) for k in range(8)], N,
                                 [(agv(k, s_, N), [ag_k[s_]]) for k in range(8)], A2, l, 0, 24)
                    for jb in range(FFC // 2):
                        wg = wg_r.next(); wu = wu_r.next()
                        load_w(wg, w_ffn_in[l, :, jb * 256:(jb + 1) * 256], 8)
                        load_w(wu, w_ffn_in[l, :, FF + jb * 256:FF + (jb + 1) * 256], 8)
                        for jj in range(2):
                            j = 2 * jb + jj
                            for s_, (ti, (c0, N)) in enumerate(grp):
                                bg = mmb2.next(); bu = mmb2.next()
                                for k in range(8):
                                    op("pe", lambda e: e.matmul(pb[bg][:, 0:N], lhsT=wg.t[:, k, jj * 128:(jj + 1) * 128], rhs=agv(k, s_, N), start=(k == 0), stop=(k == 7)),
                                       reads=[wg.k, ag_k[s_]], writes=pkr(bg, 0, N))
                                for k in range(8):
                                    op("pe", lambda e: e.matmul(pb[bu][:, 0:N], lhsT=wu.t[:, k, jj * 128:(jj + 1) * 128], rhs=agv(k, s_, N), start=(k == 0), stop=(k == 7)),
                                       reads=[wu.k, ag_k[s_]], writes=pkr(bu, 0, N))
                                t = tmp_r.next()
                                op("act", lambda e: e.activation(out=t.t[:, 0:N], in_=pb[bg][:, 0:N], func=AF.Silu), reads=pkr(bg, 0, N), writes=[t.k])
                                op("dve", lambda e: e.tensor_tensor(out=hidv(j, s_, N), in0=pb[bu][:, 0:N], in1=t.t[:, 0:N], op=ALU.mult), reads=pkr(bu, 0, N) + [t.k], writes=[hid_k[s_][j]])
                    ybs = [yb_r.next() for _ in grp]
                    for m in range(8):
                        wfo = wo_r.next()
                        P.dma("pool", wfo.t[:, :, :], w_ffn_out[l, :, m * 128:(m + 1) * 128].rearrange("(j p) c -> p j c", p=128), writes=[wfo.k])
                        for s_, (ti, (c0, N)) in enumerate(grp):
                            bank = mmb2.next()
                            for j in range(FFC):
                                op("pe", lambda e: e.matmul(pb[bank][:, 0:N], lhsT=wfo.t[:, j, :], rhs=hidv(j, s_, N), start=(j == 0), stop=(j == FFC - 1)),
                                   reads=[wfo.k, hid_k[s_][j]], writes=pkr(bank, 0, N))
                            op("act", lambda e: e.activation(out=ybs[s_].t[:, m, 0:N], in_=pb[bank][:, 0:N], func=AF.Copy), reads=pkr(bank, 0, N), writes=[ybs[s_].k])
                    for s_, (ti, (c0, N)) in enumerate(grp):
                        post_res(ybs[s_], N, l, G2, ti, c0)

        ffn(0, list(enumerate(OWN_TILES)))
        tap("hl0", h.t[:, 0, 0:512], [h.tr(0)], [128, 512])
        if stop_after == "l0":
            P.finish()
            return nc, tap_out

        ub = mix
        with scope() as sC1:
            a1 = sbuf(sC1, "a1", [128, 8, NT], BF16)
            wcv_r = Ring([sbuf(sC1, "wcv%d" % i, [128, 8, 128], BF16) for i in range(2)])
            wcg_r = Ring([sbuf(sC1, "wcg%d" % i, [128, 8, 128], BF16) for i in range(2)])
            for ti, (c0, N) in enumerate(OWN_TILES):
                norm_mod([(h.t[:, k, c0:c0 + N], [h.tr(ti)]) for k in range(8)], N,
                         [(a1.t[:, k, c0:c0 + N], [a1.tr(ti)]) for k in range(8)], A1, 1, 0, 0)
            op("pool", lambda e: e.memset(ub.t[:, :, 0:15], 0.0), writes=[ub.tr(k) for k in range(8)])
            for m in range(8):
                wcv = wcv_r.next(); wcg = wcg_r.next()
                load_w(wcv, conf_w_in[:, m * 128:(m + 1) * 128], 8)
                load_w(wcg, conf_w_in[:, D + m * 128:D + (m + 1) * 128], 8)
                for ti, (c0, N) in enumerate(OWN_TILES):
                    bv = mmb2.next(); bg = mmb2.next()
                    for k in range(8):
                        op("pe", lambda e: e.matmul(pb[bv][:, 0:N], lhsT=wcv.t[:, k, :], rhs=a1.t[:, k, c0:c0 + N], start=(k == 0), stop=(k == 7)), reads=[wcv.k, a1.tr(ti)], writes=pkr(bv, 0, N))
                    for k in range(8):
                        op("pe", lambda e: e.matmul(pb[bg][:, 0:N], lhsT=wcg.t[:, k, :], rhs=a1.t[:, k, c0:c0 + N], start=(k == 0), stop=(k == 7)), reads=[wcg.k, a1.tr(ti)], writes=pkr(bg, 0, N))
                    t = tmp_r.next()
                    op("act", lambda e: e.activation(out=t.t[:, 0:N], in_=pb[bg][:, 0:N], func=AF.Sigmoid, bias=vcol(V_CBIN + 8 + m), scale=1.0), reads=pkr(bg, 0, N) + [vec.k], writes=[t.k])
                    op("dve", lambda e: e.scalar_tensor_tensor(out=ub.t[:, m, 15 + c0:15 + c0 + N], in0=pb[bv][:, 0:N], scalar=vcol(V_CBIN + m), in1=t.t[:, 0:N], op0=ALU.add, op1=ALU.mult),
                       reads=pkr(bv, 0, N) + [vec.k, t.k], writes=[ub.tr(m)])
        with scope() as sC2:
            actb = sbuf(sC2, "actb", [128, 8, 512], BF16)
            wco = sbuf(sC2, "wco", [128, 8, D], BF16)
            dgm_r = Ring([sbuf(sC2, "dgm%d" % i, [128, 31, 128], BF16) for i in range(2)])
            lnt = [sbuf(sC2, "lnt%d" % i, [128, 512], F32) for i in range(3)]
            load_w(wco, conf_w_out[:, :], 8)
            for ti, (c0, N) in enumerate(OWN_TILES[:4]):
                cv = yb_r.next()
                for k in range(8):
                    dgm = dgm_r.next()
                    op("dve", lambda e: e.tensor_tensor(out=dgm.t[:, :, :], in0=identb.unsqueeze(1).to_broadcast([128, 31, 128]),
                                                        in1=vec.t[:, V_CDW + k * 31:V_CDW + (k + 1) * 31].unsqueeze(2).to_broadcast([128, 31, 128]), op=ALU.mult),
                       reads=[cb.k, vec.k], writes=[dgm.k])
                    bank = mmb2.next()
                    for jt in range(31):
                        op("pe", lambda e: e.matmul(pb[bank][:, 0:N], lhsT=dgm.t[:, jt, :], rhs=ub.t[:, k, c0 + jt:c0 + jt + N], start=(jt == 0), stop=(jt == 30)),
                           reads=[dgm.k, ub.tr(k)], writes=pkr(bank, 0, N))
                    op("act", lambda e: e.activation(out=cv.t[:, k, 0:N], in_=pb[bank][:, 0:N], func=AF.Identity, bias=vcol(V_CDWB + k), scale=1.0), reads=pkr(bank, 0, N) + [vec.k], writes=[cv.k])
                b1 = stat_banks.next(); b2 = stat_banks.next()
                for k in range(8):
                    op("pe", lambda e: e.matmul(pb[b1][:, 0:N], lhsT=onesf, rhs=cv.t[:, k, 0:N], start=(k == 0), stop=(k == 7)), reads=[cst.k, cv.k], writes=pkr(b1, 0, N))
                for k in range(8):
                    sq = sqr.next()
                    op("act", lambda e: e.activation(out=sq.t[:, 0:N], in_=cv.t[:, k, 0:N], func=AF.Square), reads=[cv.k], writes=[sq.k])
                    op("pe", lambda e: e.matmul(pb[b2][:, 0:N], lhsT=onesb, rhs=sq.t[:, 0:N], start=(k == 0), stop=(k == 7)), reads=[sq.k, cb.k], writes=pkr(b2, 0, N))
                mean, msq, rs = lnt
                op("act", lambda e: e.activation(out=mean.t[:, 0:N], in_=pb[b1][:, 0:N], func=AF.Copy, scale=1.0 / D), reads=pkr(b1, 0, N), writes=[mean.k])
                op("dve", lambda e: e.tensor_tensor(out=msq.t[:, 0:N], in0=mean.t[:, 0:N], in1=mean.t[:, 0:N], op=ALU.mult), reads=[mean.k], writes=[msq.k])
                op("dve", lambda e: e.scalar_tensor_tensor(out=rs.t[:, 0:N], in0=pb[b2][:, 0:N], scalar=1.0 / D, in1=msq.t[:, 0:N], op0=ALU.mult, op1=ALU.subtract),
                   reads=pkr(b2, 0, N) + [msq.k], writes=[rs.k])
                op("act", lambda e: e.activation(out=rs.t[:, 0:N], in_=rs.t[:, 0:N], func=AF.Ln, bias=EPS, scale=1.0), reads=[rs.k], writes=[rs.k])
                op("act", lambda e: e.activation(out=rs.t[:, 0:N], in_=rs.t[:, 0:N], func=AF.Exp, scale=-0.5), reads=[rs.k], writes=[rs.k])
                for k in range(8):
                    t1 = tmp_r.next(); t2 = tmp_r.next()
                    op("dve", lambda e: e.tensor_tensor(out=t1.t[:, 0:N], in0=cv.t[:, k, 0:N], in1=mean.t[:, 0:N], op=ALU.subtract), reads=[cv.k, mean.k], writes=[t1.k])
                    op("dve", lambda e: e.scalar_tensor_tensor(out=t2.t[:, 0:N], in0=t1.t[:, 0:N], scalar=vcol(V_CLNG + k), in1=rs.t[:, 0:N], op0=ALU.mult, op1=ALU.mult), reads=[t1.k, vec.k, rs.k], writes=[t2.k])
                    op("act", lambda e: e.activation(out=actb.t[:, k, 0:N], in_=t2.t[:, 0:N], func=AF.Silu, bias=vcol(V_CLNB + k), scale=1.0), reads=[t2.k, vec.k], writes=[actb.k])
                yb = cv
                for m in range(8):
                    bank = mmb2.next()
                    for k in range(8):
                        op("pe", lambda e: e.matmul(pb[bank][:, 0:N], lhsT=wco.t[:, k, m * 128:(m + 1) * 128], rhs=actb.t[:, k, 0:N], start=(k == 0), stop=(k == 7)), reads=[wco.k, actb.k], writes=pkr(bank, 0, N))
                    op("act", lambda e: e.activation(out=yb.t[:, m, 0:N], in_=pb[bank][:, 0:N], func=AF.Identity, bias=vcol(V_CBOUT + m), scale=1.0), reads=pkr(bank, 0, N) + [vec.k], writes=[yb.k])
                if ti == 0:
                    tap("yconf", yb.t[:, 0, 0:512], [yb.k], [128, 512])
                post_res(yb, N, 1, G1, ti, c0)
        tap("hmix1", h.t[:, 0, 0:512], [h.tr(0)], [128, 512])
        if stop_after == "mix1":
            P.finish()
            return nc, tap_out

        ffn(1, list(enumerate(OWN_TILES[:4])))
        for ti, (c0, N) in enumerate(OWN_TILES[:4]):
            P.dma("sp", outT[:, c0:c0 + N].rearrange("(k p) n -> p k n", p=128), h.t[:, :, c0:c0 + N], reads=[h.tr(ti)])
        P.finish()
    return nc, tap_out


def _consts():
    c = np.zeros((128, NCONST), np.float32)
    i = np.arange(128)
    c[:, C_ID:C_ID + 128] = np.eye(128, dtype=np.float32)
    c[:, C_ONES:C_ONES + 128] = 1.0
    c[:, C_TRIA:C_TRIA + 128] = (i[:, None] <= i[None, :])
    c[:, C_TRID:C_TRID + 128] = (i[:, None] >= i[None, :])
    c[:, C_NSA:C_NSA + 128] = np.where(i[None, :] > i[:, None], 0.0, NEG)
    c[:, C_NSD:C_NSD + 128] = np.where(i[None, :] < i[:, None], 0.0, NEG)
    c[:, C_BLK:C_BLK + 128] = (i[:, None] // 64 == i[None, :] // 64)
    rot = np.zeros((128, 128), np.float32)
    for hd in range(2):
        for ax in range(2):
            for f in range(16):
                a0 = hd * 64 + ax * 32 + f
                a1 = a0 + 16
                rot[a1, a0] = -1.0
                rot[a0, a1] = 1.0
    c[:, C_ROT:C_ROT + 128] = rot
    return c


def _rope_tables():
    t = np.arange(TL)
    row = (t // 64).astype(np.float32)
    col = (t % 64).astype(np.float32)
    inv = (np.float32(10000.0) ** (-np.arange(16, dtype=np.float32) / np.float32(16))).astype(np.float32)
    ar = row[:, None] * inv
    ac = col[:, None] * inv
    ang = np.concatenate([ar, ar, ac, ac], axis=-1).astype(np.float32)
    return np.cos(ang).astype(np.float32), np.sin(ang).astype(np.float32)


def _fm(v):
    return np.ascontiguousarray(v.reshape(-1, 128).T)


def prep_core(inp, core, shared):
    b, half = core // 2, core % 2
    rev = half == 1
    f = lambda a: np.ascontiguousarray(a, dtype=np.float32)
    x_b = inp["x"][b]
    ctx_b = inp["ctx"][b]
    cos, sin = shared["rope"]
    if rev:
        x_b = x_b[::-1]; ctx_b = ctx_b[::-1]; cos = cos[::-1]; sin = sin[::-1]
    m = {}
    m["xT"] = f(x_b.T)
    m["ctxT"] = f(ctx_b.T)
    cinv = np.zeros((128, 8, 2), np.float32)
    cinv[:, :, 0] = _fm(inp["c"][b]); cinv[:, :, 1] = _fm(inp["c_ctx"])
    m["cin"] = cinv.reshape(128, 16)
    v = np.zeros((128, NV), np.float32)
    for l in range(2):
        o = l * V_L
        v[:, o:o + 8] = _fm(inp["g_mix_pre"][l]); v[:, o + 8:o + 16] = _fm(inp["g_mix_post"][l])
        v[:, o + 16:o + 24] = _fm(inp["g_ffn_pre"][l]); v[:, o + 24:o + 32] = _fm(inp["g_ffn_post"][l])
        v[:, o + 32:o + 80] = _fm(inp["b_mod"][l])
    v[:, V_CBIN:V_CBIN + 16] = _fm(inp["conf_b_in"][0]); v[:, V_CDWB:V_CDWB + 8] = _fm(inp["conf_dw_b"][0])
    v[:, V_CLNG:V_CLNG + 8] = _fm(inp["conf_ln_g"][0]); v[:, V_CLNB:V_CLNB + 8] = _fm(inp["conf_ln_b"][0])
    v[:, V_CBOUT:V_CBOUT + 8] = _fm(inp["conf_b_out"][0])
    dw = inp["conf_dw_w"][0]; hc = inp["hyb_conv_w"][0]
    if rev:
        dw = dw[::-1]; hc = hc[::-1]
    for j in range(31):
        fm = _fm(dw[j])
        for k in range(8):
            v[:, V_CDW + k * 31 + j] = fm[:, k]
    for j in range(5):
        v[:, V_HCONV + j * 12:V_HCONV + j * 12 + 12] = _fm(hc[j])
    v[:, V_ONORM] = inp["hyb_out_norm"][0]
    v[:, V_QN] = np.tile(inp["hyb_q_norm"][0], 2); v[:, V_KN] = np.tile(inp["hyb_k_norm"][0], 2)
    m["vecs"] = v
    al = inp["hyb_a_log"][0]; dtb = inp["hyb_dt_bias"][0]
    if rev:
        al = al[::-1]; dtb = dtb[::-1]
    m["abc"] = f(np.tile(np.concatenate([al.reshape(-1), dtb.reshape(-1)])[None, :], (128, 1)))
    m["consts"] = shared["consts"]
    tab = np.zeros((2, 128, TL), np.float32)
    tab[0, 0:64] = cos.T; tab[0, 64:128] = cos.T; tab[1, 0:64] = sin.T; tab[1, 64:128] = sin.T
    m["rope"] = tab
    m["w_mod"] = shared["w_mod"]
    m["hyb_w_in"] = shared["hyb_w_in_rev"] if rev else shared["hyb_w_in"]
    m["hyb_w_out"] = shared["hyb_w_out"]
    m["w_ffn_in"] = shared["w_ffn_in"]; m["w_ffn_out"] = shared["w_ffn_out"]
    m["conf_w_in"] = shared["conf_w_in"]; m["conf_w_out"] = shared["conf_w_out"]
    return m


def prep_shared(inp):
    f = lambda a: np.ascontiguousarray(a, dtype=np.float32)
    sh = {"consts": _consts(), "rope": _rope_tables()}
    sh["w_mod"] = f(inp["w_mod"]); sh["w_ffn_in"] = f(inp["w_ffn_in"]); sh["w_ffn_out"] = f(inp["w_ffn_out"])
    sh["conf_w_in"] = f(inp["conf_w_in"][0]); sh["conf_w_out"] = f(inp["conf_w_out"][0])
    w = inp["hyb_w_in"][0]
    qcols = []
    for c in range(4):
        qcols += list(range(2064 + c * 64, 2064 + c * 64 + 64)) + list(range(2064 + (4 + c) * 64, 2064 + (4 + c) * 64 + 64))
    tail = list(range(2576, 2832))
    ba = list(range(2048, 2064))
    ba_rev = [2048 + kind * 8 + (1 - d) * 4 + h for kind in range(2) for d in range(2) for h in range(4)]
    base = list(range(2048))
    sh["hyb_w_in"] = f(w[:, base + ba + qcols + tail])
    sh["hyb_w_in_rev"] = f(w[:, base + ba_rev + qcols + tail])
    wo = inp["hyb_w_out"][0]
    rows = list(range(512))
    for c in range(4):
        rows += list(range(512 + c * 64, 512 + c * 64 + 64)) + list(range(512 + (4 + c) * 64, 512 + (4 + c) * 64 + 64))
    sh["hyb_w_out"] = f(wo[rows, :])
    return sh


def kernel(**inputs):
    inp = {k: np.asarray(v) for k, v in inputs.items()}
    sh = prep_shared(inp)
    nc, _ = build()
    in_maps = [prep_core(inp, c, sh) for c in range(8)]
    res = run_bass_kernel_spmd(nc, in_maps, core_ids=list(range(8)))
    out = np.zeros((4, TL, D), np.float32)
    for c in range(8):
        b, half = c // 2, c % 2
        o = res.results[c]["outT"].T
        if half == 0:
            out[b, 0:NOWN] = o
        else:
            out[b, TL - 1 - np.arange(NOWN)] = o
    return out
```

```python
import numpy as np
from contextlib import ExitStack, contextmanager
from collections import deque
import concourse.bass as bass
import concourse.mybir as mybir
from concourse.bass_utils import run_bass_kernel_spmd

F32 = mybir.dt.float32
BF16 = mybir.dt.bfloat16
AF = mybir.ActivationFunctionType
ALU = mybir.AluOpType

NDMA = 12
D = 1024
KC = 8
TC = 256
TL = 4096
TF = TC + TL
NOWN = 2048
NT = 2176
NTP = NT + 16
FF = 2816
FFC = 22
EPS = 1e-6
CH = 128
NCH = TF // CH
NEG = -30000.0
import os
UNIT_CUT = int(os.environ.get('UNIT_CUT', '99'))

C_ID, C_ONES, C_TRIA, C_TRID, C_NSA, C_NSD, C_BLK, C_ROT, NCONST = 0, 128, 256, 384, 512, 640, 768, 896, 1024
V_L = 80
V_CBIN, V_CDWB, V_CLNG, V_CLNB, V_CBOUT = 160, 176, 184, 192, 200
V_CDW = 208
V_HCONV = V_CDW + 248
V_ONORM = V_HCONV + 60
V_QN = V_ONORM + 1
V_KN = V_QN + 1
NV = V_KN + 1

OWN_TILES = [(0, 512), (512, 512), (1024, 512), (1536, 512), (2048, 128)]
FULL_TILES = [(0, 256)] + [(256 + i * 512, 512) for i in range(8)]


class Trk:
    __slots__ = ("w", "r", "psum")

    def __init__(self, psum=False):
        self.w = None
        self.r = []
        self.psum = psum


class Buf:
    def __init__(self, t):
        self.t = t
        self.k = Trk()
        self._ks = {}

    def tr(self, key):
        if key not in self._ks:
            self._ks[key] = Trk()
        return self._ks[key]


class Prog:
    def __init__(self, nc, stack):
        self.nc = nc
        self.eng = {}
        for k, h in (("pe", nc.tensor), ("act", nc.scalar), ("dve", nc.vector),
                     ("pool", nc.gpsimd), ("sp", nc.sync)):
            sem = stack.enter_context(nc.semaphore("s_" + k))
            self.eng[k] = dict(k=k, h=h, sem=sem, cnt=0, seen={}, dslots=None, dnext=0)
        for q in ("sp", "pool"):
            self.eng[q]["dslots"] = [[stack.enter_context(nc.semaphore("d%s%d" % (q, i))), 0]
                                     for i in range(NDMA)]
        self.n_ops = 0

    def _wait(self, e, ev):
        sem, val = ev
        key = id(sem)
        if e["seen"].get(key, 0) >= val:
            return
        if e["k"] == "pe" and sem is e["sem"]:
            return
        e["h"].wait_ge(sem, val)
        e["seen"][key] = val

    def _deps(self, e, reads, writes):
        for t in reads:
            if t.w is not None:
                self._wait(e, t.w)
        for t in writes:
            if t.w is not None:
                self._wait(e, t.w)
            for r in t.r:
                self._wait(e, r)

    def _commit(self, ev, reads, writes):
        for t in reads:
            t.r.append(ev)
            if len(t.r) > 16:
                best = {}
                for s, v in t.r:
                    if id(s) not in best or best[id(s)][1] < v:
                        best[id(s)] = (s, v)
                t.r = list(best.values())
        for t in writes:
            t.w = ev
            t.r = []

    def op(self, ek, fn, reads=(), writes=()):
        e = self.eng[ek]
        pr = [t for t in reads if getattr(t, "psum", False)]
        if pr:
            reads = [t for t in reads if not getattr(t, "psum", False)]
            writes = list(writes) + pr
        self._deps(e, reads, writes)
        ins = fn(e["h"])
        e["cnt"] += 1
        ins.then_inc(e["sem"], 1)
        self._commit((e["sem"], e["cnt"]), reads, writes)
        self.n_ops += 1

    def dma(self, qk, out, in_, reads=(), writes=()):
        e = self.eng[qk]
        slot = e["dslots"][e["dnext"] % NDMA]
        e["dnext"] += 1
        if slot[1] > 0:
            self._wait(e, (slot[0], slot[1]))
        self._deps(e, reads, writes)
        e["h"].dma_start(out=out, in_=in_).then_inc(slot[0], 16)
        slot[1] += 16
        self._commit((slot[0], slot[1]), reads, writes)
        self.n_ops += 1

    def barrier(self):
        evs = []
        for o in self.eng.values():
            if o["cnt"] > 0:
                evs.append((o["sem"], o["cnt"]))
            if o["dslots"]:
                for sm, v in o["dslots"]:
                    if v > 0:
                        evs.append((sm, v))
        for e in self.eng.values():
            for ev in evs:
                if ev[0] is e["sem"]:
                    continue
                self._wait(e, ev)

    def finish(self):
        e = self.eng["sp"]
        for o in self.eng.values():
            if o["dslots"]:
                for s, v in o["dslots"]:
                    if v > 0:
                        self._wait(e, (s, v))
            if o["cnt"] > 0 and o is not e:
                self._wait(e, (o["sem"], o["cnt"]))


class Ring:
    def __init__(self, bufs):
        self.b = bufs
        self.i = 0

    def next(self):
        b = self.b[self.i % len(self.b)]
        self.i += 1
        return b


def build(taps=(), stop_after=None):
    nc = bass.Bass("TRN2", target_bir_lowering=False)
    dr = {}

    def din(name, shape):
        dr[name] = nc.dram_tensor(name, list(shape), F32, kind="ExternalInput").ap()
        return dr[name]

    xT = din("xT", [D, TL]); ctxT = din("ctxT", [D, TC]); cin = din("cin", [128, 16])
    vecs_d = din("vecs", [128, NV]); abc_d = din("abc", [128, 16]); cst_d = din("consts", [128, NCONST])
    rope_d = din("rope", [2, 128, TL])
    w_mod = din("w_mod", [2, D, 6 * D]); hyb_w_in = din("hyb_w_in", [D, 2832]); hyb_w_out = din("hyb_w_out", [D, D])
    w_ffn_in = din("w_ffn_in", [2, D, 2 * FF]); w_ffn_out = din("w_ffn_out", [2, FF, D])
    conf_w_in = din("conf_w_in", [D, 2 * D]); conf_w_out = din("conf_w_out", [D, D])
    outT = nc.dram_tensor("outT", [D, NOWN], F32, kind="ExternalOutput").ap()
    tap_out = {}

    with ExitStack() as st:
        P = Prog(nc, st)
        op = P.op

        @contextmanager
        def scope():
            with ExitStack() as s_:
                yield s_
                P.barrier()

        nctr = [0]

        def sbuf(stack, name, shape, dt):
            nctr[0] += 1
            return Buf(stack.enter_context(nc.sbuf_tensor("s%d_%s" % (nctr[0], name), list(shape), dt)))

        pbw = [st.enter_context(nc.psum_tensor("pbw%d" % i, [128, 1024], F32)) for i in range(4)]
        pb = [pbw[i // 2][:, (i % 2) * 512:(i % 2) * 512 + 512] for i in range(8)]
        pk = [[Trk(psum=True) for _ in range(4)] for _ in range(8)]

        def pkr(i, c0, n):
            return [pk[i][0]]

        def tap(name, ap, trks, shape):
            if name not in taps:
                return
            t = nc.dram_tensor("tap_" + name, list(shape), F32, kind="ExternalOutput").ap()
            tap_out[name] = t
            P.dma("pool", t, ap, reads=trks)

        cst = sbuf(st, "cst", [128, NCONST], F32)
        vec = sbuf(st, "vec", [128, NV], F32)
        abc = sbuf(st, "abc", [128, 16], F32)
        cin_s = sbuf(st, "cin_s", [128, 16], F32)
        P.dma("sp", cst.t[:], cst_d[:, :], writes=[cst.k])
        P.dma("sp", vec.t[:], vecs_d[:, :], writes=[vec.k])
        P.dma("sp", abc.t[:], abc_d[:, :], writes=[abc.k])
        P.dma("sp", cin_s.t[:], cin[:, :], writes=[cin_s.k])
        cb = sbuf(st, "cb", [128, 3, 128], BF16)
        op("dve", lambda e: e.tensor_copy(out=cb.t[:, 0, :], in_=cst.t[:, C_ID:C_ID + 128]), reads=[cst.k], writes=[cb.k])
        op("dve", lambda e: e.tensor_copy(out=cb.t[:, 1, :], in_=cst.t[:, C_ONES:C_ONES + 128]), reads=[cst.k], writes=[cb.k])
        op("dve", lambda e: e.tensor_copy(out=cb.t[:, 2, :], in_=cst.t[:, C_BLK:C_BLK + 128]), reads=[cst.k], writes=[cb.k])
        identb = cb.t[:, 0, :]; onesb = cb.t[:, 1, :]; blkb = cb.t[:, 2, :]
        identf = cst.t[:, C_ID:C_ID + 128]; onesf = cst.t[:, C_ONES:C_ONES + 128]

        modv = sbuf(st, "modv", [128, 2, 48, 2], F32)
        A1 = sbuf(st, "A1", [128, 2, 8, 2], F32)
        A2 = sbuf(st, "A2", [128, 2, 8, 2], F32)
        G1 = sbuf(st, "G1", [128, 2, 8], F32)
        G2 = sbuf(st, "G2", [128, 2, 8], F32)
        sqr = Ring([sbuf(st, "sq%d" % i, [128, 512], BF16) for i in range(2)])
        rstd_r = Ring([sbuf(st, "rstd%d" % i, [128, 512], F32) for i in range(2)])
        tmp_r = Ring([sbuf(st, "tmp%d" % i, [128, 512], F32) for i in range(3)])
        stat_banks = Ring([6, 7])

        def vcol(c):
            return vec.t[:, c:c + 1]

        def load_w(stack_buf, src2d, kc, eng="pool"):
            P.dma(eng, stack_buf.t[:, 0:kc, 0:src2d.shape[1]], src2d.rearrange("(k p) c -> p k c", p=128), writes=[stack_buf.k])

        def rms_rstd(srcs, N, dsz, ones_ap, eps=EPS, f32mm=False):
            bank = stat_banks.next()
            for i, (ap, tk) in enumerate(srcs):
                sq = sqr.next()
                op("act", lambda e, sq=sq, ap=ap: e.activation(out=sq.t[:, 0:N], in_=ap, func=AF.Square), reads=tk, writes=[sq.k])
                op("pe", lambda e, sq=sq, i=i: e.matmul(pb[bank][:, 0:N], lhsT=ones_ap, rhs=sq.t[:, 0:N], start=(i == 0), stop=(i == len(srcs) - 1)),
                   reads=[sq.k, cb.k], writes=pkr(bank, 0, N))
            r = rstd_r.next()
            op("act", lambda e: e.activation(out=r.t[:, 0:N], in_=pb[bank][:, 0:N], func=AF.Ln, bias=eps, scale=1.0 / dsz), reads=pkr(bank, 0, N), writes=[r.k])
            op("act", lambda e: e.activation(out=r.t[:, 0:N], in_=r.t[:, 0:N], func=AF.Exp, scale=-0.5), reads=[r.k], writes=[r.k])
            return r

        with scope() as sa:
            scb = sbuf(sa, "scb", [128, 8, 2], BF16)
            op("act", lambda e: e.activation(out=scb.t[:].rearrange("p k j -> p (k j)"), in_=cin_s.t[:, :], func=AF.Silu), reads=[cin_s.k], writes=[scb.k])
            wmr = Ring([sbuf(sa, "wm%d" % i, [128, 8, 1536], BF16) for i in range(2)])
            for l in range(2):
                for blk in range(4):
                    wm = wmr.next()
                    load_w(wm, w_mod[l, :, blk * 1536:(blk + 1) * 1536], 8)
                    bank = blk % 2
                    for m in range(12):
                        for k in range(8):
                            op("pe", lambda e, wm=wm, m=m, k=k: e.matmul(pb[bank][:, 2 * m:2 * m + 2], lhsT=wm.t[:, k, m * 128:(m + 1) * 128], rhs=scb.t[:, k, :],
                                                                             start=(k == 0), stop=(k == 7)), reads=[wm.k, scb.k], writes=pkr(bank, 0, 24))
                    bcol = l * V_L + 32 + blk * 12
                    op("dve", lambda e, l=l, blk=blk, bcol=bcol: e.tensor_tensor(
                        out=modv.t[:, l, blk * 12:(blk + 1) * 12, :], in0=pb[bank][:, 0:24].rearrange("p (m j) -> p m j", j=2),
                        in1=vec.t[:, bcol:bcol + 12].unsqueeze(2).to_broadcast([128, 12, 2]), op=ALU.add),
                        reads=pkr(bank, 0, 24) + [vec.k], writes=[modv.k])
            for l in range(2):
                for (Ab, sc0, gcol) in ((A1, 8, l * V_L + 0), (A2, 32, l * V_L + 16)):
                    op("dve", lambda e, Ab=Ab, sc0=sc0, gcol=gcol, l=l: e.scalar_tensor_tensor(
                        out=Ab.t[:, l, :, :], in0=modv.t[:, l, sc0:sc0 + 8, :], scalar=1.0,
                        in1=vec.t[:, gcol:gcol + 8].unsqueeze(2).to_broadcast([128, 8, 2]), op0=ALU.add, op1=ALU.mult),
                        reads=[modv.k, vec.k], writes=[Ab.k])
                for (Gb, g0, gcol) in ((G1, 16, l * V_L + 8), (G2, 40, l * V_L + 24)):
                    op("dve", lambda e, Gb=Gb, g0=g0, gcol=gcol, l=l: e.tensor_tensor(
                        out=Gb.t[:, l, :], in0=modv.t[:, l, g0:g0 + 8, 0], in1=vec.t[:, gcol:gcol + 8], op=ALU.mult),
                        reads=[modv.k, vec.k], writes=[Gb.k])
            tap("mod", modv.t[:].rearrange("p l m j -> p (l m j)"), [modv.k], [128, 192])

        def norm_mod(srcs, N, outs, Ab, l, j, shift0):
            r = rms_rstd(srcs, N, D, onesb)
            for k in range(8):
                t = tmp_r.next()
                op("dve", lambda e, k=k, t=t: e.scalar_tensor_tensor(out=t.t[:, 0:N], in0=srcs[k][0], scalar=Ab.t[:, l, k, j:j + 1], in1=r.t[:, 0:N],
                                                                     op0=ALU.mult, op1=ALU.mult), reads=srcs[k][1] + [Ab.k, r.k], writes=[t.k])
                op("act", lambda e, k=k, t=t: e.activation(out=outs[k][0], in_=t.t[:, 0:N], func=AF.Identity, bias=modv.t[:, l, shift0 + k, j:j + 1], scale=1.0),
                   reads=[t.k, modv.k], writes=outs[k][1])

        mix = sbuf(st, "mix", [128, 8, NTP], BF16)

        with scope() as s0:
            aT = sbuf(s0, "aT", [128, 8, TF], BF16)
            with scope() as sb_:
                xs_r = Ring([sbuf(sb_, "xs%d" % i, [128, 8, 512], F32) for i in range(2)])
                for ti, (c0, N) in enumerate(FULL_TILES):
                    xs = xs_r.next()
                    src = ctxT[:, 0:TC] if ti == 0 else xT[:, c0 - TC:c0 - TC + N]
                    P.dma("sp", xs.t[:, :, 0:N], src.rearrange("(k p) n -> p k n", p=128), writes=[xs.k])
                    norm_mod([(xs.t[:, k, 0:N], [xs.k]) for k in range(8)], N,
                             [(aT.t[:, k, c0:c0 + N], [aT.tr(ti)]) for k in range(8)], A1, 0, 1 if ti == 0 else 0, 0)
            tap("aT", aT.t[:, :, 0:768], [aT.tr(0), aT.tr(1)], [128, 8, 768])
            if stop_after == "aT":
                P.finish()
                return nc, tap_out

            def aT_rhs(k, c0, N):
                trs = [aT.tr(ti) for ti, (t0, tn) in enumerate(FULL_TILES) if t0 < c0 + N and c0 < t0 + tn]
                return aT.t[:, k, c0:c0 + N], trs

            with scope() as sB:
                qbT = sbuf(sB, "qbT", [128, 4, NT], BF16)
                kbT = sbuf(sB, "kbT", [128, TF], BF16)
                vaug = sbuf(sB, "vaug", [128, NCH, 2, 128], BF16)
                wq = sbuf(sB, "wq", [128, 8, 512], BF16)
                wkv = sbuf(sB, "wkv", [128, 8, 256], BF16)
                rope_r = Ring([sbuf(sB, "rope%d" % i, [128, 2, 512], F32) for i in range(2)])
                kn_r = Ring([sbuf(sB, "kn%d" % i, [128, 512], F32) for i in range(2)])
                pt_r = Ring([sbuf(sB, "pt%d" % i, [128, 2, 512], BF16) for i in range(2)])
                rs_r = Ring([sbuf(sB, "rs%d" % i, [128, 512], F32) for i in range(2)])
                load_w(wq, hyb_w_in[:, 2064:2576], 8)
                load_w(wkv, hyb_w_in[:, 2576:2832], 8)
                op("pool", lambda e: e.memset(vaug.t[:, :, 0, 64:128], 1.0), writes=[vaug.k])
                op("pool", lambda e: e.memset(vaug.t[:, :, 1, 0:64], 1.0), writes=[vaug.k])
                mmb = Ring([0, 1, 2, 3])

                def qk_tile(wbuf, wc0, c0, N, gcol, use_rope, rope_c0, dst_ap, dst_trk):
                    bank = mmb.next()
                    for k in range(8):
                        rhs, trs = aT_rhs(k, c0, N)
                        op("pe", lambda e, k=k, rhs=rhs: e.matmul(pb[bank][:, 0:N], lhsT=wbuf.t[:, k, wc0:wc0 + 128], rhs=rhs, start=(k == 0), stop=(k == 7)),
                           reads=[wbuf.k] + trs, writes=pkr(bank, 0, N))
                    r = rms_rstd([(pb[bank][:, 0:N], pkr(bank, 0, N))], N, 64, blkb)
                    kn = kn_r.next()
                    if not use_rope:
                        op("dve", lambda e: e.scalar_tensor_tensor(out=dst_ap, in0=pb[bank][:, 0:N], scalar=vcol(gcol), in1=r.t[:, 0:N], op0=ALU.mult, op1=ALU.mult),
                           reads=pkr(bank, 0, N) + [vec.k, r.k], writes=dst_trk)
                        return
                    op("dve", lambda e: e.scalar_tensor_tensor(out=kn.t[:, 0:N], in0=pb[bank][:, 0:N], scalar=vcol(gcol), in1=r.t[:, 0:N], op0=ALU.mult, op1=ALU.mult),
                       reads=pkr(bank, 0, N) + [vec.k, r.k], writes=[kn.k])
                    rp = rope_r.next()
                    P.dma("sp", rp.t[:, :, 0:N], rope_d[:, :, rope_c0:rope_c0 + N].rearrange("c p n -> p c n"), writes=[rp.k])
                    b2 = mmb.next()
                    op("pe", lambda e: e.matmul(pb[b2][:, 0:N], lhsT=cst.t[:, C_ROT:C_ROT + 128], rhs=kn.t[:, 0:N], start=True, stop=True),
                       reads=[cst.k, kn.k], writes=pkr(b2, 0, N))
                    t1 = tmp_r.next(); t2 = tmp_r.next()
                    op("dve", lambda e: e.tensor_tensor(out=t1.t[:, 0:N], in0=kn.t[:, 0:N], in1=rp.t[:, 0, 0:N], op=ALU.mult), reads=[kn.k, rp.k], writes=[t1.k])
                    op("dve", lambda e: e.tensor_tensor(out=t2.t[:, 0:N], in0=pb[b2][:, 0:N], in1=rp.t[:, 1, 0:N], op=ALU.mult), reads=pkr(b2, 0, N) + [rp.k], writes=[t2.k])
                    op("dve", lambda e: e.tensor_tensor(out=dst_ap, in0=t1.t[:, 0:N], in1=t2.t[:, 0:N], op=ALU.add), reads=[t1.k, t2.k], writes=dst_trk)

                for ti, (c0, N) in enumerate(FULL_TILES):
                    qk_tile(wkv, 0, c0, N, V_KN, ti > 0, c0 - TC, kbT.t[:, c0:c0 + N], [kbT.k])
                for n in range(NCH):
                    bank = mmb.next()
                    for k in range(8):
                        lhs, trs = aT_rhs(k, n * 128, 128)
                        op("pe", lambda e, k=k, lhs=lhs: e.matmul(pb[bank][:, 0:128], lhsT=lhs, rhs=wkv.t[:, k, 128:256], start=(k == 0), stop=(k == 7)),
                           reads=[wkv.k] + trs, writes=pkr(bank, 0, 128))
                    op("act", lambda e, n=n: e.activation(out=vaug.t[:, n, 0, 0:64], in_=pb[bank][:, 0:64], func=AF.Copy), reads=pkr(bank, 0, 128), writes=[vaug.k])
                    op("act", lambda e, n=n: e.activation(out=vaug.t[:, n, 1, 64:128], in_=pb[bank][:, 64:128], func=AF.Copy), reads=pkr(bank, 0, 128), writes=[vaug.k])
                for c in range(4):
                    for (c0, N) in OWN_TILES:
                        qk_tile(wq, c * 128, TC + c0, N, V_QN, True, c0, qbT.t[:, c, c0:c0 + N], [qbT.tr(c)])
                tap("kbT", kbT.t[:, 0:1024], [kbT.k], [128, 1024])
                tap("qbT", qbT.t[:, 0, 0:512], [qbT.tr(0)], [128, 512])
                sbank = Ring([0, 1, 2, 3, 6, 7])
                obank = Ring([4, 5])
                LOOK = 2
                iters = [(c, c0, N, n) for c in range(4) for (c0, N) in OWN_TILES for n in range(NCH)]
                sb_of = {}
                obs = [4, 5]

                def issue_qk(i):
                    c, c0, N, n = iters[i]
                    bl = []
                    for hh in range(2):
                        sbk = sbank.next()
                        bl.append(sbk)
                        op("pe", lambda e: e.matmul(pb[sbk][:, 0:N], lhsT=kbT.t[64 * hh:64 * hh + 64, n * 128:(n + 1) * 128],
                                                    rhs=qbT.t[64 * hh:64 * hh + 64, c, c0:c0 + N], start=True, stop=True),
                           reads=[kbT.k, qbT.tr(c)], writes=pkr(sbk, 0, N))
                    sb_of[i] = bl

                for i in range(min(LOOK, len(iters))):
                    issue_qk(i)
                for i, (c, c0, N, n) in enumerate(iters):
                    bl = sb_of.pop(i)
                    assert bl[0] % 2 == 0 and bl[1] == bl[0] + 1
                    ptp = pt_r.next()
                    pair = pbw[bl[0] // 2][:, :].rearrange("p (b n) -> p b n", b=2)
                    op("act", lambda e: e.activation(out=ptp.t[:, :, 0:N], in_=pair[:, :, 0:N], func=AF.Exp, scale=0.125),
                       reads=pkr(bl[0], 0, N) + pkr(bl[1], 0, N), writes=[ptp.k])
                    if i + LOOK < len(iters):
                        issue_qk(i + LOOK)
                    for hh in range(2):
                        ob = obs[hh]
                        op("pe", lambda e: e.matmul(pb[ob][:, 0:N], lhsT=vaug.t[:, n, hh, :], rhs=ptp.t[:, hh, 0:N], start=(n == 0), stop=(n == NCH - 1)),
                           reads=[vaug.k, ptp.k], writes=pkr(ob, 0, N))
                    if n == NCH - 1:
                        for hh in range(2):
                            ob = obs[hh]
                            rs = rs_r.next()
                            lo, so = (0, 64) if hh == 0 else (64, 0)
                            op("dve", lambda e: e.reciprocal(out=rs.t[so:so + 64, 0:N], in_=pb[ob][so:so + 64, 0:N]), reads=pkr(ob, 0, N), writes=[rs.k])
                            op("dve", lambda e: e.tensor_tensor(out=mix.t[lo:lo + 64, 4 + c, c0:c0 + N], in0=pb[ob][lo:lo + 64, 0:N], in1=rs.t[so:so + 64, 0:N], op=ALU.mult),
                               reads=pkr(ob, 0, N) + [rs.k], writes=[mix.tr(4 + c)])
                tap("ob", mix.t[:, 4, 0:512], [mix.tr(4)], [128, 512])
            if stop_after == "attn":
                for k in range(4, 8):
                    P.dma("pool", outT[k * 128:(k + 1) * 128, :], mix.t[:, k, 0:NOWN], reads=[mix.tr(k)])
                P.finish()
                return nc, tap_out

            with scope() as sA:
                QK = sbuf(sA, "QK", [128, 2, TF], BF16)
                vT = sbuf(sA, "vT", [128, TF], BF16)
                PRE = TF + 8
                pre = sbuf(sA, "pre", [128, PRE], BF16)
                oT = sbuf(sA, "oT", [128, NT], F32)
                wh = Ring([sbuf(sA, "wh%d" % i, [128, 8, 128], BF16) for i in range(2)])
                wba = sbuf(sA, "wba", [128, 8, 16], BF16)
                acc_r = Ring([sbuf(sA, "acc%d" % i, [128, 512], F32) for i in range(1)])
                tb = {nm: sbuf(sA, "tb_" + nm, [128, 16 if nm == "ba" else 8, NCH], F32)
                      for nm in ("ba", "beta", "g", "cum", "gtot", "bec", "etail")}
                nexpA = sbuf(sA, "nexpA", [128, 8], F32)
                op("pool", lambda e: e.memset(pre.t[:, :], 0.0), writes=[pre.k])
                load_w(wba, hyb_w_in[:, 2048:2064], 8)
                for n in range(NCH):
                    bank = n % 4
                    for k in range(8):
                        lhs, trs = aT_rhs(k, n * 128, 128)
                        op("pe", lambda e: e.matmul(pb[bank][:, 0:16], lhsT=lhs, rhs=wba.t[:, k, :], start=(k == 0), stop=(k == 7)),
                           reads=[wba.k] + trs, writes=pkr(bank, 0, 16))
                    op("act", lambda e: e.activation(out=tb["ba"].t[:, :, n], in_=pb[bank][:, 0:16], func=AF.Copy), reads=pkr(bank, 0, 16), writes=[tb["ba"].k])
                op("act", lambda e: e.activation(out=tb["beta"].t[:, :, :], in_=tb["ba"].t[:, 0:8, :], func=AF.Sigmoid), reads=[tb["ba"].k], writes=[tb["beta"].k])
                op("dve", lambda e: e.tensor_tensor(out=tb["g"].t[:, :, :], in0=tb["ba"].t[:, 8:16, :], in1=abc.t[:, 8:16].unsqueeze(2).to_broadcast([128, 8, NCH]), op=ALU.add),
                   reads=[tb["ba"].k, abc.k], writes=[tb["g"].k])
                op("act", lambda e: e.activation(out=tb["g"].t[:, :, :], in_=tb["g"].t[:, :, :], func=AF.Exp), reads=[tb["g"].k], writes=[tb["g"].k])
                op("act", lambda e: e.activation(out=tb["g"].t[:, :, :], in_=tb["g"].t[:, :, :], func=AF.Ln, bias=1.0), reads=[tb["g"].k], writes=[tb["g"].k])
                op("act", lambda e: e.activation(out=nexpA.t[:, :], in_=abc.t[:, 0:8], func=AF.Exp), reads=[abc.k], writes=[nexpA.k])
                op("dve", lambda e: e.tensor_scalar(out=nexpA.t[:, :], in0=nexpA.t[:, :], scalar1=-1.0, scalar2=None, op0=ALU.mult), reads=[nexpA.k], writes=[nexpA.k])
                op("dve", lambda e: e.tensor_tensor(out=tb["g"].t[:, :, :], in0=tb["g"].t[:, :, :], in1=nexpA.t[:, :].unsqueeze(2).to_broadcast([128, 8, NCH]), op=ALU.mult),
                   reads=[tb["g"].k, nexpA.k], writes=[tb["g"].k])
                gflat = tb["g"].t[:].rearrange("p j n -> p (j n)")
                NJ = 4 * NCH
                op("pe", lambda e: e.matmul(pb[0][:, 0:NJ], lhsT=cst.t[:, C_TRIA:C_TRIA + 128], rhs=gflat[:, 0:NJ], start=True, stop=True), reads=[cst.k, tb["g"].k], writes=pkr(0, 0, NJ))
                op("pe", lambda e: e.matmul(pb[1][:, 0:NJ], lhsT=cst.t[:, C_TRID:C_TRID + 128], rhs=gflat[:, NJ:2 * NJ], start=True, stop=True), reads=[cst.k, tb["g"].k], writes=pkr(1, 0, NJ))
                op("pe", lambda e: e.matmul(pb[2][:, 0:2 * NJ], lhsT=onesf, rhs=gflat[:, 0:2 * NJ], start=True, stop=True), reads=[cst.k, tb["g"].k], writes=pkr(2, 0, 2 * NJ))
                cflat = tb["cum"].t[:].rearrange("p j n -> p (j n)")
                op("act", lambda e: e.activation(out=cflat[:, 0:NJ], in_=pb[0][:, 0:NJ], func=AF.Copy), reads=pkr(0, 0, NJ), writes=[tb["cum"].k])
                op("act", lambda e: e.activation(out=cflat[:, NJ:2 * NJ], in_=pb[1][:, 0:NJ], func=AF.Copy), reads=pkr(1, 0, NJ), writes=[tb["cum"].k])
                op("act", lambda e: e.activation(out=tb["gtot"].t[:].rearrange("p j n -> p (j n)"), in_=pb[2][:, 0:2 * NJ], func=AF.Copy), reads=pkr(2, 0, 2 * NJ), writes=[tb["gtot"].k])
                op("dve", lambda e: e.tensor_tensor(out=tb["etail"].t[:, :, :], in0=tb["gtot"].t[:, :, :], in1=tb["cum"].t[:, :, :], op=ALU.subtract), reads=[tb["gtot"].k, tb["cum"].k], writes=[tb["etail"].k])
                op("act", lambda e: e.activation(out=tb["etail"].t[:, :, :], in_=tb["etail"].t[:, :, :], func=AF.Exp), reads=[tb["etail"].k], writes=[tb["etail"].k])
                op("act", lambda e: e.activation(out=tb["gtot"].t[:, :, :], in_=tb["gtot"].t[:, :, :], func=AF.Exp), reads=[tb["gtot"].k], writes=[tb["gtot"].k])
                tb["egl"] = tb["gtot"]
                op("act", lambda e: e.activation(out=tb["bec"].t[:, :, :], in_=tb["cum"].t[:, :, :], func=AF.Exp), reads=[tb["cum"].k], writes=[tb["bec"].k])
                op("dve", lambda e: e.tensor_tensor(out=tb["bec"].t[:, :, :], in0=tb["bec"].t[:, :, :], in1=tb["beta"].t[:, :, :], op=ALU.mult), reads=[tb["bec"].k, tb["beta"].k], writes=[tb["bec"].k])
                tap("beta", tb["beta"].t[:, :, :], [tb["beta"].k], [128, 8, NCH])
                tap("g", tb["g"].t[:, :, :], [tb["g"].k], [128, 8, NCH])
                if stop_after == "dtab":
                    P.finish()
                    return nc, tap_out

                PAR_NAMES = ("M", "L", "Ld", "iT", "qd", "vb", "kbg", "ktl", "Qb", "wTn")
                US = []
                for si in range(2):
                    u = {}
                    for nm, shp, dt in (("dg", [128, 256], F32), ("Es", [128, 128], F32), ("G", [128, 128], F32),
                                        ("M", [128, 128], F32), ("L", [128, 128], F32), ("LM0", [128, 256], F32), ("LM1", [128, 256], F32),
                                        ("Q", [128, 128], F32), ("Ld", [128, 128], F32), ("Xo", [128, 128], F32), ("Qt", [128, 128], F32), ("Ei", [128, 128], F32), ("eR", [128, 128], F32),
                                        ("Qb", [128, 128], BF16), ("vb", [128, 128], BF16), ("kbg", [128, 128], BF16), ("ktl", [128, 128], BF16),
                                        ("wTn", [128, 128], BF16), ("vn", [128, 128], BF16), ("qd", [128, 128], BF16), ("iT", [128, 128], BF16),
                                        ("S", [128, 128], F32), ("Sb", [128, 128], BF16)):
                        if nm in PAR_NAMES:
                            for p_ in range(2):
                                u["%s@%d" % (nm, p_)] = sbuf(sA, "u%d_%s_%d" % (si, nm, p_), shp, dt)
                        else:
                            u[nm] = sbuf(sA, "u%d_%s" % (si, nm), shp, dt)
                    u["banks"] = (0, 1, 2, 3) if si == 0 else (4, 5, 6, 7)
                    US.append(u)

                def unit(h, d, n, need_out, written, par=0):
                    u = dict(US[d])
                    for nm_ in PAR_NAMES:
                        u[nm_] = US[d]["%s@%d" % (nm_, par)]
                    bX, bY, bZ, bW = u["banks"]
                    j = d * 4 + h
                    col = lambda nm: tb[nm].t[:, j, n:n + 1]
                    cs = slice(n * 128, (n + 1) * 128)
                    negs = cst.t[:, C_NSA:C_NSA + 128] if d == 0 else cst.t[:, C_NSD:C_NSD + 128]
                    op("dve", lambda e: e.tensor_scalar(out=u["dg"].t[:, 0:128], in0=identf, scalar1=col("beta"), scalar2=None, op0=ALU.mult), reads=[cst.k, tb["beta"].k], writes=[u["dg"].k])
                    yield
                    op("dve", lambda e: e.tensor_scalar(out=u["dg"].t[:, 128:256], in0=identf, scalar1=col("cum"), scalar2=None, op0=ALU.mult), reads=[cst.k, tb["cum"].k], writes=[u["dg"].k])
                    yield
                    op("pe", lambda e: e.matmul(pb[bX][:, 0:256], lhsT=onesf, rhs=u["dg"].t[:, 0:256], start=True, stop=True), reads=[cst.k, u["dg"].k], writes=pkr(bX, 0, 256))
                    yield
                    op("pe", lambda e: e.matmul(pb[bX][:, 256:512].rearrange("p (a b) -> p a b", a=2), lhsT=QK.t[:, 0, cs], rhs=QK.t[:, :, cs], start=True, stop=True),
                       reads=[QK.k], writes=pkr(bX, 256, 256))
                    yield
                    op("dve", lambda e: e.scalar_tensor_tensor(out=u["Es"].t[:, :], in0=pb[bX][:, 128:256], scalar=col("cum"), in1=negs, op0=ALU.subtract, op1=ALU.add),
                       reads=pkr(bX, 128, 128) + [tb["cum"].k, cst.k], writes=[u["Es"].k])
                    yield
                    op("act", lambda e: e.activation(out=u["Es"].t[:, :], in_=u["Es"].t[:, :], func=AF.Exp), reads=[u["Es"].k], writes=[u["Es"].k])
                    yield
                    op("dve", lambda e: e.tensor_tensor(out=u["G"].t[:, :], in0=pb[bX][:, 0:128], in1=u["Es"].t[:, :], op=ALU.mult), reads=pkr(bX, 0, 128) + [u["Es"].k], writes=[u["G"].k])
                    yield
                    op("dve", lambda e: e.tensor_tensor(out=u["M"].t[:, :], in0=pb[bX][:, 256:384], in1=u["G"].t[:, :], op=ALU.mult), reads=pkr(bX, 256, 128) + [u["G"].k], writes=[u["M"].k])
                    yield
                    if need_out:
                        op("pool", lambda e: e.tensor_tensor(out=u["Ei"].t[:, :], in0=u["Es"].t[:, :], in1=identf, op=ALU.add), reads=[u["Es"].k, cst.k], writes=[u["Ei"].k])
                        yield
                        op("dve", lambda e: e.tensor_tensor(out=u["iT"].t[:, :], in0=pb[bX][:, 384:512], in1=u["Ei"].t[:, :], op=ALU.mult), reads=pkr(bX, 384, 128) + [u["Ei"].k], writes=[u["iT"].k])
                        yield
                        op("act", lambda e: e.activation(out=u["eR"].t[:, :], in_=pb[bX][:, 128:256], func=AF.Exp), reads=pkr(bX, 128, 128), writes=[u["eR"].k])
                        yield
                        op("dve", lambda e: e.tensor_tensor(out=u["qd"].t[:, :], in0=QK.t[:, 1, cs], in1=u["eR"].t[:, :], op=ALU.mult), reads=[QK.k, u["eR"].k], writes=[u["qd"].k])
                        yield
                    if UNIT_CUT == 1:
                        return
                    negsT = cst.t[:, C_NSD:C_NSD + 128] if d == 0 else cst.t[:, C_NSA:C_NSA + 128]
                    op("dve", lambda e: e.scalar_tensor_tensor(out=u["L"].t[:, :], in0=pb[bX][:, 128:256], scalar=-1.0, in1=negsT, op0=ALU.mult, op1=ALU.add),
                       reads=pkr(bX, 128, 128) + [cst.k], writes=[u["L"].k])
                    yield
                    op("act", lambda e: e.activation(out=u["L"].t[:, :], in_=u["L"].t[:, :], func=AF.Exp, bias=col("cum"), scale=1.0), reads=[u["L"].k, tb["cum"].k], writes=[u["L"].k])
                    yield
                    op("dve", lambda e: e.scalar_tensor_tensor(out=u["L"].t[:, :], in0=u["L"].t[:, :], scalar=col("beta"), in1=pb[bX][:, 256:384], op0=ALU.mult, op1=ALU.mult),
                       reads=[u["L"].k, tb["beta"].k] + pkr(bX, 256, 128), writes=[u["L"].k])
                    yield
                    blkf = cst.t[:, C_BLK:C_BLK + 128]
                    op("dve", lambda e: e.tensor_tensor(out=u["Ld"].t[:, :], in0=u["L"].t[:, :], in1=blkf, op=ALU.mult), reads=[u["L"].k, cst.k], writes=[u["Ld"].k])
                    yield
                    op("dve", lambda e: e.tensor_tensor(out=u["L"].t[:, :], in0=u["L"].t[:, :], in1=u["Ld"].t[:, :], op=ALU.subtract), reads=[u["L"].k, u["Ld"].k], writes=[u["L"].k])
                    yield
                    op("dve", lambda e: e.tensor_tensor(out=u["M"].t[:, :], in0=u["M"].t[:, :], in1=blkf, op=ALU.mult), reads=[u["M"].k, cst.k], writes=[u["M"].k])
                    yield
                    pbf = pb[bZ][:, 0:128].bitcast(BF16)
                    op("pe", lambda e: e.transpose(pbf[:, 0:128], QK.t[:, 0, cs], identb), reads=[QK.k, cb.k], writes=pkr(bZ, 0, 128))
                    yield
                    op("pe", lambda e: e.transpose(pbf[:, 128:256], vT.t[:, cs], identb), reads=[vT.k, cb.k], writes=pkr(bZ, 0, 128))
                    yield
                    op("act", lambda e: e.activation(out=u["vb"].t[:, :], in_=pbf[:, 128:256], func=AF.Copy, scale=col("beta")), reads=pkr(bZ, 0, 128) + [tb["beta"].k], writes=[u["vb"].k])
                    yield
                    op("dve", lambda e: e.tensor_scalar(out=u["kbg"].t[:, :], in0=pbf[:, 0:128], scalar1=col("bec"), scalar2=None, op0=ALU.mult), reads=pkr(bZ, 0, 128) + [tb["bec"].k], writes=[u["kbg"].k])
                    yield
                    op("dve", lambda e: e.tensor_scalar(out=u["ktl"].t[:, :], in0=pbf[:, 0:128], scalar1=col("etail"), scalar2=None, op0=ALU.mult), reads=pkr(bZ, 0, 128) + [tb["etail"].k], writes=[u["ktl"].k])
                    yield
                    yield "S2"
                    op("pool", lambda e: e.tensor_tensor(out=u["Q"].t[:, :], in0=identf, in1=u["M"].t[:, :], op=ALU.subtract), reads=[cst.k, u["M"].k], writes=[u["Q"].k])
                    yield
                    Lk, Lkt, Mk, Mkt = u["Ld"].t[:, :], [u["Ld"].k], u["M"].t[:, :], [u["M"].k]
                    for lv in range(5):
                        LM = u["LM%d" % (lv % 2)]
                        last = lv == 4
                        op("pe", lambda e: e.matmul(pb[bY][:, 128:256], lhsT=Mk, rhs=Lk, start=True, stop=True), reads=Lkt + Mkt, writes=pkr(bY, 128, 128))
                        yield
                        if not last:
                            op("pe", lambda e: e.matmul(pb[bY][:, 256:384], lhsT=Lk, rhs=Mk, start=True, stop=True), reads=Lkt + Mkt, writes=pkr(bY, 256, 128))
                            yield
                        w_ = 128 if last else 256
                        op("act", lambda e: e.activation(out=LM.t[:, 0:w_], in_=pb[bY][:, 128:128 + w_], func=AF.Copy), reads=pkr(bY, 128, w_), writes=[LM.k])
                        yield
                        op("pe", lambda e: e.matmul(pb[bW][:, 128:256], lhsT=LM.t[:, 0:128], rhs=u["Q"].t[:, :], start=True, stop=True), reads=[LM.k, u["Q"].k], writes=pkr(bW, 128, 128))
                        yield
                        op("dve", lambda e: e.tensor_tensor(out=u["Q"].t[:, :], in0=pb[bW][:, 128:256], in1=u["Q"].t[:, :], op=ALU.add), reads=pkr(bW, 128, 128) + [u["Q"].k], writes=[u["Q"].k])
                        yield
                        Lk, Lkt, Mk, Mkt = LM.t[:, 0:128], [LM.k], LM.t[:, 128:256], [LM.k]
                    op("pe", lambda e: e.matmul(pb[bY][:, 128:256], lhsT=u["Q"].t[:, :], rhs=identf, start=True, stop=True), reads=[u["Q"].k, cst.k], writes=pkr(bY, 128, 128))
                    yield
                    op("act", lambda e: e.activation(out=u["Qt"].t[:, :], in_=pb[bY][:, 128:256], func=AF.Copy), reads=pkr(bY, 128, 128), writes=[u["Qt"].k])
                    yield
                    op("pe", lambda e: e.matmul(pb[bW][:, 128:256], lhsT=u["L"].t[:, :], rhs=u["Q"].t[:, :], start=True, stop=True), reads=[u["L"].k, u["Q"].k], writes=pkr(bW, 128, 128))
                    yield
                    op("dve", lambda e: e.tensor_copy(out=u["Xo"].t[:, :], in_=pb[bW][:, 128:256]), reads=pkr(bW, 128, 128), writes=[u["Xo"].k])
                    yield
                    op("pe", lambda e: e.matmul(pb[bY][:, 128:256], lhsT=u["Qt"].t[:, :], rhs=u["Xo"].t[:, :], start=True, stop=True), reads=[u["Qt"].k, u["Xo"].k], writes=pkr(bY, 128, 128))
                    yield
                    op("dve", lambda e: e.tensor_tensor(out=u["Qb"].t[:, :], in0=u["Q"].t[:, :], in1=pb[bY][:, 128:256], op=ALU.subtract), reads=[u["Q"].k] + pkr(bY, 128, 128), writes=[u["Qb"].k])
                    yield
                    if UNIT_CUT == 2:
                        return
                    op("pe", lambda e: e.matmul(pb[bZ][:, 128:256], lhsT=u["kbg"].t[:, :], rhs=u["Qb"].t[:, :], start=True, stop=True), reads=[u["kbg"].k, u["Qb"].k], writes=pkr(bZ, 128, 128))
                    yield
                    op("act", lambda e: e.activation(out=u["wTn"].t[:, :], in_=pb[bZ][:, 128:256], func=AF.Copy, scale=-1.0), reads=pkr(bZ, 128, 128), writes=[u["wTn"].k])
                    yield
                    if UNIT_CUT == 3:
                        return
                    yield "S3"
                    op("pe", lambda e: e.matmul(pb[bZ][:, 256:384], lhsT=u["Qb"].t[:, :], rhs=u["vb"].t[:, :], start=True, stop=False), reads=[u["Qb"].k, u["vb"].k], writes=pkr(bZ, 256, 128))
                    yield
                    op("pe", lambda e: e.matmul(pb[bZ][:, 256:384], lhsT=u["wTn"].t[:, :], rhs=u["Sb"].t[:, :], start=False, stop=True), reads=[u["wTn"].k, u["Sb"].k], writes=pkr(bZ, 256, 128))
                    yield
                    op("act", lambda e: e.activation(out=u["vn"].t[:, :], in_=pb[bZ][:, 256:384], func=AF.Copy), reads=pkr(bZ, 256, 128), writes=[u["vn"].k])
                    yield
                    if need_out:
                        t0 = (n - 2) * 128
                        op("pe", lambda e: e.matmul(pb[bZ][:, 384:512], lhsT=u["Sb"].t[:, :], rhs=u["qd"].t[:, :], start=True, stop=False), reads=[u["Sb"].k, u["qd"].k], writes=pkr(bZ, 384, 128))
                        yield
                        op("pe", lambda e: e.matmul(pb[bZ][:, 384:512], lhsT=u["vn"].t[:, :], rhs=u["iT"].t[:, :], start=False, stop=True), reads=[u["vn"].k, u["iT"].k], writes=pkr(bZ, 384, 128))
                        yield
                        if n not in written:
                            op("act", lambda e: e.activation(out=oT.t[:, t0:t0 + 128], in_=pb[bZ][:, 384:512], func=AF.Copy), reads=pkr(bZ, 384, 128), writes=[oT.tr(n)])
                            yield
                            written.add(n)
                        else:
                            op("dve", lambda e: e.tensor_tensor(out=oT.t[:, t0:t0 + 128], in0=pb[bZ][:, 384:512], in1=oT.t[:, t0:t0 + 128], op=ALU.add), reads=pkr(bZ, 384, 128) + [oT.tr(n)], writes=[oT.tr(n)])
                            yield
                    op("pe", lambda e: e.matmul(pb[bW][:, 0:128], lhsT=u["ktl"].t[:, :], rhs=u["vn"].t[:, :], start=True, stop=True), reads=[u["ktl"].k, u["vn"].k], writes=pkr(bW, 0, 128))
                    yield
                    op("dve", lambda e: e.scalar_tensor_tensor(out=u["S"].t[:, :], in0=u["S"].t[:, :], scalar=col("egl"), in1=pb[bW][:, 0:128], op0=ALU.mult, op1=ALU.add),
                       reads=[u["S"].k, tb["egl"].k] + pkr(bW, 0, 128), writes=[u["S"].k])
                    yield
                    op("act", lambda e: e.activation(out=u["Sb"].t[:, :], in_=u["S"].t[:, :], func=AF.Copy), reads=[u["S"].k], writes=[u["Sb"].k])
                    yield

                NOUT = NT // 128
                asc_order = [0, 1] + list(range(2, 2 + NOUT))
                desc_order = [1, 0] + list(range(NCH - 1, 1, -1))
                for h in range(4):
                    for si, (wc0, dst) in enumerate(((h * 128, "q"), (512 + h * 128, "k"), (1024 + h * 128, "v"))):
                        w_ = wh.next()
                        load_w(w_, hyb_w_in[:, wc0:wc0 + 128], 8)
                        for ti, (c0, N) in enumerate(FULL_TILES):
                            bank = ti % 4
                            for k in range(8):
                                rhs, trs = aT_rhs(k, c0, N)
                                op("pe", lambda e: e.matmul(pb[bank][:, 0:N], lhsT=w_.t[:, k, :], rhs=rhs, start=(k == 0), stop=(k == 7)), reads=[w_.k] + trs, writes=pkr(bank, 0, N))
                            po = 2 if ti == 0 else 6 + c0
                            op("act", lambda e: e.activation(out=pre.t[:, po:po + N], in_=pb[bank][:, 0:N], func=AF.Copy), reads=pkr(bank, 0, N), writes=[pre.tr(ti)])
                        ch = {"q": 0, "k": 4, "v": 8}[dst] + h
                        for ti, (c0, N) in enumerate(FULL_TILES):
                            po = 0 if ti == 0 else 4 + c0
                            acc = acc_r.next()
                            eng = "dve"
                            ptr = [pre.k] + [pre.tr(t_) for t_ in (ti - 1, ti, ti + 1) if 0 <= t_ < len(FULL_TILES)]
                            op(eng, lambda e: e.tensor_scalar(out=acc.t[:, 0:N], in0=pre.t[:, po:po + N], scalar1=vcol(V_HCONV + ch), scalar2=None, op0=ALU.mult), reads=ptr + [vec.k], writes=[acc.k])
                            for jt in range(1, 5):
                                op(eng, lambda e: e.scalar_tensor_tensor(out=acc.t[:, 0:N], in0=pre.t[:, po + jt:po + jt + N], scalar=vcol(V_HCONV + jt * 12 + ch), in1=acc.t[:, 0:N], op0=ALU.mult, op1=ALU.add),
                                   reads=ptr + [vec.k, acc.k], writes=[acc.k])
                            if dst == "v":
                                op("act", lambda e: e.activation(out=vT.t[:, c0:c0 + N], in_=acc.t[:, 0:N], func=AF.Silu), reads=[acc.k], writes=[vT.k])
                            else:
                                op("act", lambda e: e.activation(out=acc.t[:, 0:N], in_=acc.t[:, 0:N], func=AF.Silu), reads=[acc.k], writes=[acc.k])
                                r = rms_rstd([(acc.t[:, 0:N], [acc.k])], N, 1.0, onesb)
                                sc_ = (128.0 ** -0.5) if dst == "q" else 1.0
                                op("dve", lambda e: e.scalar_tensor_tensor(out=QK.t[:, 1 if dst == "q" else 0, c0:c0 + N], in0=acc.t[:, 0:N], scalar=sc_, in1=r.t[:, 0:N], op0=ALU.mult, op1=ALU.mult),
                                   reads=[acc.k, r.k], writes=[QK.k])
                    if h == 0:
                        tap("qT", QK.t[:, 1, 0:768], [QK.k], [128, 768])
                        tap("kT", QK.t[:, 0, 0:768], [QK.k], [128, 768])
                        tap("vT", vT.t[:, 0:768], [vT.k], [128, 768])
                        if stop_after == "dproj":
                            P.finish()
                            return nc, tap_out
                    for d in range(2):
                        op("pool", lambda e: e.memset(US[d]["S"].t[:, :], 0.0), writes=[US[d]["S"].k])
                        op("pool", lambda e: e.memset(US[d]["Sb"].t[:, :], 0.0), writes=[US[d]["Sb"].k])
                    written = set()
                    if stop_after == "dunit":
                        for _ in unit(h, 0, 0, False, written):
                            pass
                        for _ in unit(h, 1, 3, True, written):
                            pass
                        tap("M", US[0]["M"].t[:, :], [US[0]["M"].k], [128, 128])
                        tap("Q", US[0]["Q"].t[:, :], [US[0]["Q"].k], [128, 128])
                        P.finish()
                        return nc, tap_out
                    orders = [asc_order, desc_order]
                    nxt = [0, 0]
                    act = [[], []]
                    while act[0] or act[1] or nxt[0] < len(orders[0]) or nxt[1] < len(orders[1]):
                        for d_ in (0, 1):
                            if nxt[d_] < len(orders[d_]) and len(act[d_]) < 2 and (not act[d_] or act[d_][-1][1] >= 2 or act[d_][-1][2] is not None):
                                i_ = nxt[d_]
                                n_ = orders[d_][i_]
                                nxt[d_] += 1
                                need_ = (n_ >= 2) if d_ == 0 else (2 <= n_ < 2 + NOUT)
                                act[d_].append([unit(h, d_, n_, need_, written, i_ % 2), 1, None])
                            for ent in list(act[d_]):
                                if ent[2] == "S2":
                                    if act[d_][0] is ent or act[d_][0][1] == 3:
                                        ent[1], ent[2] = 2, None
                                    else:
                                        continue
                                elif ent[2] == "S3":
                                    if act[d_][0] is ent:
                                        ent[1], ent[2] = 3, None
                                    else:
                                        continue
                                try:
                                    r_ = next(ent[0])
                                except StopIteration:
                                    act[d_].remove(ent)
                                    continue
                                if r_ in ("S2", "S3"):
                                    ent[2] = r_
                    if h == 0:
                        tap("oT", oT.t[:, 0:512], [oT.tr(n_) for n_ in range(2, 6)], [128, 512])
                    wz = wh.next()
                    load_w(wz, hyb_w_in[:, 1536 + h * 128:1536 + (h + 1) * 128], 8)
                    for (c0, N) in OWN_TILES:
                        otr = [oT.tr(n_) for n_ in range(2 + c0 // 128, 2 + (c0 + N) // 128)]
                        r = rms_rstd([(oT.t[:, c0:c0 + N], otr)], N, 128.0, onesb)
                        bank = 0
                        for k in range(8):
                            rhs, trs = aT_rhs(k, TC + c0, N)
                            op("pe", lambda e: e.matmul(pb[bank][:, 0:N], lhsT=wz.t[:, k, :], rhs=rhs, start=(k == 0), stop=(k == 7)), reads=[wz.k] + trs, writes=pkr(bank, 0, N))
                        t1 = tmp_r.next(); t2 = tmp_r.next()
                        op("act", lambda e: e.activation(out=t1.t[:, 0:N], in_=pb[bank][:, 0:N], func=AF.Silu), reads=pkr(bank, 0, N), writes=[t1.k])
                        op("dve", lambda e: e.scalar_tensor_tensor(out=t2.t[:, 0:N], in0=oT.t[:, c0:c0 + N], scalar=vcol(V_ONORM), in1=r.t[:, 0:N], op0=ALU.mult, op1=ALU.mult),
                           reads=otr + [vec.k, r.k], writes=[t2.k])
                        op("dve", lambda e: e.tensor_tensor(out=mix.t[:, h, c0:c0 + N], in0=t2.t[:, 0:N], in1=t1.t[:, 0:N], op=ALU.mult), reads=[t1.k, t2.k], writes=[mix.tr(h)])
                tap("ya", mix.t[:, 0, 0:512], [mix.tr(0)], [128, 512])
            if stop_after == "delta":
                P.finish()
                return nc, tap_out

        h = sbuf(st, "h", [128, 8, NT], F32)
        for ti, (c0, N) in enumerate(OWN_TILES):
            P.dma("sp", h.t[:, :, c0:c0 + N], xT[:, c0:c0 + N].rearrange("(k p) n -> p k n", p=128), writes=[h.tr(ti)])
        yb_r = Ring([sbuf(st, "yb%d" % i, [128, 8, 512], F32) for i in range(2)])
        mmb2 = Ring([0, 1, 2, 3, 4, 5])

        def post_res(yb, N, l, Gb, ti, c0):
            r = rms_rstd([(yb.t[:, m, 0:N], [yb.k]) for m in range(8)], N, D, onesb)
            for m in range(8):
                t = tmp_r.next()
                op("dve", lambda e: e.scalar_tensor_tensor(out=t.t[:, 0:N], in0=yb.t[:, m, 0:N], scalar=Gb.t[:, l, m:m + 1], in1=r.t[:, 0:N], op0=ALU.mult, op1=ALU.mult),
                   reads=[yb.k, Gb.k, r.k], writes=[t.k])
                op("dve", lambda e: e.tensor_tensor(out=h.t[:, m, c0:c0 + N], in0=h.t[:, m, c0:c0 + N], in1=t.t[:, 0:N], op=ALU.add), reads=[t.k, h.tr(ti)], writes=[h.tr(ti)])

        with scope() as sF:
            wo = sbuf(sF, "wo", [128, 8, D], BF16)
            load_w(wo, hyb_w_out[:, :], 8)
            for ti, (c0, N) in enumerate(OWN_TILES):
                yb = yb_r.next()
                for m in range(8):
                    bank = mmb2.next()
                    for k in range(8):
                        op("pe", lambda e: e.matmul(pb[bank][:, 0:N], lhsT=wo.t[:, k, m * 128:(m + 1) * 128], rhs=mix.t[:, k, c0:c0 + N], start=(k == 0), stop=(k == 7)),
                           reads=[wo.k, mix.tr(k)], writes=pkr(bank, 0, N))
                    op("act", lambda e: e.activation(out=yb.t[:, m, 0:N], in_=pb[bank][:, 0:N], func=AF.Copy), reads=pkr(bank, 0, N), writes=[yb.k])
                if ti == 0:
                    tap("ylat", yb.t[:, 0, 0:512], [yb.k], [128, 512])
                post_res(yb, N, 0, G1, ti, c0)
        tap("hmix0", h.t[:, 0, 0:512], [h.tr(0)], [128, 512])
        if stop_after == "mix0":
            P.finish()
            return nc, tap_out

        def ffn(l, tiles):
            groups = [tiles[i:i + 2] for i in range(0, len(tiles), 2)]
            with scope() as sf:
                mixflat = mix.t[:].rearrange("p k n -> p (k n)")
                fx = sbuf(sf, "fx", [128, 13, 1024], BF16)
                ag_k = [Trk(), Trk()]
                hid_k = [[Trk() for _ in range(FFC)] for _ in range(2)]
                agv = lambda k, s_, N: fx.t[:, k, s_ * 512:s_ * 512 + N]

                def hidv(j, s_, N):
                    if j < 17:
                        return mixflat[:, j * 1024 + s_ * 512:j * 1024 + s_ * 512 + N]
                    return fx.t[:, 8 + j - 17, s_ * 512:s_ * 512 + N]
                wg_r = Ring([sbuf(sf, "wg%d" % i, [128, 8, 256], BF16) for i in range(2)])
                wu_r = Ring([sbuf(sf, "wu%d" % i, [128, 8, 256], BF16) for i in range(2)])
                wo_r = Ring([sbuf(sf, "wfo%d" % i, [128, FFC, 128], BF16) for i in range(2)])
                for grp in groups:
                    for s_, (ti, (c0, N)) in enumerate(grp):
                        norm_mod([(h.t[:, k, c0:c0 + N], [h.tr(ti)]) for k in range(8)], N,
                                 [(agv(k, s_, N), [ag_k[s_]]) for k in range(8)], A2, l, 0, 24)
                    for jb in range(FFC // 2):
                        wg = wg_r.next(); wu = wu_r.next()
                        load_w(wg, w_ffn_in[l, :, jb * 256:(jb + 1) * 256], 8)
                        load_w(wu, w_ffn_in[l, :, FF + jb * 256:FF + (jb + 1) * 256], 8)
                        for jj in range(2):
                            j = 2 * jb + jj
                            for s_, (ti, (c0, N)) in enumerate(grp):
                                bg = mmb2.next(); bu = mmb2.next()
                                for k in range(8):
                                    op("pe", lambda e: e.matmul(pb[bg][:, 0:N], lhsT=wg.t[:, k, jj * 128:(jj + 1) * 128], rhs=agv(k, s_, N), start=(k == 0), stop=(k == 7)),
                                       reads=[wg.k, ag_k[s_]], writes=pkr(bg, 0, N))
                                for k in range(8):
                                    op("pe", lambda e: e.matmul(pb[bu][:, 0:N], lhsT=wu.t[:, k, jj * 128:(jj + 1) * 128], rhs=agv(k, s_, N), start=(k == 0), stop=(k == 7)),
                                       reads=[wu.k, ag_k[s_]], writes=pkr(bu, 0, N))
                                t = tmp_r.next()
                                op("act", lambda e: e.activation(out=t.t[:, 0:N], in_=pb[bg][:, 0:N], func=AF.Silu), reads=pkr(bg, 0, N), writes=[t.k])
                                op("dve", lambda e: e.tensor_tensor(out=hidv(j, s_, N), in0=pb[bu][:, 0:N], in1=t.t[:, 0:N], op=ALU.mult), reads=pkr(bu, 0, N) + [t.k], writes=[hid_k[s_][j]])
                    ybs = [yb_r.next() for _ in grp]
                    for m in range(8):
                        wfo = wo_r.next()
                        P.dma("pool", wfo.t[:, :, :], w_ffn_out[l, :, m * 128:(m + 1) * 128].rearrange("(j p) c -> p j c", p=128), writes=[wfo.k])
                        for s_, (ti, (c0, N)) in enumerate(grp):
                            bank = mmb2.next()
                            for j in range(FFC):
                                op("pe", lambda e: e.matmul(pb[bank][:, 0:N], lhsT=wfo.t[:, j, :], rhs=hidv(j, s_, N), start=(j == 0), stop=(j == FFC - 1)),
                                   reads=[wfo.k, hid_k[s_][j]], writes=pkr(bank, 0, N))
                            op("act", lambda e: e.activation(out=ybs[s_].t[:, m, 0:N], in_=pb[bank][:, 0:N], func=AF.Copy), reads=pkr(bank, 0, N), writes=[ybs[s_].k])
                    for s_, (ti, (c0, N)) in enumerate(grp):
                        post_res(ybs[s_], N, l, G2, ti, c0)

        ffn(0, list(enumerate(OWN_TILES)))
        tap("hl0", h.t[:, 0, 0:512], [h.tr(0)], [128, 512])
        if stop_after == "l0":
            P.finish()
            return nc, tap_out

        ub = mix
        with scope() as sC1:
            a1 = sbuf(sC1, "a1", [128, 8, NT], BF16)
            wcv_r = Ring([sbuf(sC1, "wcv%d" % i, [128, 8, 128], BF16) for i in range(2)])
            wcg_r = Ring([sbuf(sC1, "wcg%d" % i, [128, 8, 128], BF16) for i in range(2)])
            for ti, (c0, N) in enumerate(OWN_TILES):
                norm_mod([(h.t[:, k, c0:c0 + N], [h.tr(ti)]) for k in range(8)], N,
                         [(a1.t[:, k, c0:c0 + N], [a1.tr(ti)]) for k in range(8)], A1, 1, 0, 0)
            op("pool", lambda e: e.memset(ub.t[:, :, 0:15], 0.0), writes=[ub.tr(k) for k in range(8)])
            for m in range(8):
                wcv = wcv_r.next(); wcg = wcg_r.next()
                load_w(wcv, conf_w_in[:, m * 128:(m + 1) * 128], 8)
                load_w(wcg, conf_w_in[:, D + m * 128:D + (m + 1) * 128], 8)
                for ti, (c0, N) in enumerate(OWN_TILES):
                    bv = mmb2.next(); bg = mmb2.next()
                    for k in range(8):
                        op("pe", lambda e: e.matmul(pb[bv][:, 0:N], lhsT=wcv.t[:, k, :], rhs=a1.t[:, k, c0:c0 + N], start=(k == 0), stop=(k == 7)), reads=[wcv.k, a1.tr(ti)], writes=pkr(bv, 0, N))
                    for k in range(8):
                        op("pe", lambda e: e.matmul(pb[bg][:, 0:N], lhsT=wcg.t[:, k, :], rhs=a1.t[:, k, c0:c0 + N], start=(k == 0), stop=(k == 7)), reads=[wcg.k, a1.tr(ti)], writes=pkr(bg, 0, N))
                    t = tmp_r.next()
                    op("act", lambda e: e.activation(out=t.t[:, 0:N], in_=pb[bg][:, 0:N], func=AF.Sigmoid, bias=vcol(V_CBIN + 8 + m), scale=1.0), reads=pkr(bg, 0, N) + [vec.k], writes=[t.k])
                    op("dve", lambda e: e.scalar_tensor_tensor(out=ub.t[:, m, 15 + c0:15 + c0 + N], in0=pb[bv][:, 0:N], scalar=vcol(V_CBIN + m), in1=t.t[:, 0:N], op0=ALU.add, op1=ALU.mult),
                       reads=pkr(bv, 0, N) + [vec.k, t.k], writes=[ub.tr(m)])
        with scope() as sC2:
            actb = sbuf(sC2, "actb", [128, 8, 512], BF16)
            wco = sbuf(sC2, "wco", [128, 8, D], BF16)
            dgm_r = Ring([sbuf(sC2, "dgm%d" % i, [128, 31, 128], BF16) for i in range(2)])
            lnt = [sbuf(sC2, "lnt%d" % i, [128, 512], F32) for i in range(3)]
            load_w(wco, conf_w_out[:, :], 8)
            for ti, (c0, N) in enumerate(OWN_TILES[:4]):
                cv = yb_r.next()
                for k in range(8):
                    dgm = dgm_r.next()
                    op("dve", lambda e: e.tensor_tensor(out=dgm.t[:, :, :], in0=identb.unsqueeze(1).to_broadcast([128, 31, 128]),
                                                        in1=vec.t[:, V_CDW + k * 31:V_CDW + (k + 1) * 31].unsqueeze(2).to_broadcast([128, 31, 128]), op=ALU.mult),
                       reads=[cb.k, vec.k], writes=[dgm.k])
                    bank = mmb2.next()
                    for jt in range(31):
                        op("pe", lambda e: e.matmul(pb[bank][:, 0:N], lhsT=dgm.t[:, jt, :], rhs=ub.t[:, k, c0 + jt:c0 + jt + N], start=(jt == 0), stop=(jt == 30)),
                           reads=[dgm.k, ub.tr(k)], writes=pkr(bank, 0, N))
                    op("act", lambda e: e.activation(out=cv.t[:, k, 0:N], in_=pb[bank][:, 0:N], func=AF.Identity, bias=vcol(V_CDWB + k), scale=1.0), reads=pkr(bank, 0, N) + [vec.k], writes=[cv.k])
                b1 = stat_banks.next(); b2 = stat_banks.next()
                for k in range(8):
                    op("pe", lambda e: e.matmul(pb[b1][:, 0:N], lhsT=onesf, rhs=cv.t[:, k, 0:N], start=(k == 0), stop=(k == 7)), reads=[cst.k, cv.k], writes=pkr(b1, 0, N))
                for k in range(8):
                    sq = sqr.next()
                    op("act", lambda e: e.activation(out=sq.t[:, 0:N], in_=cv.t[:, k, 0:N], func=AF.Square), reads=[cv.k], writes=[sq.k])
                    op("pe", lambda e: e.matmul(pb[b2][:, 0:N], lhsT=onesb, rhs=sq.t[:, 0:N], start=(k == 0), stop=(k == 7)), reads=[sq.k, cb.k], writes=pkr(b2, 0, N))
                mean, msq, rs = lnt
                op("act", lambda e: e.activation(out=mean.t[:, 0:N], in_=pb[b1][:, 0:N], func=AF.Copy, scale=1.0 / D), reads=pkr(b1, 0, N), writes=[mean.k])
                op("dve", lambda e: e.tensor_tensor(out=msq.t[:, 0:N], in0=mean.t[:, 0:N], in1=mean.t[:, 0:N], op=ALU.mult), reads=[mean.k], writes=[msq.k])
                op("dve", lambda e: e.scalar_tensor_tensor(out=rs.t[:, 0:N], in0=pb[b2][:, 0:N], scalar=1.0 / D, in1=msq.t[:, 0:N], op0=ALU.mult, op1=ALU.subtract),
                   reads=pkr(b2, 0, N) + [msq.k], writes=[rs.k])
                op("act", lambda e: e.activation(out=rs.t[:, 0:N], in_=rs.t[:, 0:N], func=AF.Ln, bias=EPS, scale=1.0), reads=[rs.k], writes=[rs.k])
                op("act", lambda e: e.activation(out=rs.t[:, 0:N], in_=rs.t[:, 0:N], func=AF.Exp, scale=-0.5), reads=[rs.k], writes=[rs.k])
                for k in range(8):
                    t1 = tmp_r.next(); t2 = tmp_r.next()
                    op("dve", lambda e: e.tensor_tensor(out=t1.t[:, 0:N], in0=cv.t[:, k, 0:N], in1=mean.t[:, 0:N], op=ALU.subtract), reads=[cv.k, mean.k], writes=[t1.k])
                    op("dve", lambda e: e.scalar_tensor_tensor(out=t2.t[:, 0:N], in0=t1.t[:, 0:N], scalar=vcol(V_CLNG + k), in1=rs.t[:, 0:N], op0=ALU.mult, op1=ALU.mult), reads=[t1.k, vec.k, rs.k], writes=[t2.k])
                    op("act", lambda e: e.activation(out=actb.t[:, k, 0:N], in_=t2.t[:, 0:N], func=AF.Silu, bias=vcol(V_CLNB + k), scale=1.0), reads=[t2.k, vec.k], writes=[actb.k])
                yb = cv
                for m in range(8):
                    bank = mmb2.next()
                    for k in range(8):
                        op("pe", lambda e: e.matmul(pb[bank][:, 0:N], lhsT=wco.t[:, k, m * 128:(m + 1) * 128], rhs=actb.t[:, k, 0:N], start=(k == 0), stop=(k == 7)), reads=[wco.k, actb.k], writes=pkr(bank, 0, N))
                    op("act", lambda e: e.activation(out=yb.t[:, m, 0:N], in_=pb[bank][:, 0:N], func=AF.Identity, bias=vcol(V_CBOUT + m), scale=1.0), reads=pkr(bank, 0, N) + [vec.k], writes=[yb.k])
                if ti == 0:
                    tap("yconf", yb.t[:, 0, 0:512], [yb.k], [128, 512])
                post_res(yb, N, 1, G1, ti, c0)
        tap("hmix1", h.t[:, 0, 0:512], [h.tr(0)], [128, 512])
        if stop_after == "mix1":
            P.finish()
            return nc, tap_out

        ffn(1, list(enumerate(OWN_TILES[:4])))
        for ti, (c0, N) in enumerate(OWN_TILES[:4]):
            P.dma("sp", outT[:, c0:c0 + N].rearrange("(k p) n -> p k n", p=128), h.t[:, :, c0:c0 + N], reads=[h.tr(ti)])
        P.finish()
    return nc, tap_out


def _consts():
    c = np.zeros((128, NCONST), np.float32)
    i = np.arange(128)
    c[:, C_ID:C_ID + 128] = np.eye(128, dtype=np.float32)
    c[:, C_ONES:C_ONES + 128] = 1.0
    c[:, C_TRIA:C_TRIA + 128] = (i[:, None] <= i[None, :])
    c[:, C_TRID:C_TRID + 128] = (i[:, None] >= i[None, :])
    c[:, C_NSA:C_NSA + 128] = np.where(i[None, :] > i[:, None], 0.0, NEG)
    c[:, C_NSD:C_NSD + 128] = np.where(i[None, :] < i[:, None], 0.0, NEG)
    c[:, C_BLK:C_BLK + 128] = (i[:, None] // 64 == i[None, :] // 64)
    rot = np.zeros((128, 128), np.float32)
    for hd in range(2):
        for ax in range(2):
            for f in range(16):
                a0 = hd * 64 + ax * 32 + f
                a1 = a0 + 16
                rot[a1, a0] = -1.0
                rot[a0, a1] = 1.0
    c[:, C_ROT:C_ROT + 128] = rot
    return c


def _rope_tables():
    t = np.arange(TL)
    row = (t // 64).astype(np.float32)
    col = (t % 64).astype(np.float32)
    inv = (np.float32(10000.0) ** (-np.arange(16, dtype=np.float32) / np.float32(16))).astype(np.float32)
    ar = row[:, None] * inv
    ac = col[:, None] * inv
    ang = np.concatenate([ar, ar, ac, ac], axis=-1).astype(np.float32)
    return np.cos(ang).astype(np.float32), np.sin(ang).astype(np.float32)


def _fm(v):
    return np.ascontiguousarray(v.reshape(-1, 128).T)


def prep_core(inp, core, shared):
    b, half = core // 2, core % 2
    rev = half == 1
    f = lambda a: np.ascontiguousarray(a, dtype=np.float32)
    x_b = inp["x"][b]
    ctx_b = inp["ctx"][b]
    cos, sin = shared["rope"]
    if rev:
        x_b = x_b[::-1]; ctx_b = ctx_b[::-1]; cos = cos[::-1]; sin = sin[::-1]
    m = {}
    m["xT"] = f(x_b.T)
    m["ctxT"] = f(ctx_b.T)
    cinv = np.zeros((128, 8, 2), np.float32)
    cinv[:, :, 0] = _fm(inp["c"][b]); cinv[:, :, 1] = _fm(inp["c_ctx"])
    m["cin"] = cinv.reshape(128, 16)
    v = np.zeros((128, NV), np.float32)
    for l in range(2):
        o = l * V_L
        v[:, o:o + 8] = _fm(inp["g_mix_pre"][l]); v[:, o + 8:o + 16] = _fm(inp["g_mix_post"][l])
        v[:, o + 16:o + 24] = _fm(inp["g_ffn_pre"][l]); v[:, o + 24:o + 32] = _fm(inp["g_ffn_post"][l])
        v[:, o + 32:o + 80] = _fm(inp["b_mod"][l])
    v[:, V_CBIN:V_CBIN + 16] = _fm(inp["conf_b_in"][0]); v[:, V_CDWB:V_CDWB + 8] = _fm(inp["conf_dw_b"][0])
    v[:, V_CLNG:V_CLNG + 8] = _fm(inp["conf_ln_g"][0]); v[:, V_CLNB:V_CLNB + 8] = _fm(inp["conf_ln_b"][0])
    v[:, V_CBOUT:V_CBOUT + 8] = _fm(inp["conf_b_out"][0])
    dw = inp["conf_dw_w"][0]; hc = inp["hyb_conv_w"][0]
    if rev:
        dw = dw[::-1]; hc = hc[::-1]
    for j in range(31):
        fm = _fm(dw[j])
        for k in range(8):
            v[:, V_CDW + k * 31 + j] = fm[:, k]
    for j in range(5):
        v[:, V_HCONV + j * 12:V_HCONV + j * 12 + 12] = _fm(hc[j])
    v[:, V_ONORM] = inp["hyb_out_norm"][0]
    v[:, V_QN] = np.tile(inp["hyb_q_norm"][0], 2); v[:, V_KN] = np.tile(inp["hyb_k_norm"][0], 2)
    m["vecs"] = v
    al = inp["hyb_a_log"][0]; dtb = inp["hyb_dt_bias"][0]
    if rev:
        al = al[::-1]; dtb = dtb[::-1]
    m["abc"] = f(np.tile(np.concatenate([al.reshape(-1), dtb.reshape(-1)])[None, :], (128, 1)))
    m["consts"] = shared["consts"]
    tab = np.zeros((2, 128, TL), np.float32)
    tab[0, 0:64] = cos.T; tab[0, 64:128] = cos.T; tab[1, 0:64] = sin.T; tab[1, 64:128] = sin.T
    m["rope"] = tab
    m["w_mod"] = shared["w_mod"]
    m["hyb_w_in"] = shared["hyb_w_in_rev"] if rev else shared["hyb_w_in"]
    m["hyb_w_out"] = shared["hyb_w_out"]
    m["w_ffn_in"] = shared["w_ffn_in"]; m["w_ffn_out"] = shared["w_ffn_out"]
    m["conf_w_in"] = shared["conf_w_in"]; m["conf_w_out"] = shared["conf_w_out"]
    return m


def prep_shared(inp):
    f = lambda a: np.ascontiguousarray(a, dtype=np.float32)
    sh = {"consts": _consts(), "rope": _rope_tables()}
    sh["w_mod"] = f(inp["w_mod"]); sh["w_ffn_in"] = f(inp["w_ffn_in"]); sh["w_ffn_out"] = f(inp["w_ffn_out"])
    sh["conf_w_in"] = f(inp["conf_w_in"][0]); sh["conf_w_out"] = f(inp["conf_w_out"][0])
    w = inp["hyb_w_in"][0]
    qcols = []
    for c in range(4):
        qcols += list(range(2064 + c * 64, 2064 + c * 64 + 64)) + list(range(2064 + (4 + c) * 64, 2064 + (4 + c) * 64 + 64))
    tail = list(range(2576, 2832))
    ba = list(range(2048, 2064))
    ba_rev = [2048 + kind * 8 + (1 - d) * 4 + h for kind in range(2) for d in range(2) for h in range(4)]
    base = list(range(2048))
    sh["hyb_w_in"] = f(w[:, base + ba + qcols + tail])
    sh["hyb_w_in_rev"] = f(w[:, base + ba_rev + qcols + tail])
    wo = inp["hyb_w_out"][0]
    rows = list(range(512))
    for c in range(4):
        rows += list(range(512 + c * 64, 512 + c * 64 + 64)) + list(range(512 + (4 + c) * 64, 512 + (4 + c) * 64 + 64))
    sh["hyb_w_out"] = f(wo[rows, :])
    return sh


def kernel(**inputs):
    inp = {k: np.asarray(v) for k, v in inputs.items()}
    sh = prep_shared(inp)
    nc, _ = build()
    in_maps = [prep_core(inp, c, sh) for c in range(8)]
    res = run_bass_kernel_spmd(nc, in_maps, core_ids=list(range(8)))
    out = np.zeros((4, TL, D), np.float32)
    for c in range(8):
        b, half = c // 2, c % 2
        o = res.results[c]["outT"].T
        if half == 0:
            out[b, 0:NOWN] = o
        else:
            out[b, TL - 1 - np.arange(NOWN)] = o
    return out
```
